# Optimizing a Trainium2 kernel written in Bass

```python
import math
import jax, jax.numpy as jnp
from jax import lax
import numpy as np

D_MODEL = 1024
BATCH = 8
SEQ = 2048
DEPTH = 1
DEC_BATCH = 128
DEC_SEQ = 8
PAST_LEN = 16384
PAGE_SIZE = 128

SSM_INNER = D_MODEL
SSM_HEAD_DIM = 64
SSM_HEADS = SSM_INNER // SSM_HEAD_DIM
SSM_GROUPS = 2
SSM_STATE = 128
SSM_CONV = 4
SSM_CHUNK = 64
SSM_CONV_DIM = SSM_INNER + 2 * SSM_GROUPS * SSM_STATE
ML_HEADS = 4
ML_V_DIM = D_MODEL // ML_HEADS
ML_QK_DIM = ML_V_DIM // 2
ML_INNER = ML_HEADS * ML_V_DIM
ML_QK_WIDTH = ML_HEADS * ML_QK_DIM
ML_CHUNK = 64
MIX_WIDTH = SSM_INNER + ML_INNER
IN_WIDTH = SSM_INNER + SSM_CONV_DIM + SSM_HEADS + 2 * ML_QK_WIDTH + ML_INNER + 2 * ML_HEADS + ML_INNER
N_MEM = 256
X_HEADS = 4
X_HEAD_DIM = D_MODEL // X_HEADS
X_WIDTH = X_HEADS * X_HEAD_DIM
FFN_HIDDEN = ((8 * D_MODEL + 3 * 256 - 1) // (3 * 256)) * 256
EPS = 1e-6

kernel_name = 'hymba_ssd_mlstm_memxattn_step'


def _rms(x):
    xf = x.astype(jnp.float32)
    return (xf * lax.rsqrt(jnp.mean(xf * xf, axis=-1, keepdims=True) + EPS)).astype(x.dtype)


def _chunk_len(length, chunk):
    return chunk if length % chunk == 0 else length


def _to_chunks(t, n_chunks, q):
    return jnp.moveaxis(t.reshape(t.shape[0], n_chunks, q, *t.shape[2:]), 1, 0)


def causal_conv(x, buf, w, b):
    k_w = w.shape[0]
    length = x.shape[1]
    xp = jnp.concatenate([buf.astype(x.dtype), x], axis=1)
    y = b
    for j in range(k_w):
        y = y + xp[:, j:j + length] * w[j]
    return y, xp[:, length:]


def ssd_scan(x, dt, a, bm, cm, h0):
    bsz, length, n_heads, p_dim = x.shape
    g, n_st = bm.shape[2], bm.shape[3]
    hg = n_heads // g
    q = _chunk_len(length, SSM_CHUNK)
    nc = length // q
    f32 = jnp.float32
    xf = x.astype(f32).reshape(bsz, length, g, hg, p_dim)
    dtf = dt.astype(f32).reshape(bsz, length, g, hg)
    la = dtf * a.astype(f32).reshape(g, hg)
    xs = (_to_chunks(xf, nc, q), _to_chunks(dtf, nc, q), _to_chunks(la, nc, q),
          _to_chunks(bm.astype(f32), nc, q), _to_chunks(cm.astype(f32), nc, q))
    causal = jnp.tril(jnp.ones((q, q), dtype=bool))

    def step(h, inp):
        xc, dtc, lac, bc, cc = inp
        cum = jnp.cumsum(lac, axis=1)
        seg = cum[:, :, None] - cum[:, None, :]
        lmat = jnp.exp(jnp.where(causal[None, :, :, None, None], seg, -jnp.inf))
        cb = jnp.einsum('btgn,bsgn->btsg', cc, bc)
        wts = cb[..., None] * lmat * dtc[:, None]
        y_intra = jnp.einsum('btsgj,bsgjp->btgjp', wts, xc)
        y_inter = jnp.einsum('btgn,bgjpn->btgjp', cc, h) * jnp.exp(cum)[..., None]
        to_end = jnp.exp(cum[:, -1:] - cum) * dtc
        h_new = h * jnp.exp(cum[:, -1])[..., None, None] + jnp.einsum('bsgj,bsgjp,bsgn->bgjpn', to_end, xc, bc)
        return h_new, y_intra + y_inter

    h_end, ys = lax.scan(step, h0.astype(f32).reshape(bsz, g, hg, p_dim, n_st), xs)
    y = jnp.moveaxis(ys, 0, 1).reshape(bsz, length, n_heads, p_dim)
    return y.astype(x.dtype), h_end.reshape(bsz, n_heads, p_dim, n_st).astype(h0.dtype)


def mlstm_scan(q, k, v, ig, logf, c0, n0, m0):
    bsz, length, n_heads, _ = q.shape
    ql = _chunk_len(length, ML_CHUNK)
    nc = length // ql
    f32 = jnp.float32
    xs = tuple(_to_chunks(t.astype(f32), nc, ql) for t in (q, k, v, ig, logf))
    causal = jnp.tril(jnp.ones((ql, ql), dtype=bool))

    def step(carry, inp):
        c_st, n_st, m_st = carry
        qc, kc, vc, ic, fc = inp
        b = jnp.cumsum(fc, axis=1)
        log_d = jnp.where(causal[None, :, :, None], b[:, :, None] - b[:, None] + ic[:, None], -jnp.inf)
        inter = b + m_st[:, None]
        m_t = jnp.maximum(jnp.max(log_d, axis=2), inter)
        dmat = jnp.exp(log_d - m_t[:, :, None])
        s_inter = jnp.exp(inter - m_t)
        s = jnp.einsum('bthd,bshd->btsh', qc, kc) * dmat
        num = jnp.einsum('btsh,bshv->bthv', s, vc) + s_inter[..., None] * jnp.einsum('bthd,bhdv->bthv', qc, c_st)
        den = jnp.sum(s, axis=2) + s_inter * jnp.einsum('bthd,bhd->bth', qc, n_st)
        h = num / jnp.maximum(jnp.abs(den), jnp.exp(-m_t))[..., None]
        m_end = m_t[:, -1]
        w_end = jnp.exp(b[:, -1:] - b + ic - m_end[:, None])
        scale = jnp.exp(b[:, -1] + m_st - m_end)
        c_new = scale[..., None, None] * c_st + jnp.einsum('bsh,bshd,bshv->bhdv', w_end, kc, vc)
        n_new = scale[..., None] * n_st + jnp.einsum('bsh,bshd->bhd', w_end, kc)
        return (c_new, n_new, m_end), h

    (c_end, n_end, m_end), hs = lax.scan(step, (c0.astype(f32), n0.astype(f32), m0.astype(f32)), xs)
    h = jnp.moveaxis(hs, 0, 1).reshape(bsz, length, n_heads, v.shape[-1])
    return h.astype(q.dtype), c_end.astype(c0.dtype), n_end.astype(n0.dtype), m_end.astype(m0.dtype)


def memory_kv(mem, g_mem, w_mem_k, w_mem_v):
    bsz = mem.shape[0]
    u = _rms(mem) * g_mem
    mk = (u @ w_mem_k).reshape(bsz, N_MEM, X_HEADS, X_HEAD_DIM)
    mv = (u @ w_mem_v).reshape(bsz, N_MEM, X_HEADS, X_HEAD_DIM)
    return mk, mv


def hybrid_layer(x, mem_k, mem_v, conv_buf, h_ssm, mc, mn, mm,
                 g_pre_mix, w_in, conv_w, conv_b, dt_bias, a_log, d_skip, g_ssm_out,
                 b_igate, b_fgate, g_mlstm_out, w_out, g_post_mix,
                 g_pre_x, w_xq, w_xo, g_post_x, g_pre_ffn, w_gate, w_up, w_down, g_post_ffn):
    bsz, length, _ = x.shape
    u = _rms(x) * g_pre_mix
    proj = u @ w_in
    split_at = np.cumsum([SSM_INNER, SSM_CONV_DIM, SSM_HEADS, ML_QK_WIDTH, ML_QK_WIDTH,
                          ML_INNER, ML_HEADS, ML_HEADS]).tolist()
    z, xbc, dt_raw, q, k, v, ig, fg, og = jnp.split(proj, split_at, axis=-1)
    xbc, conv_new = causal_conv(xbc, conv_buf, conv_w, conv_b)
    xbc = jax.nn.silu(xbc)
    xs, bm, cm = jnp.split(xbc, [SSM_INNER, SSM_INNER + SSM_GROUPS * SSM_STATE], axis=-1)
    xs = xs.reshape(bsz, length, SSM_HEADS, SSM_HEAD_DIM)
    bm = bm.reshape(bsz, length, SSM_GROUPS, SSM_STATE)
    cm = cm.reshape(bsz, length, SSM_GROUPS, SSM_STATE)
    dt = jax.nn.softplus(dt_raw.astype(jnp.float32) + dt_bias.astype(jnp.float32))
    a = -jnp.exp(a_log.astype(jnp.float32))
    y_ssd, h_new = ssd_scan(xs, dt, a, bm, cm, h_ssm)
    y_ssd = (y_ssd + d_skip[:, None] * xs).reshape(bsz, length, SSM_INNER) * jax.nn.silu(z)
    y_ssd = _rms(y_ssd.reshape(bsz, length, SSM_GROUPS, SSM_INNER // SSM_GROUPS)).reshape(bsz, length, SSM_INNER) * g_ssm_out
    q = q.reshape(bsz, length, ML_HEADS, ML_QK_DIM) * (ML_QK_DIM ** -0.5)
    k = k.reshape(bsz, length, ML_HEADS, ML_QK_DIM)
    v = v.reshape(bsz, length, ML_HEADS, ML_V_DIM)
    i_pre = (ig + b_igate).astype(jnp.float32)
    logf = jax.nn.log_sigmoid((fg + b_fgate).astype(jnp.float32))
    h_ml, c_new, n_new, m_new = mlstm_scan(q, k, v, i_pre, logf, mc, mn, mm)
    h_ml = _rms(h_ml).reshape(bsz, length, ML_INNER) * g_mlstm_out * jax.nn.sigmoid(og)
    mix = jnp.concatenate([y_ssd.astype(x.dtype), h_ml.astype(x.dtype)], axis=-1) @ w_out
    x = x + _rms(mix) * g_post_mix
    u = _rms(x) * g_pre_x
    qx = (u @ w_xq).reshape(bsz, length, X_HEADS, X_HEAD_DIM)
    sc = jnp.einsum('blhd,bmhd->bhlm', qx, mem_k).astype(jnp.float32) * (X_HEAD_DIM ** -0.5)
    pr = jax.nn.softmax(sc, axis=-1).astype(x.dtype)
    o = jnp.einsum('bhlm,bmhd->blhd', pr, mem_v).reshape(bsz, length, X_WIDTH) @ w_xo
    x = x + _rms(o) * g_post_x
    u = _rms(x) * g_pre_ffn
    f = (jax.nn.silu(u @ w_gate) * (u @ w_up)) @ w_down
    x = x + _rms(f) * g_post_ffn
    return x, conv_new, h_new, c_new, n_new, m_new


def setup_inputs(seed: int = 0) -> dict:
    key = jax.random.key(seed)
    ks = iter(jax.random.split(key, 64))

    def nrm(shape, scale):
        return jax.random.normal(next(ks), shape, jnp.float32) * scale

    def gain(width):
        return 1.0 + nrm((DEPTH, width), 0.05)

    dt0 = jnp.exp(jax.random.uniform(next(ks), (DEPTH, SSM_HEADS), jnp.float32, math.log(1e-3), math.log(0.1)))
    return {
        'x_prompt': nrm((BATCH, SEQ, D_MODEL), 1.0),
        'x_sample': nrm((DEC_BATCH, DEC_SEQ, D_MODEL), 1.0),
        'mem_prompt': nrm((BATCH, N_MEM, D_MODEL), 1.0),
        'state_ssm': nrm((DEPTH, DEC_BATCH, SSM_HEADS, SSM_HEAD_DIM, SSM_STATE), 0.1),
        'state_conv': nrm((DEPTH, DEC_BATCH, SSM_CONV - 1, SSM_CONV_DIM), 1.0),
        'state_mlstm_c': nrm((DEPTH, DEC_BATCH, ML_HEADS, ML_QK_DIM, ML_V_DIM), 0.1),
        'state_mlstm_n': nrm((DEPTH, DEC_BATCH, ML_HEADS, ML_QK_DIM), 0.1),
        'state_mlstm_m': nrm((DEPTH, DEC_BATCH, ML_HEADS), 0.5),
        'cache_mem_k': nrm((DEPTH, DEC_BATCH, N_MEM, X_HEADS, X_HEAD_DIM), 1.0),
        'cache_mem_v': nrm((DEPTH, DEC_BATCH, N_MEM, X_HEADS, X_HEAD_DIM), 1.0),
        'g_pre_mix': gain(D_MODEL),
        'w_in': nrm((DEPTH, D_MODEL, IN_WIDTH), D_MODEL ** -0.5),
        'conv_w': nrm((DEPTH, SSM_CONV, SSM_CONV_DIM), SSM_CONV ** -0.5),
        'conv_b': nrm((DEPTH, SSM_CONV_DIM), 0.02),
        'dt_bias': dt0 + jnp.log(-jnp.expm1(-dt0)),
        'a_log': jnp.log(jax.random.uniform(next(ks), (DEPTH, SSM_HEADS), jnp.float32, 1.0, 16.0)),
        'd_skip': 1.0 + nrm((DEPTH, SSM_HEADS), 0.1),
        'g_ssm_out': gain(SSM_INNER),
        'b_igate': nrm((DEPTH, ML_HEADS), 0.1),
        'b_fgate': 3.0 + jax.random.uniform(next(ks), (DEPTH, ML_HEADS), jnp.float32, 0.0, 3.0),
        'g_mlstm_out': gain(ML_INNER),
        'w_out': nrm((DEPTH, MIX_WIDTH, D_MODEL), MIX_WIDTH ** -0.5),
        'g_post_mix': gain(D_MODEL),
        'g_mem': gain(D_MODEL),
        'w_mem_k': nrm((DEPTH, D_MODEL, X_WIDTH), D_MODEL ** -0.5),
        'w_mem_v': nrm((DEPTH, D_MODEL, X_WIDTH), D_MODEL ** -0.5),
        'g_pre_x': gain(D_MODEL),
        'w_xq': nrm((DEPTH, D_MODEL, X_WIDTH), D_MODEL ** -0.5),
        'w_xo': nrm((DEPTH, X_WIDTH, D_MODEL), X_WIDTH ** -0.5),
        'g_post_x': gain(D_MODEL),
        'g_pre_ffn': gain(D_MODEL),
        'w_gate': nrm((DEPTH, D_MODEL, FFN_HIDDEN), D_MODEL ** -0.5),
        'w_up': nrm((DEPTH, D_MODEL, FFN_HIDDEN), D_MODEL ** -0.5),
        'w_down': nrm((DEPTH, FFN_HIDDEN, D_MODEL), FFN_HIDDEN ** -0.5),
        'g_post_ffn': gain(D_MODEL),
    }


def reference(x_prompt, x_sample, mem_prompt, state_ssm, state_conv, state_mlstm_c, state_mlstm_n,
              state_mlstm_m, cache_mem_k, cache_mem_v, g_pre_mix, w_in, conv_w, conv_b, dt_bias, a_log,
              d_skip, g_ssm_out, b_igate, b_fgate, g_mlstm_out, w_out, g_post_mix, g_mem, w_mem_k, w_mem_v,
              g_pre_x, w_xq, w_xo, g_post_x, g_pre_ffn, w_gate, w_up, w_down, g_post_ffn):
    yp, ys = x_prompt, x_sample
    dtp = x_prompt.dtype
    mk_l, mv_l = [], []
    ssm_p, conv_p, c_p, n_p, m_p = [], [], [], [], []
    ssm_s, conv_s, c_s, n_s, m_s = [], [], [], [], []
    for l in range(DEPTH):
        lw = (g_pre_mix[l], w_in[l], conv_w[l], conv_b[l], dt_bias[l], a_log[l], d_skip[l], g_ssm_out[l],
              b_igate[l], b_fgate[l], g_mlstm_out[l], w_out[l], g_post_mix[l],
              g_pre_x[l], w_xq[l], w_xo[l], g_post_x[l], g_pre_ffn[l], w_gate[l], w_up[l], w_down[l], g_post_ffn[l])
        mk, mv = memory_kv(mem_prompt, g_mem[l], w_mem_k[l], w_mem_v[l])
        bp = yp.shape[0]
        yp, cb, hs, cc, nn, mm = hybrid_layer(
            yp, mk, mv,
            jnp.zeros((bp, SSM_CONV - 1, SSM_CONV_DIM), dtp),
            jnp.zeros((bp, SSM_HEADS, SSM_HEAD_DIM, SSM_STATE), dtp),
            jnp.zeros((bp, ML_HEADS, ML_QK_DIM, ML_V_DIM), dtp),
            jnp.zeros((bp, ML_HEADS, ML_QK_DIM), dtp),
            jnp.zeros((bp, ML_HEADS), dtp), *lw)
        mk_l.append(mk); mv_l.append(mv)
        ssm_p.append(hs); conv_p.append(cb); c_p.append(cc); n_p.append(nn); m_p.append(mm)
        ys, cb, hs, cc, nn, mm = hybrid_layer(
            ys, cache_mem_k[l], cache_mem_v[l], state_conv[l], state_ssm[l],
            state_mlstm_c[l], state_mlstm_n[l], state_mlstm_m[l], *lw)
        ssm_s.append(hs); conv_s.append(cb); c_s.append(cc); n_s.append(nn); m_s.append(mm)
    return (yp, ys,
            jnp.stack(mk_l), jnp.stack(mv_l),
            jnp.stack(ssm_p), jnp.stack(conv_p), jnp.stack(c_p), jnp.stack(n_p), jnp.stack(m_p),
            jnp.stack(ssm_s), jnp.stack(conv_s), jnp.stack(c_s), jnp.stack(n_s), jnp.stack(m_s))
```

```python
import numpy as np
import concourse.bass as bass
import concourse.mybir as mybir
from concourse.bass_utils import run_bass_kernel_spmd
from contextlib import ExitStack

F32 = mybir.dt.float32
BF16 = mybir.dt.bfloat16
AF = mybir.ActivationFunctionType
ALU = mybir.AluOpType
AX = mybir.AxisListType
EPS = 1e-6
NCORES = 8
ENGS = ("pe", "dve", "act", "pool", "sp")


class Op:
    __slots__ = ("eng", "fn", "reads", "writes", "waits", "marked", "cnt",
                 "is_dma", "dsem", "dcnt", "prev_on_sem", "deps")

    def __init__(self, eng, fn, reads, writes, is_dma=False):
        self.eng = eng
        self.fn = fn
        self.reads = reads
        self.writes = writes
        self.waits = []
        self.marked = False
        self.cnt = 0
        self.is_dma = is_dma
        self.dsem = None
        self.dcnt = 0
        self.prev_on_sem = None
        self.deps = []


class Prog:
    def __init__(self, nc, n_dma_sems=80, same_engine_sync=True):
        self.nc = nc
        self.ops = {e: [] for e in ENGS}
        self.all_ops = []
        self.last_w = {}
        self.readers = {}
        self.n_dma_sems = n_dma_sems
        self.dma_rr = 0
        self.dma_rrq = [0, 0]
        self.dma_issued = [0] * n_dma_sems
        self.dma_last = [None] * n_dma_sems
        self.same_engine_sync = same_engine_sync
        self.pending_bar = {e: [] for e in ENGS}
        self.alias = {}

    def barrier(self, engs=ENGS):
        snap = []
        for e in ENGS:
            for op in reversed(self.ops[e]):
                if not op.is_dma:
                    snap.append(op)
                    break
        for j in range(self.n_dma_sems):
            if self.dma_last[j] is not None:
                snap.append(self.dma_last[j])
        for e in engs:
            self.pending_bar[e] = list(snap)

    def _add(self, op):
        if self.alias:
            rd = []
            for k in op.reads:
                rd.extend(self.alias.get(k, (k,)))
            op.reads = tuple(rd)
        deps = []
        if self.pending_bar[op.eng]:
            deps.extend(self.pending_bar[op.eng])
            self.pending_bar[op.eng] = []
        for k in op.reads:
            w = self.last_w.get(k)
            if w is not None:
                deps.append(w)
        for k in op.writes:
            w = self.last_w.get(k)
            if w is not None:
                deps.append(w)
            for r in self.readers.get(k, ()):
                deps.append(r)
        op.deps = deps
        for k in op.reads:
            self.readers.setdefault(k, []).append(op)
        for k in op.writes:
            self.last_w[k] = op
            self.readers[k] = []
        self.ops[op.eng].append(op)
        self.all_ops.append(op)
        return op

    def op(self, eng, fn, reads=(), writes=()):
        reads, writes = tuple(reads), tuple(writes)
        if eng != "pe":
            extra = tuple(k for k in reads if isinstance(k, str) and k.startswith("ps") and k[2:].isdigit() and k not in writes)
            writes = writes + extra
        return self._add(Op(eng, fn, reads, writes))

    def dma(self, q, out, in_, reads=(), writes=(), **kw):
        def fn(e, out=out, in_=in_, kw=kw):
            return e.dma_start(out=out, in_=in_, **kw)
        op = Op(q, fn, tuple(reads), tuple(writes), is_dma=True)
        half = self.n_dma_sems // 2
        qi = 0 if q == "sp" else 1
        j = qi * half + self.dma_rrq[qi]
        self.dma_rrq[qi] = (self.dma_rrq[qi] + 1) % half
        op.dsem = j
        self.dma_issued[j] += 1
        op.dcnt = self.dma_issued[j]
        op.prev_on_sem = self.dma_last[j]
        self.dma_last[j] = op
        return self._add(op)

    def _skip(self, d, op):
        return (d.eng == op.eng and not op.is_dma and not d.is_dma
                and (d.eng == "pe" or not self.same_engine_sync))

    def finalize(self):
        for op in self.all_ops:
            for d in op.deps:
                if d is op or d.is_dma or self._skip(d, op):
                    continue
                d.marked = True
        cnt = {e: 0 for e in ENGS}
        for op in self.all_ops:
            if not op.is_dma and op.marked:
                cnt[op.eng] += 1
                op.cnt = cnt[op.eng]
        waited = {e: {} for e in ENGS}
        for op in self.all_ops:
            w = waited[op.eng]
            need = {}
            for d in op.deps:
                if d is op:
                    continue
                if d.is_dma:
                    key, val = ("d", d.dsem), 16 * d.dcnt
                else:
                    if self._skip(d, op):
                        continue
                    key, val = ("e", d.eng), d.cnt
                if need.get(key, 0) < val:
                    need[key] = val
            if op.is_dma and op.prev_on_sem is not None:
                key, val = ("d", op.dsem), 16 * (op.dcnt - 1)
                if need.get(key, 0) < val:
                    need[key] = val
            op.waits = []
            for key, val in need.items():
                if w.get(key, 0) >= val:
                    continue
                w[key] = val
                op.waits.append((key, val))

    def emit(self):
        nc = self.nc
        self.finalize()
        with ExitStack() as st:
            esem = {e: st.enter_context(nc.semaphore("s_" + e)) for e in ENGS}
            dsem = [st.enter_context(nc.semaphore("d%d" % j)) for j in range(self.n_dma_sems)]
            block = st.enter_context(nc.Block())

            def replay(ename, eng):
                for op in self.ops[ename]:
                    for (kind, ident), val in op.waits:
                        eng.wait_ge(esem[ident] if kind == "e" else dsem[ident], val)
                    ins = op.fn(eng)
                    if op.is_dma:
                        ins.then_inc(dsem[op.dsem], 16)
                    elif op.marked:
                        ins.then_inc(esem[ename], 1)
                if ename == "sp":
                    for j in range(self.n_dma_sems):
                        if self.dma_issued[j] > 0:
                            eng.wait_ge(dsem[j], 16 * self.dma_issued[j])

            @block.sync
            def _(e):
                replay("sp", e)

            @block.tensor
            def _(e):
                replay("pe", e)

            @block.vector
            def _(e):
                replay("dve", e)

            @block.scalar
            def _(e):
                replay("act", e)

            @block.gpsimd
            def _(e):
                replay("pool", e)


CO = {}


def _make_consts():
    cols = []
    off = [0]

    def add(name, arr):
        arr = np.asarray(arr, np.float32)
        CO[name] = (off[0], arr.shape[1])
        off[0] += arr.shape[1]
        cols.append(arr)

    r = np.arange(128)
    for v, seq in (("P", np.zeros(128, int)), ("S", r // 8)):
        same = (seq[:, None] == seq[None, :])
        le = (r[:, None] <= r[None, :])
        add("maskU" + v, same & le)
        add("maskLs" + v, same & (r[:, None] > r[None, :]))
        add("negT" + v, np.where(same & (r[None, :] <= r[:, None]), 0.0, -1e30))
        add("same" + v, same)
        last = np.array([np.max(np.where(seq == seq[s])[0]) for s in range(128)])
        add("lastbc" + v, (r[:, None] == last[None, :]))
    add("ident", np.eye(128))
    add("ones", np.ones((128, 128)))
    add("seqmaskS", (r[:, None] // 8 == np.arange(16)[None, :]))
    add("lastselS", (r[:, None] == (np.arange(16)[None, :] * 8 + 7)))
    add("seqmaskP", np.ones((128, 1)))
    add("lastselP", (r[:, None] == 127))
    return np.concatenate(cols, axis=1)


CONSTS = _make_consts()
NCONST = CONSTS.shape[1]

W_SHAPES = {
    "g_pre_mix": (1, 1024), "w_in": (1024, 5656), "conv_w": (4, 1536), "conv_b": (1, 1536), "dt_bias": (1, 16),
    "a_log": (1, 16), "d_skip": (1, 16), "g_ssm_out": (1, 1024), "b_igate": (1, 4), "b_fgate": (1, 4),
    "g_mlstm_out": (1, 1024), "w_out": (2048, 1024), "g_post_mix": (1, 1024), "g_mem": (1, 1024),
    "w_mem_k": (1024, 1024), "w_mem_v": (1024, 1024), "g_pre_x": (1, 1024), "w_xq": (1024, 1024),
    "w_xo": (1024, 1024), "g_post_x": (1, 1024), "g_pre_ffn": (1, 1024), "w_gate": (1024, 2816),
    "w_up": (1024, 2816), "w_down": (2816, 1024), "g_post_ffn": (1, 1024),
}
IN_SHAPES = {
    "xp": (2048, 1024), "xs": (128, 1024), "mem": (256, 1024), "st_ssm": (16, 16, 64, 128),
    "st_conv": (48, 1536), "st_c": (16, 4, 128, 256), "st_n": (16, 4, 128), "st_m": (16, 4),
    "ck": (16, 256, 1024), "cv": (16, 256, 1024), "consts": (128, NCONST),
}
OUT_SHAPES = {
    "yp": (2048, 1024), "ys": (128, 1024), "mk": (256, 1024), "mv": (256, 1024),
    "ssm_p": (16, 64, 128), "conv_p": (3, 1536), "c_p": (4, 128, 256), "n_p": (4, 128), "m_p": (1, 4),
    "ssm_s": (16, 16, 64, 128), "conv_s": (16, 3, 1536), "c_s": (16, 4, 128, 256), "n_s": (16, 4, 128),
    "m_s": (16, 4),
}
NT = 17
ALLOC_LOG = None
NPT = [16]


class KM:
    def __init__(self, P, p, keys, sink, cur=None, bankmap=None):
        self.P, self.p, self.keys, self.sink = P, p, keys, sink
        self.cur, self.bankmap = cur, bankmap

    def m(self, k):
        if self.bankmap is not None and k.startswith("ps") and k[2:].isdigit():
            return "ps%d" % self.bankmap[int(k[2:])]
        return "%s@%d" % (k, self.p) if k in self.keys else k

    def op(self, eng, fn, reads=(), writes=()):
        if self.bankmap is not None:
            def fn2(e, fn=fn, mp=self.bankmap, cur=self.cur):
                cur["m"] = mp
                return fn(e)
        else:
            fn2 = fn
        self.sink.append(("op", eng, fn2, [self.m(k) for k in reads], [self.m(k) for k in writes], None))

    def dma(self, q, out, in_, reads=(), writes=(), **kw):
        self.sink.append(("dma", q, (out, in_), [self.m(k) for k in reads], [self.m(k) for k in writes], kw))


def _flush(P, recs):
    for kind, eng, a, reads, writes, kw in recs:
        if kind == "op":
            P.op(eng, a, reads, writes)
        else:
            P.dma(eng, a[0], a[1], reads, writes, **kw)


def _interleave(a, b):
    out, ia, ib = [], 0, 0
    while ia < len(a) or ib < len(b):
        if ib >= len(b) or (ia < len(a) and ia * len(b) <= ib * len(a)):
            out.append(a[ia]); ia += 1
        else:
            out.append(b[ib]); ib += 1
    return out


def run_pipelined(P, tile_fn, tiles, interleave=True, after_last_front=None):
    sinks = [[] for _ in tiles]
    gens = [tile_fn(t, sinks[i]) for i, t in enumerate(tiles)]
    next(gens[0])
    _flush(P, sinks[0]); sinks[0].clear()
    for i in range(len(gens)):
        front = []
        if i + 1 < len(gens):
            next(gens[i + 1])
            front = list(sinks[i + 1]); sinks[i + 1].clear()
        for _ in gens[i]:
            pass
        body = list(sinks[i]); sinks[i].clear()
        _flush(P, _interleave(body, front) if interleave else front + body)
        if after_last_front is not None and i + 2 == len(gens):
            after_last_front()


class StopBuild(Exception):
    pass


def build_nc(stop_after="C"):
    nc = bass.Bass("TRN2", target_bir_lowering=False)
    try:
        _build(nc, stop_after)
    except StopBuild:
        pass
    return nc


def _build(nc, stop_after):
    D = {}
    for k, s in list(IN_SHAPES.items()) + list(W_SHAPES.items()):
        D[k] = nc.dram_tensor(k, list(s), F32, kind="ExternalInput").ap()
    for k, s in OUT_SHAPES.items():
        D[k] = nc.dram_tensor(k, list(s), F32, kind="ExternalOutput").ap()
    x1s = nc.dram_tensor("x1s", [NT, 128, 1024], F32, kind="Internal").ap()
    x2s = nc.dram_tensor("x2s", [NT, 128, 1024], F32, kind="Internal").ap()
    uTs = nc.dram_tensor("uTs", [NT, 128, 1024], BF16, kind="Internal").ap()
    yTs = nc.dram_tensor("yTs", [NT, 128, 1024], BF16, kind="Internal").ap()

    P = Prog(nc)
    st = ExitStack()

    dbg = {}

    def chk(name):
        if stop_after == name:
            if name == "SA1_0":
                P.op("act", lambda e: e.copy(out=dbg["hout0"][:, 0, 0:8], in_=dbg["hout0"][:, 0, 0:8]), reads=["hout0"], writes=["hout0"])
            elif name.startswith("SA1"):
                P.dma("sp", D["yp"][0:128, :], dbg["hout0"][:].rearrange("p j n -> p (j n)"), reads=["hout0"])
            P.emit()
            raise StopBuild()

    with st:
        base0 = (int(nc.sbuf_base) + 63) // 64 * 64
        top = [int(nc.sbuf_top)]
        cur = [base0]
        cnt = [0]

        curL = [0, 0]

        def alloc(shape, dt, low=False):
            nbytes = int(np.prod(shape[1:])) * (2 if dt == BF16 else 4)
            nbytes = (nbytes + 63) // 64 * 64
            if low:
                off = curL[0]
                curL[0] += nbytes
                assert curL[0] <= curL[1], ("SBUF low-region overflow", curL[0], curL[1])
            else:
                off = cur[0]
                cur[0] += nbytes
                assert cur[0] <= top[0], ("SBUF overflow", cur[0], top[0])
            cnt[0] += 1
            t = nc.alloc_sbuf_tensor_at("t%d" % cnt[0], list(shape), dt, offset=off)
            if ALLOC_LOG is not None:
                ALLOC_LOG.append((cnt[0], off, nbytes, tuple(shape)))
            return t

        ps = [st.enter_context(nc.psum_tensor("ps%d" % i, [128, 512], F32)) for i in range(8)]

        def psf(i):
            return ps[i][:]

        def psb(i):
            return ps[i][:].bitcast(BF16)

        cst = alloc([128, NCONST], F32)
        identb = alloc([128, 128], BF16)
        KT = alloc([128, 8, 256], BF16)
        Vb = alloc([128, 2, 1024], BF16)
        phase_base = cur[0]

        def C(name):
            o, n = CO[name]
            return cst[:, o:o + n]

        P.dma("sp", cst[:], D["consts"], writes=["cst"])
        P.dma("pool", identb[:], D["consts"][:, CO["ident"][0]:CO["ident"][0] + 128], writes=["identb"])
        identf = C("ident")
        onesf = C("ones")

        def new_phase():
            P.barrier()
            cur[0] = phase_base

        def load_w(dst, src, k0, nk, c0, c1, key):
            for kc in range(nk):
                cc = c0
                while cc < c1:
                    ce = min(cc + 1024, c1)
                    ck = "%s#%d" % (key, len(P.alias.setdefault(key, [])))
                    P.alias[key].append(ck)
                    P.dma("pool", dst[:, kc, cc - c0:ce - c0], src[(k0 + kc) * 128:(k0 + kc + 1) * 128, cc:ce],
                          writes=[ck])
                    cc = ce

        def bcast_load(dst, src_row, key, n):
            P.dma("sp", dst, src_row.partition_broadcast(128), writes=[key])

        def mm_group(out, pairs):
            def f(e, out=out, pairs=pairs):
                n = len(pairs)
                ins = None
                for i, (l, r) in enumerate(pairs):
                    ins = e.matmul(out, lhsT=l, rhs=r, start=(i == 0), stop=(i == n - 1))
                return ins
            return f

        def transposes(items):
            def f(e, items=items):
                ins = None
                for (o, i, idn) in items:
                    ins = e.transpose(o, i, idn)
                return ins
            return f

        def x_src(ti):
            return D["xp"][ti * 128:(ti + 1) * 128, :] if ti < 16 else D["xs"]

        def y_dst(ti):
            return D["yp"][ti * 128:(ti + 1) * 128, :] if ti < 16 else D["ys"]

        def rms_uT(xt, xkey, gbc, gkey, wk, uT, uTkey, psi, P=P, sfx="", psb=psb):
            P.op("act", lambda e: e.activation(out=wk["junk"][:], in_=xt, func=AF.Square, accum_out=wk["ss"][:, 0:1]),
                 reads=[xkey], writes=["junk" + sfx, "ss" + sfx])
            P.op("act", lambda e: e.activation(out=wk["ss"][:, 1:2], in_=wk["ss"][:, 0:1], func=AF.Sqrt, bias=EPS,
                                               scale=1.0 / 1024), reads=["ss" + sfx], writes=["ss1" + sfx])
            P.op("dve", lambda e: e.reciprocal(out=wk["ss"][:, 2:3], in_=wk["ss"][:, 1:2]), reads=["ss1" + sfx], writes=["ss2" + sfx])
            P.op("dve", lambda e: e.scalar_tensor_tensor(out=wk["u"][:], in0=xt, scalar=wk["ss"][:, 2:3], in1=gbc,
                                                         op0=ALU.mult, op1=ALU.mult),
                 reads=[xkey, "ss2" + sfx, gkey], writes=["u" + sfx])
            pk = "ps%d" % psi
            P.op("pe", transposes([(psb(psi)[:, kc * 128:(kc + 1) * 128], wk["u"][:, kc * 128:(kc + 1) * 128], identb[:])
                                   for kc in range(8)]), reads=["u" + sfx, "identb"], writes=[pk])
            P.op("act", lambda e: e.copy(out=uT[:].rearrange("p k t -> p (k t)"), in_=psb(psi)), reads=[pk], writes=[uTkey])

        def post_norm_res(psa, psbk, xr, xrkey, gbc, gkey, wk, xo, xokey, P=P, sfx="", psf=psf):
            ka, kb = "ps%d" % psa, "ps%d" % psbk
            P.op("act", lambda e: e.activation(out=wk["junk"][:, 0:512], in_=psf(psa), func=AF.Square,
                                               accum_out=wk["ss"][:, 0:1]), reads=[ka], writes=["junk" + sfx, "ss" + sfx])
            P.op("act", lambda e: e.activation(out=wk["junk"][:, 512:1024], in_=psf(psbk), func=AF.Square,
                                               accum_out=wk["ss"][:, 1:2]), reads=[kb], writes=["junkb" + sfx, "ss1" + sfx])
            P.op("dve", lambda e: e.tensor_tensor(out=wk["ss"][:, 2:3], in0=wk["ss"][:, 0:1], in1=wk["ss"][:, 1:2],
                                                  op=ALU.add), reads=["ss" + sfx, "ss1" + sfx], writes=["ss2" + sfx])
            P.op("act", lambda e: e.activation(out=wk["ss"][:, 3:4], in_=wk["ss"][:, 2:3], func=AF.Sqrt, bias=EPS,
                                               scale=1.0 / 1024), reads=["ss2" + sfx], writes=["ss3" + sfx])
            P.op("dve", lambda e: e.reciprocal(out=wk["ss"][:, 3:4], in_=wk["ss"][:, 3:4]), reads=["ss3" + sfx], writes=["ss3" + sfx])
            for half, pi, pk in ((0, psa, ka), (1, psbk, kb)):
                sl = slice(half * 512, (half + 1) * 512)
                P.op("dve", lambda e, pi=pi, sl=sl: e.scalar_tensor_tensor(
                    out=xo[:, sl], in0=psf(pi), scalar=wk["ss"][:, 3:4], in1=gbc[:, sl], op0=ALU.mult, op1=ALU.mult),
                    reads=[pk, "ss3" + sfx, gkey], writes=[xokey + str(half)])
                P.op("dve", lambda e, sl=sl: e.tensor_tensor(out=xo[:, sl], in0=xo[:, sl], in1=xr[:, sl], op=ALU.add),
                     reads=[xokey + str(half), xrkey], writes=[xokey + str(half)])

        new_phase()
        LOW = 86016
        cur[0] = phase_base + LOW + 8192
        Wk = alloc([128, 8, 1024], BF16)
        Wv = alloc([128, 8, 1024], BF16)
        load_w(Wk, D["w_mem_k"], 0, 8, 0, 1024, "Wk")
        load_w(Wv, D["w_mem_v"], 0, 8, 0, 1024, "Wv")
        W1 = nc.alloc_sbuf_tensor_at("W1e", [128, 8, 2576], BF16, offset=phase_base)
        load_w(W1, D["w_in"], 0, 8, 0, 2576, "W1")
        gmem = alloc([128, 1024], F32)
        bcast_load(gmem[:], D["g_mem"], "gmem", 1024)
        wk0 = {"junk": alloc([128, 1024], BF16), "ss": alloc([128, 4], F32), "u": alloc([128, 1024], BF16)}
        memt = [alloc([128, 1024], F32) for _ in range(2)]
        uTm = [alloc([128, 8, 128], BF16) for _ in range(2)]
        osb = [alloc([128, 512], F32) for _ in range(2)]
        for mt in range(2):
            P.dma("sp", memt[mt][:], D["mem"][mt * 128:(mt + 1) * 128, :], writes=["memt%d" % mt])
            rms_uT(memt[mt][:], "memt%d" % mt, gmem[:], "gmem", wk0, uTm[mt], "uTm%d" % mt, 0)
        chk("0a")
        oi = 0
        for mt in range(2):
            for (Wt, wkey, dst) in ((Wk, "Wk", "mk"), (Wv, "Wv", "mv")):
                for n in range(2):
                    pi = 1 + (oi % 2)
                    P.op("pe", mm_group(psf(pi), [(uTm[mt][:, kc, :], Wt[:, kc, n * 512:(n + 1) * 512]) for kc in range(8)]),
                         reads=["uTm%d" % mt, wkey], writes=["ps%d" % pi])
                    ob = osb[oi % 2]
                    P.op("act", lambda e, ob=ob, pi=pi: e.copy(out=ob[:], in_=psf(pi)), reads=["ps%d" % pi],
                         writes=["osb%d" % (oi % 2)])
                    if dst == "mv":
                        P.op("act", lambda e, pi=pi, mt=mt, n=n: e.copy(out=Vb[:, mt, n * 512:(n + 1) * 512], in_=psf(pi)),
                             reads=["ps%d" % pi], writes=["Vb"])
                    P.dma("sp", D[dst][mt * 128:(mt + 1) * 128, n * 512:(n + 1) * 512], ob[:], reads=["osb%d" % (oi % 2)])
                    oi += 1
        chk("0b")
        for fc in range(8):
            pi = 3 + (fc % 2)
            pairs = []
            P.op("pe", lambda e, fc=fc, pi=pi: [
                [e.matmul(psf(pi)[:, mt * 128:(mt + 1) * 128], lhsT=Wk[:, kc, fc * 128:(fc + 1) * 128], rhs=uTm[mt][:, kc, :],
                          start=(kc == 0), stop=(kc == 7)) for kc in range(8)] for mt in range(2)][-1][-1],
                 reads=["uTm0", "uTm1", "Wk"], writes=["ps%d" % pi])
            P.op("act", lambda e, fc=fc, pi=pi: e.copy(out=KT[:, fc, :], in_=psf(pi)[:, 0:256]), reads=["ps%d" % pi], writes=["KT"])

        chk("0")

        new_phase()
        curL[0], curL[1] = phase_base, phase_base + LOW
        cur[0] = phase_base + LOW
        alloc([128, 8, 2576], BF16, low=True)
        gpm = alloc([128, 1024], F32, low=True)
        gso = alloc([128, 1024], F32)
        bcast_load(gpm[:], D["g_pre_mix"], "gpm", 1024)
        bcast_load(gso[:], D["g_ssm_out"], "gso", 1024)
        sm = alloc([128, 64], F32)
        bcast_load(sm[:, 0:16], D["dt_bias"], "sm_dtb", 16)
        bcast_load(sm[:, 16:32], D["a_log"], "sm_al", 16)
        bcast_load(sm[:, 32:48], D["d_skip"], "sm_dsk", 16)
        P.op("act", lambda e: e.activation(out=sm[:, 16:32], in_=sm[:, 16:32], func=AF.Exp), reads=["sm_al"], writes=["sm_al"])
        P.op("dve", lambda e: e.tensor_scalar_mul(out=sm[:, 16:32], in0=sm[:, 16:32], scalar1=-1.0), reads=["sm_al"], writes=["sm_a"])
        cw = alloc([128, 12, 4], F32, low=True)
        cbias = alloc([128, 12], F32, low=True)
        for ci in range(12):
            P.dma("sp", cw[:, ci, :], D["conv_w"][:, ci * 128:(ci + 1) * 128].rearrange("j p -> p j"), writes=["cw"],
                  allow_slow_non_contiguous=True)
        P.dma("sp", cbias[:], D["conv_b"].rearrange("o (c p) -> p (o c)", p=128), writes=["cbias"], allow_slow_non_contiguous=True)
        wk1 = {"junk": alloc([128, 1024], BF16, low=True), "ss": alloc([128, 4], F32, low=True), "u": alloc([128, 1024], BF16, low=True)}
        xt = alloc([128, 1024], F32, low=True)
        uT = alloc([128, 8, 128], BF16, low=True)
        zs_ = [alloc([128, 1024], F32) for _ in range(2)]
        xbcT = alloc([128, 12 * 176], BF16, low=True)
        vP = xbcT[:, 0:12 * 131].rearrange("p (c t) -> p c t", c=12)
        vS = xbcT[:].rearrange("p (c b j) -> p c b j", c=12, b=16)
        diagW = alloc([128, 48, 128], BF16, low=True)
        for ci in range(12):
            for j in range(4):
                P.op("dve", lambda e, ci=ci, j=j: e.tensor_scalar_mul(out=diagW[:, ci * 4 + j, :], in0=identf, scalar1=cw[:, ci, j:j + 1]),
                     reads=["cst", "cw"], writes=["diagW"])
        xbcS_ = [alloc([128, 12, 128], BF16) for _ in range(2)]
        xtok = alloc([128, 1536], F32, low=True)
        scv = alloc([48, 1536], F32, low=True)
        xs_tok_ = [alloc([128, 1024], BF16) for _ in range(2)]
        xsd_ = [alloc([128, 1024], BF16) for _ in range(2)]
        xdt_ = [alloc([128, 1024], BF16) for _ in range(2)]
        xw_ = [alloc([128, 1024], BF16) for _ in range(2)]
        Btok_ = [alloc([128, 256], BF16) for _ in range(2)]
        Bp = [alloc([128, 256], BF16) for _ in range(2)]
        CTp = [alloc([128, 2, 128], BF16) for _ in range(2)]
        R = alloc([128, 16, 128], F32)
        Lq = [alloc([128, 512], F32) for _ in range(2)]
        WT = alloc([128, 16, 128], BF16)
        cbm = alloc([128, 2, 128], F32)
        yi = alloc([128, 1024], F32)
        yg = alloc([128, 1024], F32)
        yn = alloc([128, 1024], BF16)
        yT = alloc([128, 8, 128], BF16)
        hT = alloc([128, 1024], F32)
        hTb = alloc([128, 1024], BF16)
        hT2 = alloc([128, 1024], F32)
        hTb2 = alloc([128, 1024], BF16)
        hnat = [alloc([128, 8, 128], F32) for _ in range(2)]
        hout = [alloc([128, 8, 128], F32) for _ in range(2)]
        dbg["hout0"] = hout[0]
        dbg["hT"] = hT
        junk2 = alloc([128, 1024], BF16)
        sv_ = [alloc([128, 256], F32) for _ in range(2)]
        sv = None
        rhsd = alloc([128, 16, 16], F32)
        decay = alloc([128, 256], F32)
        P.op("dve", lambda e: e.memset(vP[:, :, 0:3], 0.0), writes=["xbcT"])
        P.op("dve", lambda e: e.memset(hT[:], 0.0), writes=["hT"])
        P.op("dve", lambda e: e.memset(hTb[:], 0.0), writes=["hTb"])
        for k in range(2):
            P.op("dve", lambda e, k=k: e.memset(CTp[k][:], 0.0), writes=["CTp%d" % k])

        LASTP = NPT[0] - 1
        def a1_tile(ti, sink):
            S = (ti == 16)
            p = ti % 2
            FM = {0: 0, 1: 1, 2: 2, 3: 3, 4: 0, 5: 1, 6: 2, 7: 3}
            BM = {1: 2, 2: 3, 3: 7, 4: 4, 5: 5, 6: 6, 7: 7}
            cur = {"m": FM}
            PP = KM(P, p, {"xs_tok", "xsd", "xdt", "xw", "Btok", "xbcS", "zs0", "zs1", "dtr", "dte", "dt", "la", "cum", "ecum", "toend0", "toend", "rg0", "rg1", "rgs0", "rgs1"}, sink, cur=cur, bankmap=FM)

            def psf(i):
                return ps[cur["m"][i]][:]

            def psb(i):
                return ps[cur["m"][i]][:].bitcast(BF16)
            zs, xbcS, sv = zs_[p], xbcS_[p], sv_[p]
            xs_tok, xsd, xdt, xw, Btok = xs_tok_[p], xsd_[p], xdt_[p], xw_[p], Btok_[p]
            V = "S" if S else "P"
            nseq = 16 if S else 1
            maskU, maskLs, samem = C("maskU" + V), C("maskLs" + V), C("same" + V)
            seqmask = C("seqmask" + V)
            if S:
                chk("SA1_0")
            PP.dma("sp", xt[:], x_src(ti), writes=["xt"])
            if S:
                chk("SA1_1")
            rms_uT(xt[:], "xt", gpm[:], "gpm", wk1, uT, "uT", 0, P=PP, psb=psb)
            if S:
                chk("SA1_2")
            PP.dma("sp", uTs[ti], uT[:].rearrange("p k t -> p (k t)"), reads=["uT"])
            chk("A1a")
            if S:
                chk("SA1a")
            for n in range(2):
                pi = 1 + n
                PP.op("pe", mm_group(psf(pi), [(uT[:, kc, :], W1[:, kc, n * 512:(n + 1) * 512]) for kc in range(8)]),
                     reads=["uT", "W1"], writes=["ps%d" % pi])
                PP.op("act", lambda e, n=n, pi=pi: e.activation(out=zs[:, n * 512:(n + 1) * 512], in_=psf(pi), func=AF.Silu),
                     reads=["ps%d" % pi], writes=["zs%d" % n])
            PP.op("pe", mm_group(psf(3)[:, 0:16], [(uT[:, kc, :], W1[:, kc, 2560:2576]) for kc in range(8)]),
                 reads=["uT", "W1"], writes=["ps3"])
            PP.op("dve", lambda e: e.tensor_tensor(out=sv[:, 0:16], in0=psf(3)[:, 0:16], in1=sm[:, 0:16], op=ALU.add),
                 reads=["ps3", "sm_dtb"], writes=["dtr"])
            PP.op("act", lambda e: e.activation(out=sv[:, 16:32], in_=sv[:, 0:16], func=AF.Exp), reads=["dtr"], writes=["dte"])
            PP.op("act", lambda e: e.activation(out=sv[:, 32:48], in_=sv[:, 16:32], func=AF.Ln, bias=1.0), reads=["dte"], writes=["dt"])
            PP.op("dve", lambda e: e.tensor_tensor(out=sv[:, 48:64], in0=sv[:, 32:48], in1=sm[:, 16:32], op=ALU.mult),
                 reads=["dt", "sm_a"], writes=["la"])
            dt_ap = sv[:, 32:48]
            la_ap = sv[:, 48:64]
            chk("A1b")
            if S:
                chk("SA1b")
            if S:
                PP.dma("sp", scv[:], D["st_conv"], writes=["scv"])
                for q in range(3):
                    pi = 4
                    PP.op("pe", transposes([(psf(pi)[:, c4 * 48:(c4 + 1) * 48], scv[:, (q * 4 + c4) * 128:(q * 4 + c4 + 1) * 128],
                                            identf[0:48, 0:48]) for c4 in range(4)]), reads=["scv", "cst"], writes=["ps4"])
                    for c4 in range(4):
                        PP.op("dve", lambda e, q=q, c4=c4: e.tensor_copy(
                            out=vS[:, q * 4 + c4, :, 0:3], in_=psf(4)[:, c4 * 48:(c4 + 1) * 48].rearrange("p (b j) -> p b j", b=16)),
                            reads=["ps4"], writes=["xbcT"])
            for q in range(3):
                pi = 4 + (q % 2)
                def fx(e, q=q, pi=pi):
                    ins = None
                    for c4 in range(4):
                        ci = q * 4 + c4
                        for kc in range(8):
                            ins = e.matmul(psf(pi)[:, c4 * 128:(c4 + 1) * 128], lhsT=W1[:, kc, 1024 + ci * 128:1024 + (ci + 1) * 128],
                                           rhs=uT[:, kc, :], start=(kc == 0), stop=(kc == 7))
                    return ins
                PP.op("pe", fx, reads=["uT", "W1"], writes=["ps%d" % pi])
                if S:
                    for c4 in range(4):
                        ci = q * 4 + c4
                        PP.op("act", lambda e, ci=ci, c4=c4, pi=pi: e.copy(
                            out=vS[:, ci, :, 3:11], in_=psf(pi)[:, c4 * 128:(c4 + 1) * 128].rearrange("p (b j) -> p b j", b=16)),
                            reads=["ps%d" % pi], writes=["xbcT"])
                else:
                    PP.op("act", lambda e, q=q, pi=pi: e.copy(out=vP[:, 4 * q:4 * q + 4, 3:131], in_=psf(pi).rearrange("p (c t) -> p c t", c=4)),
                          reads=["ps%d" % pi], writes=["xbcT"])
            chk("A1c")
            if S:
                chk("SA1c")
            if S or ti == LASTP:
                for n in range(3):
                    pi = 6 + (n % 2)
                    PP.op("pe", mm_group(psf(pi), [(uT[:, kc, :], W1[:, kc, 1024 + n * 512:1024 + (n + 1) * 512]) for kc in range(8)]),
                         reads=["uT", "W1"], writes=["ps%d" % pi])
                    PP.op("act", lambda e, n=n, pi=pi: e.copy(out=xtok[:, n * 512:(n + 1) * 512], in_=psf(pi)),
                         reads=["ps%d" % pi], writes=["xtok"])
                if S:
                    for b in range(16):
                        PP.dma("sp", D["conv_s"][b], xtok[8 * b + 5:8 * b + 8, :], reads=["xtok"])
                else:
                    PP.dma("sp", D["conv_p"], xtok[125:128, :], reads=["xtok"])
            for q in range(3):
                pi = 5 - (q % 2)

                def fcv(e, q=q, pi=pi):
                    ins = None
                    for c4 in range(4):
                        ci = q * 4 + c4
                        for j in range(4):
                            tap = vS[:, ci, :, j:j + 8] if S else vP[:, ci, j:j + 128]
                            ins = e.matmul(psf(pi)[:, c4 * 128:(c4 + 1) * 128], lhsT=diagW[:, ci * 4 + j, :], rhs=tap,
                                           start=(j == 0), stop=(j == 3))
                    return ins
                PP.op("pe", fcv, reads=["xbcT", "diagW"], writes=["ps%d" % pi])
                for c4 in range(4):
                    ci = q * 4 + c4
                    PP.op("act", lambda e, ci=ci, c4=c4, pi=pi: e.activation(out=xbcS[:, ci, :], in_=psf(pi)[:, c4 * 128:(c4 + 1) * 128],
                                                                            func=AF.Silu, bias=cbias[:, ci:ci + 1], scale=1.0),
                          reads=["ps%d" % pi, "cbias"], writes=["xbcS"])
            if not S:
                PP.op("dve", lambda e: e.tensor_copy(out=vP[:, :, 0:3], in_=vP[:, :, 128:131]), reads=["xbcT"], writes=["xbcT"])
            chk("A1d")
            if S:
                chk("SA1d")
            PP.op("pe", transposes([(psb(1)[:, ci * 128:(ci + 1) * 128], xbcS[:, ci, :], identb[:]) for ci in range(8)]),
                 reads=["xbcS", "identb"], writes=["ps1"])
            PP.op("pe", transposes([(psb(2)[:, g * 128:(g + 1) * 128], xbcS[:, 8 + g, :], identb[:]) for g in range(2)]),
                 reads=["xbcS", "identb"], writes=["ps2"])
            PP.op("act", lambda e: e.copy(out=xs_tok[:], in_=psb(1)), reads=["ps1"], writes=["xs_tok"])
            PP.op("dve", lambda e: e.tensor_tensor(out=xdt[:].rearrange("p (h q) -> p h q", h=16),
                                                  in0=psb(1).rearrange("p (h q) -> p h q", h=16),
                                                  in1=dt_ap.unsqueeze(2).to_broadcast([128, 16, 64]), op=ALU.mult),
                 reads=["ps1", "dt"], writes=["xdt"])
            PP.op("dve", lambda e: e.tensor_tensor(out=xsd[:].rearrange("p (h q) -> p h q", h=16),
                                                  in0=xs_tok[:].rearrange("p (h q) -> p h q", h=16),
                                                  in1=sm[:, 32:48].unsqueeze(2).to_broadcast([128, 16, 64]), op=ALU.mult),
                 reads=["xs_tok", "sm_dsk"], writes=["xsd"])
            PP.op("act", lambda e: e.copy(out=Btok[:], in_=psb(2)[:, 0:256]), reads=["ps2"], writes=["Btok"])
            chk("A1e")
            if S:
                chk("SA1e")
            PP.op("pe", lambda e: [e.matmul(psf(3)[:, 0:16], lhsT=maskU, rhs=la_ap, start=True, stop=True),
                                  e.matmul(psf(3)[:, 16:32], lhsT=samem, rhs=la_ap, start=True, stop=True)][-1],
                 reads=["la", "cst"], writes=["ps3"])
            PP.op("dve", lambda e: e.tensor_copy(out=sv[:, 64:96], in_=psf(3)[:, 0:32]), reads=["ps3"], writes=["cum"])
            PP.op("act", lambda e: e.activation(out=sv[:, 96:112], in_=sv[:, 64:80], func=AF.Exp), reads=["cum"], writes=["ecum"])
            PP.op("dve", lambda e: e.tensor_tensor(out=sv[:, 112:128], in0=sv[:, 80:96], in1=sv[:, 64:80], op=ALU.subtract),
                 reads=["cum"], writes=["toend0"])
            PP.op("act", lambda e: e.activation(out=sv[:, 112:128], in_=sv[:, 112:128], func=AF.Exp), reads=["toend0"], writes=["toend"])
            PP.op("dve", lambda e: e.tensor_tensor(out=xw[:].rearrange("p (h q) -> p h q", h=16),
                                                  in0=xdt[:].rearrange("p (h q) -> p h q", h=16),
                                                  in1=sv[:, 112:128].unsqueeze(2).to_broadcast([128, 16, 64]), op=ALU.mult),
                 reads=["xdt", "toend"], writes=["xw"])
            chk("A1f")
            if S:
                chk("SA1f")
            yield
            PP.bankmap = BM
            cur["m"] = BM
            PP.op("pe", lambda e: [e.matmul(psf(6)[:, g * 128:(g + 1) * 128], lhsT=xbcS[:, 8 + g, :], rhs=xbcS[:, 10 + g, :],
                                           start=True, stop=True) for g in range(2)][-1],
                 reads=["xbcS"], writes=["ps6"])
            PP.op("dve", lambda e: e.tensor_tensor(out=cbm[:], in0=psf(6)[:, 0:256].rearrange("p (g t) -> p g t", g=2),
                                                  in1=maskU.unsqueeze(1).to_broadcast([128, 2, 128]), op=ALU.mult),
                 reads=["ps6", "cst"], writes=["cbm"])
            PP.op("dve", lambda e: e.tensor_tensor(out=R[:], in0=la_ap.unsqueeze(2).to_broadcast([128, 16, 128]),
                                                  in1=maskU.unsqueeze(1).to_broadcast([128, 16, 128]), op=ALU.mult),
                 reads=["la", "cst"], writes=["R"])
            for q in range(4):
                pi = 4 + (q % 2)
                PP.op("pe", lambda e, q=q, pi=pi: e.matmul(psf(pi), lhsT=maskLs, rhs=R[:, 4 * q:4 * q + 4, :].rearrange("p h t -> p (h t)"),
                                                          start=True, stop=True), reads=["R", "cst"], writes=["ps%d" % pi])
                PP.op("act", lambda e, q=q, pi=pi: e.activation(out=Lq[q % 2][:], in_=psf(pi), func=AF.Exp),
                     reads=["ps%d" % pi], writes=["Lq%d" % (q % 2)])
                PP.op("dve", lambda e, q=q: e.tensor_tensor(out=WT[:, 4 * q:4 * q + 4, :], in0=Lq[q % 2][:].rearrange("p (h t) -> p h t", h=4),
                                                           in1=cbm[:, q // 2, :].unsqueeze(1).to_broadcast([128, 4, 128]), op=ALU.mult),
                     reads=["Lq%d" % (q % 2), "cbm"], writes=["WT%d" % q])
            chk("A1g")
            if S:
                chk("SA1g")
            for half in range(2):
                pi = 4 + half
                def fy(e, half=half, pi=pi):
                    ins = None
                    for hh in range(8):
                        h = half * 8 + hh
                        ins = e.matmul(psf(pi)[:, hh * 64:(hh + 1) * 64], lhsT=WT[:, h, :], rhs=xdt[:, h * 64:(h + 1) * 64],
                                       start=True, stop=True)
                    return ins
                PP.op("pe", fy, reads=["WT0", "WT1", "WT2", "WT3", "xdt"], writes=["ps%d" % pi])
                PP.op("act", lambda e, half=half, pi=pi: e.copy(out=yi[:, half * 512:(half + 1) * 512], in_=psf(pi)),
                     reads=["ps%d" % pi], writes=["yi%d" % half])
            chk("A1h")
            if S:
                chk("SA1h")
            PP.op("dve", lambda e: e.tensor_tensor(out=rhsd[:, 0:nseq, :], in0=seqmask.unsqueeze(2).to_broadcast([128, nseq, 16]),
                                                  in1=la_ap.unsqueeze(1).to_broadcast([128, nseq, 16]), op=ALU.mult),
                 reads=["la", "cst"], writes=["rhsd"])
            PP.op("pe", lambda e: e.matmul(psf(3)[:, 0:nseq * 16], lhsT=onesf, rhs=rhsd[:, 0:nseq, :].rearrange("p b h -> p (b h)"),
                                          start=True, stop=True), reads=["rhsd", "cst"], writes=["ps3"])
            PP.op("act", lambda e: e.activation(out=decay[:, 0:nseq * 16], in_=psf(3)[:, 0:nseq * 16], func=AF.Exp),
                 reads=["ps3"], writes=["decay"])
            chk("A1i")
            if S:
                chk("SA1i")
            def stage_load(bq):
                kq = bq % 2
                hTq, hTbq, hkq, hbkq = (hT, hTb, "hT", "hTb") if kq == 0 else (hT2, hTb2, "hT2", "hTb2")
                for bb in ([0, 1] if bq == 0 else [bq + 1]):
                    if bb < nseq:
                        PP.dma("sp", hnat[bb % 2][:], D["st_ssm"][bb].rearrange("(hp h2) q n -> (h2 q) hp n", h2=2), writes=["hnat%d" % (bb % 2)])
                for half in range(2):
                    pi = 1 + half
                    PP.op("pe", transposes([(psf(pi)[:, j * 128:(j + 1) * 128], hnat[kq][:, half * 4 + j, :], identf) for j in range(4)]),
                          reads=["hnat%d" % kq, "cst"], writes=["ps%d" % pi])
                    PP.op("act", lambda e, half=half, pi=pi, hTq=hTq: e.copy(out=hTq[:, half * 512:(half + 1) * 512], in_=psf(pi)),
                          reads=["ps%d" % pi], writes=[hkq])
                    PP.op("act", lambda e, half=half, pi=pi, hTbq=hTbq: e.copy(out=hTbq[:, half * 512:(half + 1) * 512], in_=psf(pi)),
                          reads=["ps%d" % pi], writes=[hbkq])

            for b in range(nseq):
                k2 = b % 2
                hTx, hTbx, hk, hbk = (hT, hTb, "hT", "hTb") if k2 == 0 else (hT2, hTb2, "hT2", "hTb2")
                if S and stop_after == "X2":
                    pass
                elif S:
                    if b == 0:
                        stage_load(0)
                    PP.op("dve", lambda e, k2=k2, b=b, hTx=hTx, hTbx=hTbx: e.tensor_copy(out=CTp[k2][:, :, 8 * b:8 * b + 8], in_=xbcS[:, 10:12, 8 * b:8 * b + 8]),
                         reads=["xbcS"], writes=["CTp%d" % k2])
                    PP.op("dve", lambda e, k2=k2, b=b, hTx=hTx, hTbx=hTbx: e.tensor_scalar_mul(out=Bp[k2][:], in0=Btok[:], scalar1=seqmask[:, b:b + 1]),
                         reads=["Btok", "cst"], writes=["Bp%d" % k2])
                    ct = [CTp[k2][:, g, :] for g in range(2)]
                    bt = [Bp[k2][:, g * 128:(g + 1) * 128] for g in range(2)]
                    ctk, btk = "CTp%d" % k2, "Bp%d" % k2
                else:
                    ct = [xbcS[:, 10 + g, :] for g in range(2)]
                    bt = [Btok[:, g * 128:(g + 1) * 128] for g in range(2)]
                    ctk, btk = "xbcS", "Btok"
                for g in range(2):
                    PP.op("pe", lambda e, g=g, ct=ct, b=b, hTx=hTx, hTbx=hTbx: e.matmul(psf(6 + g), lhsT=ct[g], rhs=hTbx[:, g * 512:(g + 1) * 512],
                                                                  start=(b == 0), stop=(b == nseq - 1)),
                         reads=[ctk, hbk], writes=["ps%d" % (6 + g)])
                if S:
                    PP.op("dve", lambda e, k2=k2, b=b, hTx=hTx, hTbx=hTbx: e.memset(CTp[k2][:, :, 8 * b:8 * b + 8], 0.0), reads=[], writes=["CTp%d" % k2])
                for g in range(2):
                    pi = 4 + g
                    PP.op("pe", lambda e, g=g, bt=bt, pi=pi, hTx=hTx, hTbx=hTbx: e.matmul(psf(pi), lhsT=bt[g], rhs=xw[:, g * 512:(g + 1) * 512], start=True, stop=True),
                         reads=[btk, "xw"], writes=["ps%d" % pi])
                    PP.op("dve", lambda e, g=g, b=b, hTx=hTx, hTbx=hTbx: e.tensor_tensor(
                        out=hTx[:, g * 512:(g + 1) * 512].rearrange("p (h q) -> p h q", h=8),
                        in0=hTx[:, g * 512:(g + 1) * 512].rearrange("p (h q) -> p h q", h=8),
                        in1=decay[:, b * 16 + g * 8:b * 16 + g * 8 + 8].unsqueeze(2).to_broadcast([128, 8, 64]), op=ALU.mult),
                        reads=[hk, "decay"], writes=[hk])
                    PP.op("dve", lambda e, g=g, pi=pi, hTx=hTx, hTbx=hTbx: e.tensor_tensor(out=hTx[:, g * 512:(g + 1) * 512], in0=hTx[:, g * 512:(g + 1) * 512],
                                                                     in1=psf(pi), op=ALU.add), reads=[hk, "ps%d" % pi], writes=[hk])
                if S and b + 1 < nseq:
                    stage_load(b + 1)
                if not S:
                    PP.op("dve", lambda e, hTx=hTx, hTbx=hTbx: e.tensor_copy(out=hTbx[:], in_=hTx[:]), reads=[hk], writes=[hbk])
                if (S and stop_after not in ('X1', 'X2')) or ti == LASTP:
                    for half in range(2):
                        pi = 4 + half
                        PP.op("pe", transposes([(psf(pi)[:, j * 128:(j + 1) * 128], hTx[:, (half * 4 + j) * 128:(half * 4 + j + 1) * 128], identf)
                                               for j in range(4)]), reads=[hk, "cst"], writes=["ps%d" % pi])
                        PP.op("act", lambda e, half=half, pi=pi, k2=k2, hTx=hTx, hTbx=hTbx: e.copy(
                            out=hout[k2][:, half * 4:half * 4 + 4, :].rearrange("p j n -> p (j n)"), in_=psf(pi)),
                            reads=["ps%d" % pi], writes=["hout%d" % k2])
                    dst = D["ssm_s"][b] if S else D["ssm_p"]
                    PP.dma("sp", dst.rearrange("(hp h2) q n -> (h2 q) hp n", h2=2), hout[k2][:], reads=["hout%d" % k2])
            chk("A1j")
            if S:
                chk("SA1j")
            if stop_after == "DBG" and ti == 0:
                PP.dma("sp", D["yp"][0:128, 0:256], sv[:], reads=["cum", "ecum", "toend", "la", "dt"])
                PP.dma("pool", D["yp"][128:256, :], xw[:], reads=["xw"])
                PP.dma("pool", D["yp"][256:384, 0:256], Btok[:], reads=["Btok"])
                PP.dma("pool", D["yp"][384:512, :], xdt[:], reads=["xdt"])
                PP.dma("pool", D["yp"][512:640, :], xs_tok[:], reads=["xs_tok"])
                PP.dma("sp", D["yp"][640:768, :], hTx[:], reads=[hk])
                PP.dma("sp", D["yp"][768:896, 0:256], decay[:], reads=["decay"])
                PP.dma("sp", D["yp"][896:1024, :], xc[:, 0:8, :].rearrange("p c t -> p (c t)"), reads=["xc%d" % i for i in range(12)])
                chk("DBG")
            for g in range(2):
                sl = slice(g * 512, (g + 1) * 512)
                PP.op("dve", lambda e, g=g, sl=sl: e.tensor_tensor(
                    out=yg[:, sl].rearrange("p (h q) -> p h q", h=8), in0=psf(6 + g).rearrange("p (h q) -> p h q", h=8),
                    in1=sv[:, 96 + g * 8:96 + g * 8 + 8].unsqueeze(2).to_broadcast([128, 8, 64]), op=ALU.mult),
                    reads=["ps%d" % (6 + g), "ecum"], writes=["yg%d" % g])
                PP.op("dve", lambda e, sl=sl: e.tensor_tensor(out=yg[:, sl], in0=yg[:, sl], in1=yi[:, sl], op=ALU.add),
                     reads=["yg%d" % g, "yi%d" % g], writes=["yg%d" % g])
                PP.op("dve", lambda e, sl=sl: e.tensor_tensor(out=yg[:, sl], in0=yg[:, sl], in1=xsd[:, sl], op=ALU.add),
                     reads=["yg%d" % g, "xsd"], writes=["yg%d" % g])
                PP.op("dve", lambda e, sl=sl: e.tensor_tensor(out=yg[:, sl], in0=yg[:, sl], in1=zs[:, sl], op=ALU.mult),
                     reads=["yg%d" % g, "zs%d" % g], writes=["yg%d" % g])
                PP.op("act", lambda e, g=g, sl=sl: e.activation(out=junk2[:, sl], in_=yg[:, sl], func=AF.Square,
                                                               accum_out=sv[:, 128 + g:129 + g]), reads=["yg%d" % g], writes=["junk2", "rg%d" % g])
                PP.op("act", lambda e, g=g: e.activation(out=sv[:, 130 + g:131 + g], in_=sv[:, 128 + g:129 + g], func=AF.Sqrt, bias=EPS,
                                                        scale=1.0 / 512), reads=["rg%d" % g], writes=["rgs%d" % g])
                PP.op("dve", lambda e, g=g: e.reciprocal(out=sv[:, 130 + g:131 + g], in_=sv[:, 130 + g:131 + g]),
                     reads=["rgs%d" % g], writes=["rgs%d" % g])
                PP.op("dve", lambda e, g=g, sl=sl: e.scalar_tensor_tensor(out=yn[:, sl], in0=yg[:, sl], scalar=sv[:, 130 + g:131 + g],
                                                                          in1=gso[:, sl], op0=ALU.mult, op1=ALU.mult),
                     reads=["yg%d" % g, "rgs%d" % g, "gso"], writes=["yn%d" % g])
            PP.op("pe", transposes([(psb(4)[:, kc * 128:(kc + 1) * 128], yn[:, kc * 128:(kc + 1) * 128], identb[:]) for kc in range(8)]),
                 reads=["yn0", "yn1", "identb"], writes=["ps4"])
            PP.op("act", lambda e: e.copy(out=yT[:].rearrange("p k t -> p (k t)"), in_=psb(4)), reads=["ps4"], writes=["yT"])
            PP.dma("sp", yTs[ti], yT[:].rearrange("p k t -> p (k t)"), reads=["yT"])
            if ti == 0:
                chk("T0")

        W2 = nc.alloc_sbuf_tensor_at("W2e", [128, 8, 3080], BF16, offset=phase_base)
        WO = nc.alloc_sbuf_tensor_at("WOe", [128, 16, 1024], BF16, offset=phase_base + 49280)

        def prefetch_a2():
            P.barrier(engs=("pool",))
            load_w(W2, D["w_in"], 0, 8, 2576, 5656, "W2")
            load_w(WO, D["w_out"], 0, 16, 0, 1024, "WO")
        run_pipelined(P, a1_tile, list(range(NPT[0])) + [16], after_last_front=prefetch_a2)

        chk("A1")

        new_phase()
        alloc([128, 8, 3080], BF16)
        alloc([128, 16, 1024], BF16)
        gml = alloc([128, 1024], F32)
        gpo = alloc([128, 1024], F32)
        bcast_load(gml[:], D["g_mlstm_out"], "gml", 1024)
        bcast_load(gpo[:], D["g_post_mix"], "gpo", 1024)
        bg = alloc([128, 8], F32)
        bcast_load(bg[:, 0:4], D["b_igate"], "bg", 4)
        bcast_load(bg[:, 4:8], D["b_fgate"], "bg2", 4)
        wk2 = {"junk": alloc([128, 1024], BF16), "ss": alloc([128, 4], F32)}
        xt2_ = [alloc([128, 1024], F32) for _ in range(2)]
        uT2_ = [alloc([128, 8, 128], BF16) for _ in range(2)]
        yTl_ = [alloc([128, 8, 128], BF16) for _ in range(2)]
        ktok_ = [alloc([128, 4, 128], BF16) for _ in range(2)]
        v_ext_ = [alloc([128, 4, 257], BF16) for _ in range(2)]
        sig_ = [alloc([128, 1024], F32) for _ in range(2)]
        sg_ = [alloc([128, 64], F32) for _ in range(2)]
        sg2 = alloc([128, 64], F32)
        qT_ = [alloc([128, 4, 128], BF16) for _ in range(2)]
        kT_ = [alloc([128, 4, 128], BF16) for _ in range(2)]
        rhsB = alloc([128, 4, 128], F32)
        tmpB = alloc([128, 4, 128], F32)
        Dm = alloc([128, 4, 128], F32)
        Sb = alloc([128, 4, 128], BF16)
        ST = alloc([128, 4, 128], BF16)
        numA = alloc([128, 4, 257], F32)
        num = alloc([128, 4, 256], F32)
        hn = alloc([128, 1024], BF16)
        hTT = alloc([128, 8, 128], BF16)
        Cst = [alloc([128, 4, 257], F32) for _ in range(2)]
        Cb_ = [alloc([128, 4, 257], BF16) for _ in range(2)]
        Cb = Cb_[0]
        nall = alloc([64, 128], F32)
        nT = alloc([128, 16, 4], F32)
        nout = alloc([128, 16, 4], F32)
        nrow = alloc([64, 128], F32)
        kw = alloc([128, 4, 128], BF16)
        kwp = [alloc([128, 4, 128], BF16) for _ in range(2)]
        qTp = [alloc([128, 4, 128], BF16) for _ in range(2)]
        rhsc = alloc([128, 16, 4], F32)
        scale_bc = alloc([128, 64], F32)
        x1t = alloc([128, 1024], F32)
        mpv = alloc([128, 4], F32)
        mpvS = alloc([128, 4], F32)
        for k in range(2):
            P.op("dve", lambda e, k=k: e.memset(v_ext_[k][:, :, 256:257], 1.0), writes=["v_ext1c@%d" % k])
        P.op("dve", lambda e: e.memset(mpv[:], 0.0), writes=["mprev"])
        P.op("dve", lambda e: e.memset(Cst[0][:], 0.0), writes=["Cst0", "Cst0n"])
        P.op("dve", lambda e: e.memset(Cb[:], 0.0), writes=["Cb0"])
        for k in range(2):
            P.op("dve", lambda e, k=k: e.memset(qTp[k][:], 0.0), writes=["qTp%d" % k])

        def a2_tile(ti, sink):
            S = (ti == 16)
            p = ti % 2
            mp, mpk = (mpvS, "mprevS") if S else (mpv, "mprev")
            FM = {1: 0, 2: 1, 3: 0, 4: 1, 5: 0, 6: 1, 7: 1}
            BM = {8: 6, 9: 7, 6: 6, 7: 7, 0: 6, 1: 7, 2: 2, 3: 3, 4: 4, 5: 5}
            cur = {"m": FM}
            PP = KM(P, p, set(['uT2', 'yTl', 'xt2', 'ktok', 'v_ext0', 'v_ext1', 'v_ext1c', 'gates', 'sig0', 'sig1', 'qT', 'kT', 'ge', 'lfn', 'csl', 'beta', 'cmx', 'mu', 'nmu', 'mt', 'muend', 'sint']), sink, cur=cur, bankmap=FM)

            def psf(i):
                return ps[cur["m"][i]][:]

            def psb(i):
                return ps[cur["m"][i]][:].bitcast(BF16)
            xt2, uT2, yTl, ktok, v_ext, sig, sg, qT, kT = xt2_[p], uT2_[p], yTl_[p], ktok_[p], v_ext_[p], sig_[p], sg_[p], qT_[p], kT_[p]
            V = "S" if S else "P"
            nseq = 16 if S else 1
            maskU, negT, lastbc = C("maskU" + V), C("negT" + V), C("lastbc" + V)
            seqmask, lastsel = C("seqmask" + V), C("lastsel" + V)
            PP.dma("sp", uT2[:].rearrange("p k t -> p (k t)"), uTs[ti], writes=["uT2"])
            PP.dma("sp", yTl[:].rearrange("p k t -> p (k t)"), yTs[ti], writes=["yTl"])
            PP.dma("sp", xt2[:], x_src(ti), writes=["xt2"])
            if S:
                for b in range(16):
                    P.alias.setdefault(mpk, []).append("%s#%d" % (mpk, b))
                    PP.dma("sp", mp[8 * b:8 * b + 8, 0:4], D["st_m"][b:b + 1, :].partition_broadcast(8), writes=["%s#%d" % (mpk, b)])
                PP.dma("sp", nall[:], D["st_n"].rearrange("b h d -> (b h) d"), writes=["nall"])
                PP.op("pe", lambda e: e.transpose(psf(2)[:, 0:64], nall[:], identf[0:64, 0:64]), reads=["nall", "cst"], writes=["ps2"])
                PP.op("act", lambda e: e.copy(out=nT[:].rearrange("p b h -> p (b h)"), in_=psf(2)[:, 0:64]), reads=["ps2"], writes=["nT"])

            def tokmm(pi, c0, c1):
                PP.op("pe", mm_group(psf(pi)[:, 0:c1 - c0], [(uT2[:, kc, :], W2[:, kc, c0:c1]) for kc in range(8)]),
                     reads=["uT2", "W2"], writes=["ps%d" % pi])
            tokmm(1, 512, 1024)
            PP.op("act", lambda e: e.copy(out=ktok[:].rearrange("p h d -> p (h d)"), in_=psf(1)), reads=["ps1"], writes=["ktok"])
            for n in range(2):
                tokmm(2 + n, 1024 + n * 512, 1536 + n * 512)
                PP.op("act", lambda e, n=n: e.copy(out=v_ext[:, 2 * n:2 * n + 2, 0:256], in_=psf(2 + n).rearrange("p (h v) -> p h v", h=2)),
                     reads=["ps%d" % (2 + n)], writes=["v_ext%d" % n])
            vkeys = ["v_ext0", "v_ext1", "v_ext1c"]
            tokmm(4, 2048, 2056)
            PP.op("dve", lambda e: e.tensor_tensor(out=sg[:, 0:8], in0=psf(4)[:, 0:8], in1=bg[:], op=ALU.add),
                 reads=["ps4", "bg", "bg2"], writes=["gates"])
            for n in range(2):
                tokmm(5 + n, 2056 + n * 512, 2568 + n * 512)
                PP.op("act", lambda e, n=n: e.activation(out=sig[:, n * 512:(n + 1) * 512], in_=psf(5 + n), func=AF.Sigmoid),
                     reads=["ps%d" % (5 + n)], writes=["sig%d" % n])
            for (pi, c0, dst, dkey, scl) in ((7, 0, qT, "qT", 128.0 ** -0.5), (1, 512, kT, "kT", 1.0)):
                def fq(e, pi=pi, c0=c0):
                    ins = None
                    for h in range(4):
                        for kc in range(8):
                            ins = e.matmul(psf(pi)[:, h * 128:(h + 1) * 128], lhsT=W2[:, kc, c0 + h * 128:c0 + (h + 1) * 128],
                                           rhs=uT2[:, kc, :], start=(kc == 0), stop=(kc == 7))
                    return ins
                PP.op("pe", fq, reads=["uT2", "W2"], writes=["ps%d" % pi])
                PP.op("act", lambda e, pi=pi, dst=dst, scl=scl: e.mul(out=dst[:].rearrange("p h t -> p (h t)"), in_=psf(pi), mul=scl),
                     reads=["ps%d" % pi], writes=[dkey])
            yield
            PP.bankmap = BM
            cur["m"] = BM
            PP.op("act", lambda e: e.activation(out=sg[:, 8:12], in_=sg[:, 4:8], func=AF.Exp, scale=-1.0), reads=["gates"], writes=["ge"])
            PP.op("act", lambda e: e.activation(out=sg[:, 12:16], in_=sg[:, 8:12], func=AF.Ln, bias=1.0), reads=["ge"], writes=["lfn"])
            PP.op("pe", lambda e: e.matmul(psf(8)[:, 8:12], lhsT=maskU, rhs=sg[:, 12:16], start=True, stop=True),
                 reads=["lfn", "cst"], writes=["ps8"])
            PP.op("dve", lambda e: e.tensor_copy(out=sg[:, 60:64], in_=psf(8)[:, 8:12]), reads=["ps8"], writes=["csl"])
            PP.op("dve", lambda e: e.tensor_tensor(out=sg[:, 16:20], in0=sg[:, 0:4], in1=sg[:, 60:64], op=ALU.add),
                 reads=["gates", "csl"], writes=["beta"])
            PP.op("dve", lambda e: e.tensor_tensor(out=rhsB[:], in0=identf.unsqueeze(1).to_broadcast([128, 4, 128]),
                                                  in1=sg[:, 16:20].unsqueeze(2).to_broadcast([128, 4, 128]), op=ALU.mult),
                 reads=["beta", "cst"], writes=["rhsB"])
            PP.op("pe", lambda e: e.matmul(psf(9), lhsT=onesf, rhs=rhsB[:].rearrange("p h s -> p (h s)"), start=True, stop=True),
                 reads=["rhsB", "cst"], writes=["ps9"])
            PP.op("dve", lambda e: e.tensor_tensor(out=tmpB[:], in0=psf(9).rearrange("p (h s) -> p h s", h=4),
                                                  in1=negT.unsqueeze(1).to_broadcast([128, 4, 128]), op=ALU.add),
                 reads=["ps9", "cst"], writes=["tmpB"])
            PP.op("dve", lambda e: e.reduce_max(out=sg[:, 20:24], in_=tmpB[:], axis=AX.X), reads=["tmpB"], writes=["cmx"])
            PP.op("dve", lambda e: e.tensor_tensor(out=sg[:, 24:28], in0=sg[:, 20:24], in1=mp[:, 0:4], op=ALU.max),
                 reads=["cmx", mpk], writes=["mu"])
            PP.op("dve", lambda e: e.tensor_scalar_mul(out=sg[:, 36:40], in0=sg[:, 24:28], scalar1=-1.0), reads=["mu"], writes=["nmu"])
            for h in range(4):
                PP.op("act", lambda e, h=h: e.activation(out=Dm[:, h, :], in_=tmpB[:, h, :], func=AF.Exp, bias=sg[:, 36 + h:37 + h], scale=1.0),
                     reads=["tmpB", "nmu"], writes=["Dm%d" % h])
            PP.op("pe", lambda e: [e.matmul(psf(6)[:, h * 128:(h + 1) * 128], lhsT=qT[:, h, :], rhs=kT[:, h, :], start=True, stop=True)
                                  for h in range(4)][-1], reads=["qT", "kT"], writes=["ps6"])
            PP.op("dve", lambda e: e.tensor_tensor(out=Sb[:], in0=psf(6).rearrange("p (h s) -> p h s", h=4), in1=Dm[:], op=ALU.mult),
                 reads=["ps6", "Dm0", "Dm1", "Dm2", "Dm3"], writes=["Sb"])
            PP.op("pe", transposes([(psb(7)[:, h * 128:(h + 1) * 128], Sb[:, h, :], identb[:]) for h in range(4)]),
                 reads=["Sb", "identb"], writes=["ps7"])
            PP.op("act", lambda e: e.copy(out=ST[:].rearrange("p h t -> p (h t)"), in_=psb(7)[:, 0:512]), reads=["ps7"], writes=["ST"])
            PP.op("dve", lambda e: e.tensor_tensor(out=sg2[:, 20:24], in0=mp[:, 0:4], in1=sg[:, 24:28], op=ALU.subtract),
                 reads=[mpk, "mu"], writes=["scd"])
            PP.op("act", lambda e: e.activation(out=sg[:, 40:44], in_=sg2[:, 20:24], func=AF.Exp), reads=["scd"], writes=["sint"])
            PP.op("dve", lambda e: e.tensor_tensor(out=sg2[:, 24:28], in0=sg[:, 60:64], in1=sg[:, 24:28], op=ALU.subtract),
                 reads=["csl", "mu"], writes=["emn0"])
            PP.op("act", lambda e: e.activation(out=sg2[:, 24:28], in_=sg2[:, 24:28], func=AF.Exp), reads=["emn0"], writes=["emn"])
            PP.op("dve", lambda e: e.tensor_tensor(out=sg[:, 28:32], in0=sg[:, 24:28], in1=sg[:, 60:64], op=ALU.subtract),
                 reads=["mu", "csl"], writes=["mt"])
            PP.op("pe", lambda e: e.matmul(psf(8)[:, 16:24], lhsT=lastbc, rhs=sg[:, 24:32], start=True, stop=True),
                 reads=["mu", "mt", "cst"], writes=["ps8"])
            PP.op("dve", lambda e: e.tensor_copy(out=sg[:, 44:52], in_=psf(8)[:, 16:24]), reads=["ps8"], writes=["muend"])
            PP.op("dve", lambda e: e.tensor_tensor(out=sg2[:, 0:4], in0=sg[:, 16:20], in1=sg[:, 44:48], op=ALU.subtract),
                 reads=["beta", "muend"], writes=["wend0"])
            PP.op("act", lambda e: e.activation(out=sg2[:, 0:4], in_=sg2[:, 0:4], func=AF.Exp), reads=["wend0"], writes=["wend"])
            PP.op("dve", lambda e: e.tensor_tensor(out=kw[:], in0=ktok[:], in1=sg2[:, 0:4].unsqueeze(2).to_broadcast([128, 4, 128]), op=ALU.mult),
                 reads=["ktok", "wend"], writes=["kw"])
            PP.op("dve", lambda e: e.tensor_tensor(out=rhsc[:, 0:nseq, :], in0=lastsel.unsqueeze(2).to_broadcast([128, nseq, 4]),
                                                  in1=sg2[:, 20:24].unsqueeze(1).to_broadcast([128, nseq, 4]), op=ALU.mult),
                 reads=["scd", "cst"], writes=["rhsc"])
            PP.op("pe", lambda e: e.matmul(psf(8)[:, 32:32 + nseq * 4], lhsT=onesf, rhs=rhsc[:, 0:nseq, :].rearrange("p b h -> p (b h)"),
                                          start=True, stop=True), reads=["rhsc", "cst"], writes=["ps8"])
            PP.op("act", lambda e: e.activation(out=scale_bc[:, 0:nseq * 4], in_=psf(8)[:, 32:32 + nseq * 4], func=AF.Exp),
                 reads=["ps8"], writes=["scale_bc"])
            for h in range(4):
                pi = h % 2
                PP.op("pe", lambda e, h=h, pi=pi: e.matmul(psf(pi)[:, 0:257], lhsT=ST[:, h, :], rhs=v_ext[:, h, :], start=True, stop=True),
                     reads=["ST"] + vkeys, writes=["ps%d" % pi])
                PP.op("act", lambda e, h=h, pi=pi: e.copy(out=numA[:, h, :], in_=psf(pi)[:, 0:257]), reads=["ps%d" % pi], writes=["numA%d" % h])
            for b in range(nseq):
                k2 = b % 2 if S else 0
                ck = "Cst%d" % k2
                if S:
                    for bb in ([0, 1] if b == 0 else [b + 1]):
                        if bb < nseq:
                            PP.dma("sp", Cst[bb % 2][:, :, 0:256], D["st_c"][bb].rearrange("h d v -> d h v"), writes=["Cst%d" % (bb % 2)])
                    PP.op("dve", lambda e, k2=k2, b=b: e.tensor_copy(out=Cst[k2][:, :, 256], in_=nT[:, b, :]), reads=["nT"], writes=[ck + "n"])
                    PP.op("act", lambda e, k2=k2: e.copy(out=Cb_[k2][:], in_=Cst[k2][:]), reads=[ck, ck + "n"], writes=["Cb%d" % k2])
                    PP.op("dve", lambda e, k2=k2, b=b: e.tensor_copy(out=qTp[k2][:, :, 8 * b:8 * b + 8], in_=qT[:, :, 8 * b:8 * b + 8]),
                         reads=["qT"], writes=["qTp%d" % k2])
                    PP.op("dve", lambda e, k2=k2, b=b: e.tensor_scalar_mul(out=kwp[k2][:].rearrange("p h d -> p (h d)"),
                                                                           in0=kw[:].rearrange("p h d -> p (h d)"), scalar1=seqmask[:, b:b + 1]),
                         reads=["kw", "cst"], writes=["kwp%d" % k2])
                    qx, qk = qTp[k2], "qTp%d" % k2
                    kx, kk = kwp[k2], "kwp%d" % k2
                else:
                    qx, qk = qT, "qT"
                    kx, kk = kw, "kw"
                for h in range(4):
                    PP.op("pe", lambda e, h=h, qx=qx, b=b, k2=k2: e.matmul(psf(2 + h)[:, 0:257], lhsT=qx[:, h, :], rhs=Cb_[k2][:, h, :],
                                                                  start=(b == 0), stop=(b == nseq - 1)),
                         reads=[qk, "Cb%d" % k2], writes=["ps%d" % (2 + h)])
                if S:
                    PP.op("dve", lambda e, k2=k2, b=b: e.memset(qTp[k2][:, :, 8 * b:8 * b + 8], 0.0), writes=["qTp%d" % k2])
                for h in range(4):
                    pi = h % 2
                    PP.op("pe", lambda e, h=h, pi=pi, kx=kx: e.matmul(psf(pi)[:, 0:257], lhsT=kx[:, h, :], rhs=v_ext[:, h, :], start=True, stop=True),
                         reads=[kk] + vkeys, writes=["ps%d" % pi])
                    PP.op("dve", lambda e, h=h, pi=pi, k2=k2, b=b: e.scalar_tensor_tensor(
                        out=Cst[k2][:, h, :], in0=Cst[k2][:, h, :], scalar=scale_bc[:, b * 4 + h:b * 4 + h + 1], in1=psf(pi)[:, 0:257],
                        op0=ALU.mult, op1=ALU.add), reads=[ck, ck + "n", "scale_bc", "ps%d" % pi], writes=[ck, ck + "n"])
                if S:
                    PP.dma("sp", D["c_s"][b].rearrange("h d v -> d h v"), Cst[k2][:, :, 0:256], reads=[ck, ck + "n"])
                    PP.op("dve", lambda e, k2=k2, b=b: e.tensor_copy(out=nout[:, b, :], in_=Cst[k2][:, :, 256]), reads=[ck, ck + "n"], writes=["nout"])
                else:
                    if ti == LASTP:
                        PP.dma("sp", D["c_p"].rearrange("h d v -> d h v"), Cst[0][:, :, 0:256], reads=[ck, ck + "n"])
                        PP.dma("sp", D["n_p"].rearrange("h d -> d h"), Cst[0][:, :, 256], reads=[ck, ck + "n"], allow_slow_non_contiguous=True)
            if S:
                PP.op("pe", lambda e: e.transpose(psf(8)[0:64, 0:128], nout[:].rearrange("p b h -> p (b h)"), identf), reads=["nout", "cst"], writes=["ps8"])
                PP.op("act", lambda e: e.copy(out=nrow[:], in_=psf(8)[0:64, 0:128]), reads=["ps8"], writes=["nrow"])
                PP.dma("sp", D["n_s"].rearrange("b h d -> (b h) d"), nrow[:], reads=["nrow"])
            for h in range(4):
                PP.op("dve", lambda e, h=h: e.scalar_tensor_tensor(out=num[:, h, :], in0=psf(2 + h)[:, 0:256], scalar=sg[:, 40 + h:41 + h],
                                                                   in1=numA[:, h, 0:256], op0=ALU.mult, op1=ALU.add),
                     reads=["ps%d" % (2 + h), "sint", "numA%d" % h], writes=["num%d" % h])
                PP.op("dve", lambda e, h=h: e.scalar_tensor_tensor(out=sg2[:, 4 + h:5 + h], in0=psf(2 + h)[:, 256:257], scalar=sg[:, 40 + h:41 + h],
                                                                   in1=numA[:, h, 256:257], op0=ALU.mult, op1=ALU.add),
                     reads=["ps%d" % (2 + h), "sint", "numA%d" % h], writes=["den%d" % h])
            if not S:
                PP.op("act", lambda e: e.copy(out=Cb[:], in_=Cst[0][:]), reads=["Cst0", "Cst0n"], writes=["Cb0"])
            dkeys = ["den%d" % h for h in range(4)]
            PP.op("dve", lambda e: e.tensor_scalar_mul(out=sg2[:, 28:32], in0=sg2[:, 4:8], scalar1=-1.0), reads=dkeys, writes=["denn"])
            PP.op("dve", lambda e: e.tensor_tensor(out=sg2[:, 4:8], in0=sg2[:, 4:8], in1=sg2[:, 28:32], op=ALU.max), reads=dkeys + ["denn"], writes=["dena"])
            PP.op("dve", lambda e: e.tensor_tensor(out=sg2[:, 4:8], in0=sg2[:, 4:8], in1=sg2[:, 24:28], op=ALU.max), reads=["dena", "emn"], writes=["denm"])
            PP.op("dve", lambda e: e.reciprocal(out=sg2[:, 8:12], in_=sg2[:, 4:8]), reads=["denm"], writes=["rden"])
            for h in range(4):
                PP.op("act", lambda e, h=h: e.activation(out=wk2["junk"][:, 0:256], in_=num[:, h, :], func=AF.Square, scale=sg2[:, 8 + h:9 + h],
                                                        accum_out=sg2[:, 12 + h:13 + h]), reads=["num%d" % h, "rden"], writes=["junk", "ssh%d" % h])
            skeys = ["ssh%d" % h for h in range(4)]
            PP.op("act", lambda e: e.activation(out=sg2[:, 16:20], in_=sg2[:, 12:16], func=AF.Sqrt, bias=EPS, scale=1.0 / 256), reads=skeys, writes=["rsh"])
            PP.op("dve", lambda e: e.reciprocal(out=sg2[:, 16:20], in_=sg2[:, 16:20]), reads=["rsh"], writes=["rsh"])
            PP.op("dve", lambda e: e.tensor_tensor(out=sg2[:, 16:20], in0=sg2[:, 16:20], in1=sg2[:, 8:12], op=ALU.mult), reads=["rsh", "rden"], writes=["comb"])
            for h in range(4):
                PP.op("dve", lambda e, h=h: e.scalar_tensor_tensor(out=num[:, h, :], in0=num[:, h, :], scalar=sg2[:, 16 + h:17 + h],
                                                                   in1=gml[:, h * 256:(h + 1) * 256], op0=ALU.mult, op1=ALU.mult),
                     reads=["num%d" % h, "comb", "gml"], writes=["num%d" % h])
            nkeys = ["num%d" % h for h in range(4)]
            PP.op("dve", lambda e: e.tensor_tensor(out=hn[:], in0=num[:].rearrange("p h v -> p (h v)"), in1=sig[:], op=ALU.mult),
                 reads=nkeys + ["sig0", "sig1"], writes=["hn"])
            PP.op("pe", transposes([(psb(7)[:, kc * 128:(kc + 1) * 128], hn[:, kc * 128:(kc + 1) * 128], identb[:]) for kc in range(8)]),
                 reads=["hn", "identb"], writes=["ps7"])
            PP.op("act", lambda e: e.copy(out=hTT[:].rearrange("p k t -> p (k t)"), in_=psb(7)), reads=["ps7"], writes=["hTT"])
            for n in range(2):
                PP.op("pe", mm_group(psf(n), [(yTl[:, kc, :], WO[:, kc, n * 512:(n + 1) * 512]) for kc in range(8)] +
                                    [(hTT[:, kc, :], WO[:, 8 + kc, n * 512:(n + 1) * 512]) for kc in range(8)]),
                     reads=["yTl", "hTT", "WO"], writes=["ps%d" % n])
            post_norm_res(0, 1, xt2[:], "xt2", gpo[:], "gpo", wk2, x1t, "x1t", P=PP, psf=psf)
            PP.dma("sp", x1s[ti], x1t[:], reads=["x1t0", "x1t1"])
            if S:
                for b in range(16):
                    PP.dma("sp", D["m_s"][b:b + 1, :], sg[8 * b + 7:8 * b + 8, 28:32], reads=["mt"])
            else:
                if ti == LASTP:
                    PP.dma("sp", D["m_p"], sg[127:128, 28:32], reads=["mt"])
                PP.op("dve", lambda e: e.tensor_copy(out=mp[:, 0:4], in_=sg[:, 48:52]), reads=["muend"], writes=[mpk])

        WQ = nc.alloc_sbuf_tensor_at("WQe", [128, 8, 1024], BF16, offset=phase_base)
        WX = nc.alloc_sbuf_tensor_at("WXe", [128, 8, 1024], BF16, offset=phase_base + 16384)

        def prefetch_b():
            P.barrier(engs=("pool",))
            load_w(WQ, D["w_xq"], 0, 8, 0, 1024, "WQ")
            load_w(WX, D["w_xo"], 0, 8, 0, 1024, "WX")
        run_pipelined(P, a2_tile, list(range(NPT[0])) + [16], after_last_front=prefetch_b)
        chk("A2")

        new_phase()
        alloc([128, 8, 1024], BF16)
        alloc([128, 8, 1024], BF16)
        HI = (top[0] - 2 * 45056) // 64 * 64
        WG = nc.alloc_sbuf_tensor_at("WGe", [128, 8, 2816], BF16, offset=HI)
        WU = nc.alloc_sbuf_tensor_at("WUe", [128, 8, 2816], BF16, offset=HI + 45056)
        load_w(WG, D["w_gate"], 0, 8, 0, 2816, "WG")
        load_w(WU, D["w_up"], 0, 8, 0, 2816, "WU")
        top[0] = HI
        gpx = alloc([128, 1024], F32)
        gox = alloc([128, 1024], F32)
        bcast_load(gpx[:], D["g_pre_x"], "gpx", 1024)
        bcast_load(gox[:], D["g_post_x"], "gox", 1024)
        junk3 = alloc([128, 1024], BF16)
        wk3_ = [{"junk": junk3, "ss": alloc([128, 4], F32), "u": alloc([128, 1024], BF16)} for _ in range(2)]
        xt3_ = [alloc([128, 1024], F32) for _ in range(2)]
        uT3_ = [alloc([128, 8, 128], BF16) for _ in range(2)]
        qT3 = alloc([128, 8, 128], BF16)
        Pn = alloc([128, 4, 256], BF16)
        PT = alloc([128, 8, 128], BF16)
        otok = alloc([128, 1024], BF16)
        oT = alloc([128, 8, 128], BF16)
        x2t_ = [alloc([128, 1024], F32)] * 2
        sm3 = alloc([128, 16], F32)
        Kb = [alloc([128, 2, 1024], BF16) for _ in range(2)]
        Vs = [alloc([128, 2, 1024], BF16) for _ in range(2)]
        KTb = [alloc([128, 8, 256], BF16) for _ in range(2)]
        qTp3 = [alloc([128, 8, 128], BF16) for _ in range(2)]
        PTp = [alloc([128, 8, 128], BF16) for _ in range(2)]
        for k in range(2):
            P.op("dve", lambda e, k=k: e.memset(qTp3[k][:], 0.0), writes=["qTp3%d" % k])
            P.op("dve", lambda e, k=k: e.memset(PTp[k][:], 0.0), writes=["PTp%d" % k])

        def b_tile(ti, sink):
            S = (ti == 16)
            p = ti % 2
            PP = KM(P, p, {"xt3", "uT3"}, sink)
            wk3, xt3, uT3, x2t = wk3_[p], xt3_[p], uT3_[p], x2t_[p]
            nseq = 16 if S else 1
            PP.dma("sp", xt3[:], x1s[ti], writes=["xt3"])
            rms_uT(xt3[:], "xt3", gpx[:], "gpx", wk3, uT3, "uT3", 0, P=PP, sfx="@%d" % p)
            if S:
                for bb in range(2):
                    PP.dma("pool", Kb[bb][:], D["ck"][bb].rearrange("(mc m) f -> m mc f", mc=2), writes=["Kb%d" % bb])
                for bb in range(2):
                    PP.dma("pool", Vs[bb][:], D["cv"][bb].rearrange("(mc m) f -> m mc f", mc=2), writes=["Vs%d" % bb])
            yield
            for half in range(2):
                pi = 1 + half
                def fq(e, half=half, pi=pi):
                    ins = None
                    for c4 in range(4):
                        fc = half * 4 + c4
                        for kc in range(8):
                            ins = e.matmul(psf(pi)[:, c4 * 128:(c4 + 1) * 128], lhsT=WQ[:, kc, fc * 128:(fc + 1) * 128], rhs=uT3[:, kc, :],
                                           start=(kc == 0), stop=(kc == 7))
                    return ins
                PP.op("pe", fq, reads=["uT3", "WQ"], writes=["ps%d" % pi])
                PP.op("act", lambda e, half=half, pi=pi: e.mul(out=qT3[:, half * 4:half * 4 + 4, :].rearrange("p c t -> p (c t)"), in_=psf(pi), mul=1.0 / 16.0),
                     reads=["ps%d" % pi], writes=["qT3%d" % half])
            qkeys = ["qT30", "qT31"]
            for b in range(nseq):
                k2 = b % 2
                if S:
                    for half in range(2):
                        pi = 1 + half
                        PP.op("pe", transposes([(psb(pi)[:, c4 * 256 + mc * 128:c4 * 256 + (mc + 1) * 128],
                                                Kb[k2][:, mc, (half * 4 + c4) * 128:(half * 4 + c4 + 1) * 128], identb[:])
                                               for c4 in range(4) for mc in range(2)]), reads=["Kb%d" % k2, "identb"], writes=["ps%d" % pi])
                        PP.op("act", lambda e, half=half, pi=pi, k2=k2: e.copy(out=KTb[k2][:, half * 4:half * 4 + 4, :].rearrange("p c m -> p (c m)"), in_=psb(pi)),
                             reads=["ps%d" % pi], writes=["KTb%d" % k2])
                    if b + 2 < nseq:
                        PP.dma("pool", Kb[k2][:], D["ck"][b + 2].rearrange("(mc m) f -> m mc f", mc=2), writes=["Kb%d" % k2])
                    PP.op("dve", lambda e, k2=k2, b=b: e.tensor_copy(out=qTp3[k2][:, :, 8 * b:8 * b + 8], in_=qT3[:, :, 8 * b:8 * b + 8]),
                         reads=qkeys, writes=["qTp3%d" % k2])
                    qx, qk, kx, kk = qTp3[k2], ["qTp3%d" % k2], KTb[k2], "KTb%d" % k2
                else:
                    qx, qk, kx, kk = qT3, qkeys, KT, "KT"
                for h in range(4):
                    def fs(e, h=h, qx=qx, kx=kx, b=b):
                        ins = None
                        for dc in range(2):
                            ins = e.matmul(psf(3 + h)[:, 0:256], lhsT=qx[:, 2 * h + dc, :], rhs=kx[:, 2 * h + dc, :],
                                           start=(b == 0 and dc == 0), stop=(b == nseq - 1 and dc == 1))
                        return ins
                    PP.op("pe", fs, reads=qk + [kk], writes=["ps%d" % (3 + h)])
                if S:
                    PP.op("dve", lambda e, k2=k2, b=b: e.memset(qTp3[k2][:, :, 8 * b:8 * b + 8], 0.0), writes=["qTp3%d" % k2])
            for h in range(4):
                PP.op("dve", lambda e, h=h: e.reduce_max(out=sm3[:, h:h + 1], in_=psf(3 + h)[:, 0:256], axis=AX.X), reads=["ps%d" % (3 + h)], writes=["mx%d" % h])
                PP.op("dve", lambda e, h=h: e.tensor_scalar_mul(out=sm3[:, 4 + h:5 + h], in0=sm3[:, h:h + 1], scalar1=-1.0), reads=["mx%d" % h], writes=["nmx%d" % h])
                PP.op("act", lambda e, h=h: e.activation(out=Pn[:, h, :], in_=psf(3 + h)[:, 0:256], func=AF.Exp, bias=sm3[:, 4 + h:5 + h], scale=1.0,
                                                        accum_out=sm3[:, 8 + h:9 + h]), reads=["ps%d" % (3 + h), "nmx%d" % h], writes=["Pn%d" % h, "rs%d" % h])
                PP.op("dve", lambda e, h=h: e.reciprocal(out=sm3[:, 12 + h:13 + h], in_=sm3[:, 8 + h:9 + h]), reads=["rs%d" % h], writes=["ri%d" % h])
            pkeys = ["Pn%d" % h for h in range(4)]
            PP.op("pe", transposes([(psb(7)[:, (2 * h + mc) * 128:(2 * h + mc + 1) * 128], Pn[:, h, mc * 128:(mc + 1) * 128], identb[:])
                                   for h in range(4) for mc in range(2)]), reads=pkeys + ["identb"], writes=["ps7"])
            PP.op("act", lambda e: e.copy(out=PT[:].rearrange("p c t -> p (c t)"), in_=psb(7)), reads=["ps7"], writes=["PT"])
            for b in range(nseq):
                k2 = b % 2
                if S:
                    PP.op("dve", lambda e, k2=k2, b=b: e.tensor_copy(out=PTp[k2][:, :, 8 * b:8 * b + 8], in_=PT[:, :, 8 * b:8 * b + 8]),
                         reads=["PT"], writes=["PTp%d" % k2])
                    px, pk, vx, vk = PTp[k2], "PTp%d" % k2, Vs[k2], "Vs%d" % k2
                else:
                    px, pk, vx, vk = PT, "PT", Vb, "Vb"
                for h in range(4):
                    def fo(e, h=h, px=px, vx=vx, b=b):
                        ins = None
                        for mc in range(2):
                            ins = e.matmul(psf(3 + h)[:, 0:256], lhsT=px[:, 2 * h + mc, :], rhs=vx[:, mc, h * 256:(h + 1) * 256],
                                           start=(b == 0 and mc == 0), stop=(b == nseq - 1 and mc == 1))
                        return ins
                    PP.op("pe", fo, reads=[pk, vk], writes=["ps%d" % (3 + h)])
                if S:
                    if b + 2 < nseq:
                        PP.dma("pool", Vs[k2][:], D["cv"][b + 2].rearrange("(mc m) f -> m mc f", mc=2), writes=["Vs%d" % k2])
                    PP.op("dve", lambda e, k2=k2, b=b: e.memset(PTp[k2][:, :, 8 * b:8 * b + 8], 0.0), writes=["PTp%d" % k2])
            for h in range(4):
                PP.op("dve", lambda e, h=h: e.tensor_scalar_mul(out=otok[:, h * 256:(h + 1) * 256], in0=psf(3 + h)[:, 0:256], scalar1=sm3[:, 12 + h:13 + h]),
                     reads=["ps%d" % (3 + h), "ri%d" % h], writes=["otok%d" % h])
            okeys = ["otok%d" % h for h in range(4)]
            PP.op("pe", transposes([(psb(7)[:, kc * 128:(kc + 1) * 128], otok[:, kc * 128:(kc + 1) * 128], identb[:]) for kc in range(8)]),
                 reads=okeys + ["identb"], writes=["ps7"])
            PP.op("act", lambda e: e.copy(out=oT[:].rearrange("p k t -> p (k t)"), in_=psb(7)), reads=["ps7"], writes=["oT"])
            for n in range(2):
                PP.op("pe", mm_group(psf(1 + n), [(oT[:, kc, :], WX[:, kc, n * 512:(n + 1) * 512]) for kc in range(8)]),
                     reads=["oT", "WX"], writes=["ps%d" % (1 + n)])
            post_norm_res(1, 2, xt3[:], "xt3", gox[:], "gox", wk3, x2t, "x2t", P=PP, sfx="@%d" % p)
            PP.dma("sp", x2s[ti], x2t[:], reads=["x2t0", "x2t1"])

        run_pipelined(P, b_tile, list(range(NPT[0])) + [16])
        chk("B")

        new_phase()
        WD = alloc([128, 22, 1024], BF16)
        load_w(WD, D["w_down"], 0, 22, 0, 1024, "WD")
        gpf = alloc([128, 1024], F32)
        gof = alloc([128, 1024], F32)
        bcast_load(gpf[:], D["g_pre_ffn"], "gpf", 1024)
        bcast_load(gof[:], D["g_post_ffn"], "gof", 1024)
        wk4_ = [{"junk": alloc([128, 1024], BF16), "ss": alloc([128, 4], F32), "u": alloc([128, 1024], BF16)} for _ in range(2)]
        xt4_ = [alloc([128, 1024], F32) for _ in range(2)]
        uT4_ = [alloc([128, 8, 128], BF16) for _ in range(2)]
        hT4 = alloc([128, 22, 128], BF16)
        gs = [alloc([128, 512], F32) for _ in range(2)]
        yt_ = [alloc([128, 1024], F32) for _ in range(2)]

        def c_tile(ti, sink):
            p = ti % 2
            PP = KM(P, p, {"xt4", "uT4", "yt0", "yt1"}, sink)
            wk4, xt4, uT4, yt = wk4_[p], xt4_[p], uT4_[p], yt_[p]
            PP.dma("sp", xt4[:], x2s[ti], writes=["xt4"])
            rms_uT(xt4[:], "xt4", gpf[:], "gpf", wk4, uT4, "uT4", 0, P=PP, sfx="@%d" % p)
            yield
            for q in range(6):
                nch = 4 if q < 5 else 2
                pg, pu = 1 + (q % 2) * 2, 2 + (q % 2) * 2
                for (pi, Wt, wkey) in ((pg, WG, "WG"), (pu, WU, "WU")):
                    def fg(e, q=q, nch=nch, pi=pi, Wt=Wt):
                        ins = None
                        for c4 in range(nch):
                            hc = q * 4 + c4
                            for kc in range(8):
                                ins = e.matmul(psf(pi)[:, c4 * 128:(c4 + 1) * 128], lhsT=Wt[:, kc, hc * 128:(hc + 1) * 128], rhs=uT4[:, kc, :],
                                               start=(kc == 0), stop=(kc == 7))
                        return ins
                    PP.op("pe", fg, reads=["uT4", wkey], writes=["ps%d" % pi])
                PP.op("act", lambda e, q=q, nch=nch, pg=pg: e.activation(out=gs[q % 2][:, 0:nch * 128], in_=psf(pg)[:, 0:nch * 128], func=AF.Silu),
                     reads=["ps%d" % pg], writes=["gs%d" % (q % 2)])
                PP.op("dve", lambda e, q=q, nch=nch, pu=pu: e.tensor_tensor(out=hT4[:, q * 4:q * 4 + nch, :].rearrange("p c t -> p (c t)"),
                                                                          in0=psf(pu)[:, 0:nch * 128], in1=gs[q % 2][:, 0:nch * 128], op=ALU.mult),
                     reads=["ps%d" % pu, "gs%d" % (q % 2)], writes=["hT4_%d" % q])
            hkeys = ["hT4_%d" % q for q in range(6)]
            for n in range(2):
                PP.op("pe", mm_group(psf(5 + n), [(hT4[:, hc, :], WD[:, hc, n * 512:(n + 1) * 512]) for hc in range(22)]),
                     reads=hkeys + ["WD"], writes=["ps%d" % (5 + n)])
            post_norm_res(5, 6, xt4[:], "xt4", gof[:], "gof", wk4, yt, "yt", P=PP, sfx="@%d" % p)
            PP.dma("sp", y_dst(ti), yt[:], reads=["yt0", "yt1"])

        run_pipelined(P, c_tile, list(range(NPT[0])) + [16])
        chk("C")
        P.emit()


_NC_CACHE = {}


def _in_maps(inp):
    f = lambda a: np.ascontiguousarray(np.asarray(a, dtype=np.float32))
    w = {k: f(inp[k]).reshape(W_SHAPES[k]) for k in W_SHAPES}
    maps = []
    for c in range(NCORES):
        sl = slice(16 * c, 16 * c + 16)
        m = dict(w)
        m["xp"] = f(inp["x_prompt"][c])
        m["xs"] = f(inp["x_sample"][sl]).reshape(128, 1024)
        m["mem"] = f(inp["mem_prompt"][c])
        m["st_ssm"] = f(inp["state_ssm"][0, sl])
        m["st_conv"] = f(inp["state_conv"][0, sl]).reshape(48, 1536)
        m["st_c"] = f(inp["state_mlstm_c"][0, sl])
        m["st_n"] = f(inp["state_mlstm_n"][0, sl])
        m["st_m"] = f(inp["state_mlstm_m"][0, sl])
        m["ck"] = f(inp["cache_mem_k"][0, sl]).reshape(16, 256, 1024)
        m["cv"] = f(inp["cache_mem_v"][0, sl]).reshape(16, 256, 1024)
        m["consts"] = CONSTS
        maps.append(m)
    return maps


def _assemble(results):
    g = lambda k: [np.asarray(r[k], dtype=np.float32) for r in results]
    yp = np.stack(g("yp"))
    ys = np.concatenate(g("ys")).reshape(128, 8, 1024)
    mk = np.stack(g("mk")).reshape(1, 8, 256, 4, 256)
    mv = np.stack(g("mv")).reshape(1, 8, 256, 4, 256)
    ssm_p = np.stack(g("ssm_p")).reshape(1, 8, 16, 64, 128)
    conv_p = np.stack(g("conv_p")).reshape(1, 8, 3, 1536)
    c_p = np.stack(g("c_p")).reshape(1, 8, 4, 128, 256)
    n_p = np.stack(g("n_p")).reshape(1, 8, 4, 128)
    m_p = np.stack(g("m_p")).reshape(1, 8, 4)
    ssm_s = np.concatenate(g("ssm_s")).reshape(1, 128, 16, 64, 128)
    conv_s = np.concatenate(g("conv_s")).reshape(1, 128, 3, 1536)
    c_s = np.concatenate(g("c_s")).reshape(1, 128, 4, 128, 256)
    n_s = np.concatenate(g("n_s")).reshape(1, 128, 4, 128)
    m_s = np.concatenate(g("m_s")).reshape(1, 128, 4)
    return (yp, ys, mk, mv, ssm_p, conv_p, c_p, n_p, m_p, ssm_s, conv_s, c_s, n_s, m_s)


def kernel(**inputs):
    if "nc" not in _NC_CACHE:
        _NC_CACHE["nc"] = build_nc("C")
    res = run_bass_kernel_spmd(_NC_CACHE["nc"], _in_maps(inputs), core_ids=list(range(NCORES)))
    return _assemble(res.results)
```

```python
import numpy as np
import concourse.bass as bass
import concourse.mybir as mybir
from concourse.bass_utils import run_bass_kernel_spmd
from contextlib import ExitStack

F32 = mybir.dt.float32
BF16 = mybir.dt.bfloat16
AF = mybir.ActivationFunctionType
ALU = mybir.AluOpType
AX = mybir.AxisListType
EPS = 1e-6
NCORES = 8
ENGS = ("pe", "dve", "act", "pool", "sp")


class Op:
    __slots__ = ("eng", "fn", "reads", "writes", "waits", "marked", "cnt",
                 "is_dma", "dsem", "dcnt", "prev_on_sem", "deps")

    def __init__(self, eng, fn, reads, writes, is_dma=False):
        self.eng = eng
        self.fn = fn
        self.reads = reads
        self.writes = writes
        self.waits = []
        self.marked = False
        self.cnt = 0
        self.is_dma = is_dma
        self.dsem = None
        self.dcnt = 0
        self.prev_on_sem = None
        self.deps = []


class Prog:
    def __init__(self, nc, n_dma_sems=80, same_engine_sync=True):
        self.nc = nc
        self.ops = {e: [] for e in ENGS}
        self.all_ops = []
        self.last_w = {}
        self.readers = {}
        self.n_dma_sems = n_dma_sems
        self.dma_rr = 0
        self.dma_rrq = [0, 0]
        self.dma_issued = [0] * n_dma_sems
        self.dma_last = [None] * n_dma_sems
        self.same_engine_sync = same_engine_sync
        self.pending_bar = {e: [] for e in ENGS}
        self.alias = {}

    def barrier(self, engs=ENGS):
        snap = []
        for e in ENGS:
            for op in reversed(self.ops[e]):
                if not op.is_dma:
                    snap.append(op)
                    break
        for j in range(self.n_dma_sems):
            if self.dma_last[j] is not None:
                snap.append(self.dma_last[j])
        for e in engs:
            self.pending_bar[e] = list(snap)

    def _add(self, op):
        if self.alias:
            rd = []
            for k in op.reads:
                rd.extend(self.alias.get(k, (k,)))
            op.reads = tuple(rd)
        deps = []
        if self.pending_bar[op.eng]:
            deps.extend(self.pending_bar[op.eng])
            self.pending_bar[op.eng] = []
        for k in op.reads:
            w = self.last_w.get(k)
            if w is not None:
                deps.append(w)
        for k in op.writes:
            w = self.last_w.get(k)
            if w is not None:
                deps.append(w)
            for r in self.readers.get(k, ()):
                deps.append(r)
        op.deps = deps
        for k in op.reads:
            self.readers.setdefault(k, []).append(op)
        for k in op.writes:
            self.last_w[k] = op
            self.readers[k] = []
        self.ops[op.eng].append(op)
        self.all_ops.append(op)
        return op

    def op(self, eng, fn, reads=(), writes=()):
        reads, writes = tuple(reads), tuple(writes)
        if eng != "pe":
            extra = tuple(k for k in reads if isinstance(k, str) and k.startswith("ps") and k[2:].isdigit() and k not in writes)
            writes = writes + extra
        return self._add(Op(eng, fn, reads, writes))

    def dma(self, q, out, in_, reads=(), writes=(), **kw):
        def fn(e, out=out, in_=in_, kw=kw):
            return e.dma_start(out=out, in_=in_, **kw)
        op = Op(q, fn, tuple(reads), tuple(writes), is_dma=True)
        half = self.n_dma_sems // 2
        qi = 0 if q == "sp" else 1
        j = qi * half + self.dma_rrq[qi]
        self.dma_rrq[qi] = (self.dma_rrq[qi] + 1) % half
        op.dsem = j
        self.dma_issued[j] += 1
        op.dcnt = self.dma_issued[j]
        op.prev_on_sem = self.dma_last[j]
        self.dma_last[j] = op
        return self._add(op)

    def _skip(self, d, op):
        return (d.eng == op.eng and not op.is_dma and not d.is_dma
                and (d.eng == "pe" or not self.same_engine_sync))

    def finalize(self):
        for op in self.all_ops:
            for d in op.deps:
                if d is op or d.is_dma or self._skip(d, op):
                    continue
                d.marked = True
        cnt = {e: 0 for e in ENGS}
        for op in self.all_ops:
            if not op.is_dma and op.marked:
                cnt[op.eng] += 1
                op.cnt = cnt[op.eng]
        waited = {e: {} for e in ENGS}
        for op in self.all_ops:
            w = waited[op.eng]
            need = {}
            for d in op.deps:
                if d is op:
                    continue
                if d.is_dma:
                    key, val = ("d", d.dsem), 16 * d.dcnt
                else:
                    if self._skip(d, op):
                        continue
                    key, val = ("e", d.eng), d.cnt
                if need.get(key, 0) < val:
                    need[key] = val
            if op.is_dma and op.prev_on_sem is not None:
                key, val = ("d", op.dsem), 16 * (op.dcnt - 1)
                if need.get(key, 0) < val:
                    need[key] = val
            op.waits = []
            for key, val in need.items():
                if w.get(key, 0) >= val:
                    continue
                w[key] = val
                op.waits.append((key, val))

    def emit(self):
        nc = self.nc
        self.finalize()
        with ExitStack() as st:
            esem = {e: st.enter_context(nc.semaphore("s_" + e)) for e in ENGS}
            dsem = [st.enter_context(nc.semaphore("d%d" % j)) for j in range(self.n_dma_sems)]
            block = st.enter_context(nc.Block())

            def replay(ename, eng):
                for op in self.ops[ename]:
                    for (kind, ident), val in op.waits:
                        eng.wait_ge(esem[ident] if kind == "e" else dsem[ident], val)
                    ins = op.fn(eng)
                    if op.is_dma:
                        ins.then_inc(dsem[op.dsem], 16)
                    elif op.marked:
                        ins.then_inc(esem[ename], 1)
                if ename == "sp":
                    for j in range(self.n_dma_sems):
                        if self.dma_issued[j] > 0:
                            eng.wait_ge(dsem[j], 16 * self.dma_issued[j])

            @block.sync
            def _(e):
                replay("sp", e)

            @block.tensor
            def _(e):
                replay("pe", e)

            @block.vector
            def _(e):
                replay("dve", e)

            @block.scalar
            def _(e):
                replay("act", e)

            @block.gpsimd
            def _(e):
                replay("pool", e)


CO = {}


def _make_consts():
    cols = []
    off = [0]

    def add(name, arr):
        arr = np.asarray(arr, np.float32)
        CO[name] = (off[0], arr.shape[1])
        off[0] += arr.shape[1]
        cols.append(arr)

    r = np.arange(128)
    for v, seq in (("P", np.zeros(128, int)), ("S", r // 8)):
        same = (seq[:, None] == seq[None, :])
        le = (r[:, None] <= r[None, :])
        add("maskU" + v, same & le)
        add("maskLs" + v, same & (r[:, None] > r[None, :]))
        add("negT" + v, np.where(same & (r[None, :] <= r[:, None]), 0.0, -1e30))
        add("same" + v, same)
        last = np.array([np.max(np.where(seq == seq[s])[0]) for s in range(128)])
        add("lastbc" + v, (r[:, None] == last[None, :]))
    add("ident", np.eye(128))
    add("ones", np.ones((128, 128)))
    add("seqmaskS", (r[:, None] // 8 == np.arange(16)[None, :]))
    add("lastselS", (r[:, None] == (np.arange(16)[None, :] * 8 + 7)))
    add("seqmaskP", np.ones((128, 1)))
    add("lastselP", (r[:, None] == 127))
    return np.concatenate(cols, axis=1)


CONSTS = _make_consts()
NCONST = CONSTS.shape[1]

W_SHAPES = {
    "g_pre_mix": (1, 1024), "w_in": (1024, 5656), "conv_w": (4, 1536), "conv_b": (1, 1536), "dt_bias": (1, 16),
    "a_log": (1, 16), "d_skip": (1, 16), "g_ssm_out": (1, 1024), "b_igate": (1, 4), "b_fgate": (1, 4),
    "g_mlstm_out": (1, 1024), "w_out": (2048, 1024), "g_post_mix": (1, 1024), "g_mem": (1, 1024),
    "w_mem_k": (1024, 1024), "w_mem_v": (1024, 1024), "g_pre_x": (1, 1024), "w_xq": (1024, 1024),
    "w_xo": (1024, 1024), "g_post_x": (1, 1024), "g_pre_ffn": (1, 1024), "w_gate": (1024, 2816),
    "w_up": (1024, 2816), "w_down": (2816, 1024), "g_post_ffn": (1, 1024),
}
IN_SHAPES = {
    "xp": (2048, 1024), "xs": (128, 1024), "mem": (256, 1024), "st_ssm": (16, 16, 64, 128),
    "st_conv": (48, 1536), "st_c": (16, 4, 128, 256), "st_n": (16, 4, 128), "st_m": (16, 4),
    "ck": (16, 256, 1024), "cv": (16, 256, 1024), "consts": (128, NCONST),
}
OUT_SHAPES = {
    "yp": (2048, 1024), "ys": (128, 1024), "mk": (256, 1024), "mv": (256, 1024),
    "ssm_p": (16, 64, 128), "conv_p": (3, 1536), "c_p": (4, 128, 256), "n_p": (4, 128), "m_p": (1, 4),
    "ssm_s": (16, 16, 64, 128), "conv_s": (16, 3, 1536), "c_s": (16, 4, 128, 256), "n_s": (16, 4, 128),
    "m_s": (16, 4),
}
NT = 17
ALLOC_LOG = None
NPT = [16]


class KM:
    def __init__(self, P, p, keys, sink, cur=None, bankmap=None):
        self.P, self.p, self.keys, self.sink = P, p, keys, sink
        self.cur, self.bankmap = cur, bankmap

    def m(self, k):
        if self.bankmap is not None and k.startswith("ps") and k[2:].isdigit():
            return "ps%d" % self.bankmap[int(k[2:])]
        return "%s@%d" % (k, self.p) if k in self.keys else k

    def op(self, eng, fn, reads=(), writes=()):
        if self.bankmap is not None:
            def fn2(e, fn=fn, mp=self.bankmap, cur=self.cur):
                cur["m"] = mp
                return fn(e)
        else:
            fn2 = fn
        self.sink.append(("op", eng, fn2, [self.m(k) for k in reads], [self.m(k) for k in writes], None))

    def dma(self, q, out, in_, reads=(), writes=(), **kw):
        self.sink.append(("dma", q, (out, in_), [self.m(k) for k in reads], [self.m(k) for k in writes], kw))


def _flush(P, recs):
    for kind, eng, a, reads, writes, kw in recs:
        if kind == "op":
            P.op(eng, a, reads, writes)
        else:
            P.dma(eng, a[0], a[1], reads, writes, **kw)


def _interleave(a, b):
    out, ia, ib = [], 0, 0
    while ia < len(a) or ib < len(b):
        if ib >= len(b) or (ia < len(a) and ia * len(b) <= ib * len(a)):
            out.append(a[ia]); ia += 1
        else:
            out.append(b[ib]); ib += 1
    return out


def run_pipelined(P, tile_fn, tiles, interleave=True, after_last_front=None):
    sinks = [[] for _ in tiles]
    gens = [tile_fn(t, sinks[i]) for i, t in enumerate(tiles)]
    next(gens[0])
    _flush(P, sinks[0]); sinks[0].clear()
    for i in range(len(gens)):
        front = []
        if i + 1 < len(gens):
            next(gens[i + 1])
            front = list(sinks[i + 1]); sinks[i + 1].clear()
        for _ in gens[i]:
            pass
        body = list(sinks[i]); sinks[i].clear()
        _flush(P, _interleave(body, front) if interleave else front + body)
        if after_last_front is not None and i + 2 == len(gens):
            after_last_front()


class StopBuild(Exception):
    pass


def build_nc(stop_after="C"):
    nc = bass.Bass("TRN2", target_bir_lowering=False)
    try:
        _build(nc, stop_after)
    except StopBuild:
        pass
    return nc


def _build(nc, stop_after):
    D = {}
    for k, s in list(IN_SHAPES.items()) + list(W_SHAPES.items()):
        D[k] = nc.dram_tensor(k, list(s), F32, kind="ExternalInput").ap()
    for k, s in OUT_SHAPES.items():
        D[k] = nc.dram_tensor(k, list(s), F32, kind="ExternalOutput").ap()
    x1s = nc.dram_tensor("x1s", [NT, 128, 1024], F32, kind="Internal").ap()
    x2s = nc.dram_tensor("x2s", [NT, 128, 1024], F32, kind="Internal").ap()
    uTs = nc.dram_tensor("uTs", [NT, 128, 1024], BF16, kind="Internal").ap()
    yTs = nc.dram_tensor("yTs", [NT, 128, 1024], BF16, kind="Internal").ap()

    P = Prog(nc)
    st = ExitStack()

    dbg = {}

    def chk(name):
        if stop_after == name:
            if name == "SA1_0":
                P.op("act", lambda e: e.copy(out=dbg["hout0"][:, 0, 0:8], in_=dbg["hout0"][:, 0, 0:8]), reads=["hout0"], writes=["hout0"])
            elif name.startswith("SA1"):
                P.dma("sp", D["yp"][0:128, :], dbg["hout0"][:].rearrange("p j n -> p (j n)"), reads=["hout0"])
            P.emit()
            raise StopBuild()

    with st:
        base0 = (int(nc.sbuf_base) + 63) // 64 * 64
        top = [int(nc.sbuf_top)]
        cur = [base0]
        cnt = [0]

        curL = [0, 0]

        def alloc(shape, dt, low=False):
            nbytes = int(np.prod(shape[1:])) * (2 if dt == BF16 else 4)
            nbytes = (nbytes + 63) // 64 * 64
            if low:
                off = curL[0]
                curL[0] += nbytes
                assert curL[0] <= curL[1], ("SBUF low-region overflow", curL[0], curL[1])
            else:
                off = cur[0]
                cur[0] += nbytes
                assert cur[0] <= top[0], ("SBUF overflow", cur[0], top[0])
            cnt[0] += 1
            t = nc.alloc_sbuf_tensor_at("t%d" % cnt[0], list(shape), dt, offset=off)
            if ALLOC_LOG is not None:
                ALLOC_LOG.append((cnt[0], off, nbytes, tuple(shape)))
            return t

        ps = [st.enter_context(nc.psum_tensor("ps%d" % i, [128, 512], F32)) for i in range(8)]

        def psf(i):
            return ps[i][:]

        def psb(i):
            return ps[i][:].bitcast(BF16)

        cst = alloc([128, NCONST], F32)
        identb = alloc([128, 128], BF16)
        KT = alloc([128, 8, 256], BF16)
        Vb = alloc([128, 2, 1024], BF16)
        phase_base = cur[0]

        def C(name):
            o, n = CO[name]
            return cst[:, o:o + n]

        P.dma("sp", cst[:], D["consts"], writes=["cst"])
        P.dma("pool", identb[:], D["consts"][:, CO["ident"][0]:CO["ident"][0] + 128], writes=["identb"])
        identf = C("ident")
        onesf = C("ones")

        def new_phase():
            P.barrier()
            cur[0] = phase_base

        def load_w(dst, src, k0, nk, c0, c1, key):
            for kc in range(nk):
                cc = c0
                while cc < c1:
                    ce = min(cc + 1024, c1)
                    ck = "%s#%d" % (key, len(P.alias.setdefault(key, [])))
                    P.alias[key].append(ck)
                    P.dma("pool", dst[:, kc, cc - c0:ce - c0], src[(k0 + kc) * 128:(k0 + kc + 1) * 128, cc:ce],
                          writes=[ck])
                    cc = ce

        def bcast_load(dst, src_row, key, n):
            P.dma("sp", dst, src_row.partition_broadcast(128), writes=[key])

        def mm_group(out, pairs):
            def f(e, out=out, pairs=pairs):
                n = len(pairs)
                ins = None
                for i, (l, r) in enumerate(pairs):
                    ins = e.matmul(out, lhsT=l, rhs=r, start=(i == 0), stop=(i == n - 1))
                return ins
            return f

        def transposes(items):
            def f(e, items=items):
                ins = None
                for (o, i, idn) in items:
                    ins = e.transpose(o, i, idn)
                return ins
            return f

        def x_src(ti):
            return D["xp"][ti * 128:(ti + 1) * 128, :] if ti < 16 else D["xs"]

        def y_dst(ti):
            return D["yp"][ti * 128:(ti + 1) * 128, :] if ti < 16 else D["ys"]

        def rms_uT(xt, xkey, gbc, gkey, wk, uT, uTkey, psi, P=P, sfx="", psb=psb):
            P.op("act", lambda e: e.activation(out=wk["junk"][:], in_=xt, func=AF.Square, accum_out=wk["ss"][:, 0:1]),
                 reads=[xkey], writes=["junk" + sfx, "ss" + sfx])
            P.op("act", lambda e: e.activation(out=wk["ss"][:, 1:2], in_=wk["ss"][:, 0:1], func=AF.Sqrt, bias=EPS,
                                               scale=1.0 / 1024), reads=["ss" + sfx], writes=["ss1" + sfx])
            P.op("dve", lambda e: e.reciprocal(out=wk["ss"][:, 2:3], in_=wk["ss"][:, 1:2]), reads=["ss1" + sfx], writes=["ss2" + sfx])
            P.op("dve", lambda e: e.scalar_tensor_tensor(out=wk["u"][:], in0=xt, scalar=wk["ss"][:, 2:3], in1=gbc,
                                                         op0=ALU.mult, op1=ALU.mult),
                 reads=[xkey, "ss2" + sfx, gkey], writes=["u" + sfx])
            pk = "ps%d" % psi
            P.op("pe", transposes([(psb(psi)[:, kc * 128:(kc + 1) * 128], wk["u"][:, kc * 128:(kc + 1) * 128], identb[:])
                                   for kc in range(8)]), reads=["u" + sfx, "identb"], writes=[pk])
            P.op("act", lambda e: e.copy(out=uT[:].rearrange("p k t -> p (k t)"), in_=psb(psi)), reads=[pk], writes=[uTkey])

        def post_norm_res(psa, psbk, xr, xrkey, gbc, gkey, wk, xo, xokey, P=P, sfx="", psf=psf):
            ka, kb = "ps%d" % psa, "ps%d" % psbk
            P.op("act", lambda e: e.activation(out=wk["junk"][:, 0:512], in_=psf(psa), func=AF.Square,
                                               accum_out=wk["ss"][:, 0:1]), reads=[ka], writes=["junk" + sfx, "ss" + sfx])
            P.op("act", lambda e: e.activation(out=wk["junk"][:, 512:1024], in_=psf(psbk), func=AF.Square,
                                               accum_out=wk["ss"][:, 1:2]), reads=[kb], writes=["junkb" + sfx, "ss1" + sfx])
            P.op("dve", lambda e: e.tensor_tensor(out=wk["ss"][:, 2:3], in0=wk["ss"][:, 0:1], in1=wk["ss"][:, 1:2],
                                                  op=ALU.add), reads=["ss" + sfx, "ss1" + sfx], writes=["ss2" + sfx])
            P.op("act", lambda e: e.activation(out=wk["ss"][:, 3:4], in_=wk["ss"][:, 2:3], func=AF.Sqrt, bias=EPS,
                                               scale=1.0 / 1024), reads=["ss2" + sfx], writes=["ss3" + sfx])
            P.op("dve", lambda e: e.reciprocal(out=wk["ss"][:, 3:4], in_=wk["ss"][:, 3:4]), reads=["ss3" + sfx], writes=["ss3" + sfx])
            for half, pi, pk in ((0, psa, ka), (1, psbk, kb)):
                sl = slice(half * 512, (half + 1) * 512)
                P.op("dve", lambda e, pi=pi, sl=sl: e.scalar_tensor_tensor(
                    out=xo[:, sl], in0=psf(pi), scalar=wk["ss"][:, 3:4], in1=gbc[:, sl], op0=ALU.mult, op1=ALU.mult),
                    reads=[pk, "ss3" + sfx, gkey], writes=[xokey + str(half)])
                P.op("dve", lambda e, sl=sl: e.tensor_tensor(out=xo[:, sl], in0=xo[:, sl], in1=xr[:, sl], op=ALU.add),
                     reads=[xokey + str(half), xrkey], writes=[xokey + str(half)])

        new_phase()
        LOW = 86016
        cur[0] = phase_base + LOW + 8192
        Wk = alloc([128, 8, 1024], BF16)
        Wv = alloc([128, 8, 1024], BF16)
        load_w(Wk, D["w_mem_k"], 0, 8, 0, 1024, "Wk")
        load_w(Wv, D["w_mem_v"], 0, 8, 0, 1024, "Wv")
        W1 = nc.alloc_sbuf_tensor_at("W1e", [128, 8, 2576], BF16, offset=phase_base)
        load_w(W1, D["w_in"], 0, 8, 0, 2576, "W1")
        gmem = alloc([128, 1024], F32)
        bcast_load(gmem[:], D["g_mem"], "gmem", 1024)
        wk0 = {"junk": alloc([128, 1024], BF16), "ss": alloc([128, 4], F32), "u": alloc([128, 1024], BF16)}
        memt = [alloc([128, 1024], F32) for _ in range(2)]
        uTm = [alloc([128, 8, 128], BF16) for _ in range(2)]
        osb = [alloc([128, 512], F32) for _ in range(2)]
        for mt in range(2):
            P.dma("sp", memt[mt][:], D["mem"][mt * 128:(mt + 1) * 128, :], writes=["memt%d" % mt])
            rms_uT(memt[mt][:], "memt%d" % mt, gmem[:], "gmem", wk0, uTm[mt], "uTm%d" % mt, 0)
        chk("0a")
        oi = 0
        for mt in range(2):
            for (Wt, wkey, dst) in ((Wk, "Wk", "mk"), (Wv, "Wv", "mv")):
                for n in range(2):
                    pi = 1 + (oi % 2)
                    P.op("pe", mm_group(psf(pi), [(uTm[mt][:, kc, :], Wt[:, kc, n * 512:(n + 1) * 512]) for kc in range(8)]),
                         reads=["uTm%d" % mt, wkey], writes=["ps%d" % pi])
                    ob = osb[oi % 2]
                    P.op("act", lambda e, ob=ob, pi=pi: e.copy(out=ob[:], in_=psf(pi)), reads=["ps%d" % pi],
                         writes=["osb%d" % (oi % 2)])
                    if dst == "mv":
                        P.op("act", lambda e, pi=pi, mt=mt, n=n: e.copy(out=Vb[:, mt, n * 512:(n + 1) * 512], in_=psf(pi)),
                             reads=["ps%d" % pi], writes=["Vb"])
                    P.dma("sp", D[dst][mt * 128:(mt + 1) * 128, n * 512:(n + 1) * 512], ob[:], reads=["osb%d" % (oi % 2)])
                    oi += 1
        chk("0b")
        for fc in range(8):
            pi = 3 + (fc % 2)
            pairs = []
            P.op("pe", lambda e, fc=fc, pi=pi: [
                [e.matmul(psf(pi)[:, mt * 128:(mt + 1) * 128], lhsT=Wk[:, kc, fc * 128:(fc + 1) * 128], rhs=uTm[mt][:, kc, :],
                          start=(kc == 0), stop=(kc == 7)) for kc in range(8)] for mt in range(2)][-1][-1],
                 reads=["uTm0", "uTm1", "Wk"], writes=["ps%d" % pi])
            P.op("act", lambda e, fc=fc, pi=pi: e.copy(out=KT[:, fc, :], in_=psf(pi)[:, 0:256]), reads=["ps%d" % pi], writes=["KT"])

        chk("0")

        new_phase()
        curL[0], curL[1] = phase_base, phase_base + LOW
        cur[0] = phase_base + LOW
        alloc([128, 8, 2576], BF16, low=True)
        gpm = alloc([128, 1024], F32, low=True)
        gso = alloc([128, 1024], F32)
        bcast_load(gpm[:], D["g_pre_mix"], "gpm", 1024)
        bcast_load(gso[:], D["g_ssm_out"], "gso", 1024)
        sm = alloc([128, 64], F32)
        bcast_load(sm[:, 0:16], D["dt_bias"], "sm_dtb", 16)
        bcast_load(sm[:, 16:32], D["a_log"], "sm_al", 16)
        bcast_load(sm[:, 32:48], D["d_skip"], "sm_dsk", 16)
        P.op("act", lambda e: e.activation(out=sm[:, 16:32], in_=sm[:, 16:32], func=AF.Exp), reads=["sm_al"], writes=["sm_al"])
        P.op("dve", lambda e: e.tensor_scalar_mul(out=sm[:, 16:32], in0=sm[:, 16:32], scalar1=-1.0), reads=["sm_al"], writes=["sm_a"])
        cw = alloc([128, 12, 4], F32, low=True)
        cbias = alloc([128, 12], F32, low=True)
        for ci in range(12):
            P.dma("sp", cw[:, ci, :], D["conv_w"][:, ci * 128:(ci + 1) * 128].rearrange("j p -> p j"), writes=["cw"],
                  allow_slow_non_contiguous=True)
        P.dma("sp", cbias[:], D["conv_b"].rearrange("o (c p) -> p (o c)", p=128), writes=["cbias"], allow_slow_non_contiguous=True)
        wk1 = {"junk": alloc([128, 1024], BF16, low=True), "ss": alloc([128, 4], F32, low=True), "u": alloc([128, 1024], BF16, low=True)}
        xt = alloc([128, 1024], F32, low=True)
        uT = alloc([128, 8, 128], BF16, low=True)
        zs_ = [alloc([128, 1024], F32) for _ in range(2)]
        xbcT = alloc([128, 12 * 176], BF16, low=True)
        vP = xbcT[:, 0:12 * 131].rearrange("p (c t) -> p c t", c=12)
        vS = xbcT[:].rearrange("p (c b j) -> p c b j", c=12, b=16)
        diagW = alloc([128, 48, 128], BF16, low=True)
        for ci in range(12):
            for j in range(4):
                P.op("dve", lambda e, ci=ci, j=j: e.tensor_scalar_mul(out=diagW[:, ci * 4 + j, :], in0=identf, scalar1=cw[:, ci, j:j + 1]),
                     reads=["cst", "cw"], writes=["diagW"])
        xbcS_ = [alloc([128, 12, 128], BF16) for _ in range(2)]
        xtok = alloc([128, 1536], F32, low=True)
        scv = alloc([48, 1536], F32, low=True)
        xs_tok_ = [alloc([128, 1024], BF16) for _ in range(2)]
        xsd_ = [alloc([128, 1024], BF16) for _ in range(2)]
        xdt_ = [alloc([128, 1024], BF16) for _ in range(2)]
        xw_ = [alloc([128, 1024], BF16) for _ in range(2)]
        Btok_ = [alloc([128, 256], BF16) for _ in range(2)]
        Bp = [alloc([128, 256], BF16) for _ in range(2)]
        CTp = [alloc([128, 2, 128], BF16) for _ in range(2)]
        R = alloc([128, 16, 128], F32)
        Lq = [alloc([128, 512], F32) for _ in range(2)]
        WT = alloc([128, 16, 128], BF16)
        cbm = alloc([128, 2, 128], F32)
        yi = alloc([128, 1024], F32)
        yg = alloc([128, 1024], F32)
        yn = alloc([128, 1024], BF16)
        yT = alloc([128, 8, 128], BF16)
        hT = alloc([128, 1024], F32)
        hTb = alloc([128, 1024], BF16)
        hT2 = alloc([128, 1024], F32)
        hTb2 = alloc([128, 1024], BF16)
        hnat = [alloc([128, 8, 128], F32) for _ in range(2)]
        hout = [alloc([128, 8, 128], F32) for _ in range(2)]
        dbg["hout0"] = hout[0]
        dbg["hT"] = hT
        junk2 = alloc([128, 1024], BF16)
        sv_ = [alloc([128, 256], F32) for _ in range(2)]
        sv = None
        rhsd = alloc([128, 16, 16], F32)
        decay = alloc([128, 256], F32)
        P.op("dve", lambda e: e.memset(vP[:, :, 0:3], 0.0), writes=["xbcT"])
        P.op("dve", lambda e: e.memset(hT[:], 0.0), writes=["hT"])
        P.op("dve", lambda e: e.memset(hTb[:], 0.0), writes=["hTb"])
        for k in range(2):
            P.op("dve", lambda e, k=k: e.memset(CTp[k][:], 0.0), writes=["CTp%d" % k])

        LASTP = NPT[0] - 1
        def a1_tile(ti, sink):
            S = (ti == 16)
            p = ti % 2
            FM = {0: 0, 1: 1, 2: 2, 3: 3, 4: 0, 5: 1, 6: 2, 7: 3}
            BM = {1: 2, 2: 3, 3: 7, 4: 4, 5: 5, 6: 6, 7: 7}
            cur = {"m": FM}
            PP = KM(P, p, {"xs_tok", "xsd", "xdt", "xw", "Btok", "xbcS", "zs0", "zs1", "dtr", "dte", "dt", "la", "cum", "ecum", "toend0", "toend", "rg0", "rg1", "rgs0", "rgs1"}, sink, cur=cur, bankmap=FM)

            def psf(i):
                return ps[cur["m"][i]][:]

            def psb(i):
                return ps[cur["m"][i]][:].bitcast(BF16)
            zs, xbcS, sv = zs_[p], xbcS_[p], sv_[p]
            xs_tok, xsd, xdt, xw, Btok = xs_tok_[p], xsd_[p], xdt_[p], xw_[p], Btok_[p]
            V = "S" if S else "P"
            nseq = 16 if S else 1
            maskU, maskLs, samem = C("maskU" + V), C("maskLs" + V), C("same" + V)
            seqmask = C("seqmask" + V)
            if S:
                chk("SA1_0")
            PP.dma("sp", xt[:], x_src(ti), writes=["xt"])
            if S:
                chk("SA1_1")
            rms_uT(xt[:], "xt", gpm[:], "gpm", wk1, uT, "uT", 0, P=PP, psb=psb)
            if S:
                chk("SA1_2")
            PP.dma("sp", uTs[ti], uT[:].rearrange("p k t -> p (k t)"), reads=["uT"])
            chk("A1a")
            if S:
                chk("SA1a")
            for n in range(2):
                pi = 1 + n
                PP.op("pe", mm_group(psf(pi), [(uT[:, kc, :], W1[:, kc, n * 512:(n + 1) * 512]) for kc in range(8)]),
                     reads=["uT", "W1"], writes=["ps%d" % pi])
                PP.op("act", lambda e, n=n, pi=pi: e.activation(out=zs[:, n * 512:(n + 1) * 512], in_=psf(pi), func=AF.Silu),
                     reads=["ps%d" % pi], writes=["zs%d" % n])
            PP.op("pe", mm_group(psf(3)[:, 0:16], [(uT[:, kc, :], W1[:, kc, 2560:2576]) for kc in range(8)]),
                 reads=["uT", "W1"], writes=["ps3"])
            PP.op("dve", lambda e: e.tensor_tensor(out=sv[:, 0:16], in0=psf(3)[:, 0:16], in1=sm[:, 0:16], op=ALU.add),
                 reads=["ps3", "sm_dtb"], writes=["dtr"])
            PP.op("act", lambda e: e.activation(out=sv[:, 16:32], in_=sv[:, 0:16], func=AF.Exp), reads=["dtr"], writes=["dte"])
            PP.op("act", lambda e: e.activation(out=sv[:, 32:48], in_=sv[:, 16:32], func=AF.Ln, bias=1.0), reads=["dte"], writes=["dt"])
            PP.op("dve", lambda e: e.tensor_tensor(out=sv[:, 48:64], in0=sv[:, 32:48], in1=sm[:, 16:32], op=ALU.mult),
                 reads=["dt", "sm_a"], writes=["la"])
            dt_ap = sv[:, 32:48]
            la_ap = sv[:, 48:64]
            chk("A1b")
            if S:
                chk("SA1b")
            if S:
                PP.dma("sp", scv[:], D["st_conv"], writes=["scv"])
                for q in range(3):
                    pi = 4
                    PP.op("pe", transposes([(psf(pi)[:, c4 * 48:(c4 + 1) * 48], scv[:, (q * 4 + c4) * 128:(q * 4 + c4 + 1) * 128],
                                            identf[0:48, 0:48]) for c4 in range(4)]), reads=["scv", "cst"], writes=["ps4"])
                    for c4 in range(4):
                        PP.op("dve", lambda e, q=q, c4=c4: e.tensor_copy(
                            out=vS[:, q * 4 + c4, :, 0:3], in_=psf(4)[:, c4 * 48:(c4 + 1) * 48].rearrange("p (b j) -> p b j", b=16)),
                            reads=["ps4"], writes=["xbcT"])
            for q in range(3):
                pi = 4 + (q % 2)
                def fx(e, q=q, pi=pi):
                    ins = None
                    for c4 in range(4):
                        ci = q * 4 + c4
                        for kc in range(8):
                            ins = e.matmul(psf(pi)[:, c4 * 128:(c4 + 1) * 128], lhsT=W1[:, kc, 1024 + ci * 128:1024 + (ci + 1) * 128],
                                           rhs=uT[:, kc, :], start=(kc == 0), stop=(kc == 7))
                    return ins
                PP.op("pe", fx, reads=["uT", "W1"], writes=["ps%d" % pi])
                if S:
                    for c4 in range(4):
                        ci = q * 4 + c4
                        PP.op("act", lambda e, ci=ci, c4=c4, pi=pi: e.copy(
                            out=vS[:, ci, :, 3:11], in_=psf(pi)[:, c4 * 128:(c4 + 1) * 128].rearrange("p (b j) -> p b j", b=16)),
                            reads=["ps%d" % pi], writes=["xbcT"])
                else:
                    PP.op("act", lambda e, q=q, pi=pi: e.copy(out=vP[:, 4 * q:4 * q + 4, 3:131], in_=psf(pi).rearrange("p (c t) -> p c t", c=4)),
                          reads=["ps%d" % pi], writes=["xbcT"])
            chk("A1c")
            if S:
                chk("SA1c")
            if S or ti == LASTP:
                for n in range(3):
                    pi = 6 + (n % 2)
                    PP.op("pe", mm_group(psf(pi), [(uT[:, kc, :], W1[:, kc, 1024 + n * 512:1024 + (n + 1) * 512]) for kc in range(8)]),
                         reads=["uT", "W1"], writes=["ps%d" % pi])
                    PP.op("act", lambda e, n=n, pi=pi: e.copy(out=xtok[:, n * 512:(n + 1) * 512], in_=psf(pi)),
                         reads=["ps%d" % pi], writes=["xtok"])
                if S:
                    for b in range(16):
                        PP.dma("sp", D["conv_s"][b], xtok[8 * b + 5:8 * b + 8, :], reads=["xtok"])
                else:
                    PP.dma("sp", D["conv_p"], xtok[125:128, :], reads=["xtok"])
            for q in range(3):
                pi = 5 - (q % 2)

                def fcv(e, q=q, pi=pi):
                    ins = None
                    for c4 in range(4):
                        ci = q * 4 + c4
                        for j in range(4):
                            tap = vS[:, ci, :, j:j + 8] if S else vP[:, ci, j:j + 128]
                            ins = e.matmul(psf(pi)[:, c4 * 128:(c4 + 1) * 128], lhsT=diagW[:, ci * 4 + j, :], rhs=tap,
                                           start=(j == 0), stop=(j == 3))
                    return ins
                PP.op("pe", fcv, reads=["xbcT", "diagW"], writes=["ps%d" % pi])
                for c4 in range(4):
                    ci = q * 4 + c4
                    PP.op("act", lambda e, ci=ci, c4=c4, pi=pi: e.activation(out=xbcS[:, ci, :], in_=psf(pi)[:, c4 * 128:(c4 + 1) * 128],
                                                                            func=AF.Silu, bias=cbias[:, ci:ci + 1], scale=1.0),
                          reads=["ps%d" % pi, "cbias"], writes=["xbcS"])
            if not S:
                PP.op("dve", lambda e: e.tensor_copy(out=vP[:, :, 0:3], in_=vP[:, :, 128:131]), reads=["xbcT"], writes=["xbcT"])
            chk("A1d")
            if S:
                chk("SA1d")
            PP.op("pe", transposes([(psb(1)[:, ci * 128:(ci + 1) * 128], xbcS[:, ci, :], identb[:]) for ci in range(8)]),
                 reads=["xbcS", "identb"], writes=["ps1"])
            PP.op("pe", transposes([(psb(2)[:, g * 128:(g + 1) * 128], xbcS[:, 8 + g, :], identb[:]) for g in range(2)]),
                 reads=["xbcS", "identb"], writes=["ps2"])
            PP.op("act", lambda e: e.copy(out=xs_tok[:], in_=psb(1)), reads=["ps1"], writes=["xs_tok"])
            PP.op("dve", lambda e: e.tensor_tensor(out=xdt[:].rearrange("p (h q) -> p h q", h=16),
                                                  in0=psb(1).rearrange("p (h q) -> p h q", h=16),
                                                  in1=dt_ap.unsqueeze(2).to_broadcast([128, 16, 64]), op=ALU.mult),
                 reads=["ps1", "dt"], writes=["xdt"])
            PP.op("dve", lambda e: e.tensor_tensor(out=xsd[:].rearrange("p (h q) -> p h q", h=16),
                                                  in0=xs_tok[:].rearrange("p (h q) -> p h q", h=16),
                                                  in1=sm[:, 32:48].unsqueeze(2).to_broadcast([128, 16, 64]), op=ALU.mult),
                 reads=["xs_tok", "sm_dsk"], writes=["xsd"])
            PP.op("act", lambda e: e.copy(out=Btok[:], in_=psb(2)[:, 0:256]), reads=["ps2"], writes=["Btok"])
            chk("A1e")
            if S:
                chk("SA1e")
            PP.op("pe", lambda e: [e.matmul(psf(3)[:, 0:16], lhsT=maskU, rhs=la_ap, start=True, stop=True),
                                  e.matmul(psf(3)[:, 16:32], lhsT=samem, rhs=la_ap, start=True, stop=True)][-1],
                 reads=["la", "cst"], writes=["ps3"])
            PP.op("dve", lambda e: e.tensor_copy(out=sv[:, 64:96], in_=psf(3)[:, 0:32]), reads=["ps3"], writes=["cum"])
            PP.op("act", lambda e: e.activation(out=sv[:, 96:112], in_=sv[:, 64:80], func=AF.Exp), reads=["cum"], writes=["ecum"])
            PP.op("dve", lambda e: e.tensor_tensor(out=sv[:, 112:128], in0=sv[:, 80:96], in1=sv[:, 64:80], op=ALU.subtract),
                 reads=["cum"], writes=["toend0"])
            PP.op("act", lambda e: e.activation(out=sv[:, 112:128], in_=sv[:, 112:128], func=AF.Exp), reads=["toend0"], writes=["toend"])
            PP.op("dve", lambda e: e.tensor_tensor(out=xw[:].rearrange("p (h q) -> p h q", h=16),
                                                  in0=xdt[:].rearrange("p (h q) -> p h q", h=16),
                                                  in1=sv[:, 112:128].unsqueeze(2).to_broadcast([128, 16, 64]), op=ALU.mult),
                 reads=["xdt", "toend"], writes=["xw"])
            chk("A1f")
            if S:
                chk("SA1f")
            if S:
                for bb in range(2):
                    PP.dma("sp", hnat[bb][:], D["st_ssm"][bb].rearrange("(hp h2) q n -> (h2 q) hp n", h2=2), writes=["hnat%d" % bb])
            yield
            PP.bankmap = BM
            cur["m"] = BM
            PP.op("pe", lambda e: [e.matmul(psf(6)[:, g * 128:(g + 1) * 128], lhsT=xbcS[:, 8 + g, :], rhs=xbcS[:, 10 + g, :],
                                           start=True, stop=True) for g in range(2)][-1],
                 reads=["xbcS"], writes=["ps6"])
            PP.op("dve", lambda e: e.tensor_tensor(out=cbm[:], in0=psf(6)[:, 0:256].rearrange("p (g t) -> p g t", g=2),
                                                  in1=maskU.unsqueeze(1).to_broadcast([128, 2, 128]), op=ALU.mult),
                 reads=["ps6", "cst"], writes=["cbm"])
            PP.op("dve", lambda e: e.tensor_tensor(out=R[:], in0=la_ap.unsqueeze(2).to_broadcast([128, 16, 128]),
                                                  in1=maskU.unsqueeze(1).to_broadcast([128, 16, 128]), op=ALU.mult),
                 reads=["la", "cst"], writes=["R"])
            for q in range(4):
                pi = 4 + (q % 2)
                PP.op("pe", lambda e, q=q, pi=pi: e.matmul(psf(pi), lhsT=maskLs, rhs=R[:, 4 * q:4 * q + 4, :].rearrange("p h t -> p (h t)"),
                                                          start=True, stop=True), reads=["R", "cst"], writes=["ps%d" % pi])
                PP.op("act", lambda e, q=q, pi=pi: e.activation(out=Lq[q % 2][:], in_=psf(pi), func=AF.Exp),
                     reads=["ps%d" % pi], writes=["Lq%d" % (q % 2)])
                PP.op("dve", lambda e, q=q: e.tensor_tensor(out=WT[:, 4 * q:4 * q + 4, :], in0=Lq[q % 2][:].rearrange("p (h t) -> p h t", h=4),
                                                           in1=cbm[:, q // 2, :].unsqueeze(1).to_broadcast([128, 4, 128]), op=ALU.mult),
                     reads=["Lq%d" % (q % 2), "cbm"], writes=["WT%d" % q])
            chk("A1g")
            if S:
                chk("SA1g")
            for half in range(2):
                pi = 4 + half
                def fy(e, half=half, pi=pi):
                    ins = None
                    for hh in range(8):
                        h = half * 8 + hh
                        ins = e.matmul(psf(pi)[:, hh * 64:(hh + 1) * 64], lhsT=WT[:, h, :], rhs=xdt[:, h * 64:(h + 1) * 64],
                                       start=True, stop=True)
                    return ins
                PP.op("pe", fy, reads=["WT0", "WT1", "WT2", "WT3", "xdt"], writes=["ps%d" % pi])
                PP.op("act", lambda e, half=half, pi=pi: e.copy(out=yi[:, half * 512:(half + 1) * 512], in_=psf(pi)),
                     reads=["ps%d" % pi], writes=["yi%d" % half])
            chk("A1h")
            if S:
                chk("SA1h")
            PP.op("dve", lambda e: e.tensor_tensor(out=rhsd[:, 0:nseq, :], in0=seqmask.unsqueeze(2).to_broadcast([128, nseq, 16]),
                                                  in1=la_ap.unsqueeze(1).to_broadcast([128, nseq, 16]), op=ALU.mult),
                 reads=["la", "cst"], writes=["rhsd"])
            PP.op("pe", lambda e: e.matmul(psf(3)[:, 0:nseq * 16], lhsT=onesf, rhs=rhsd[:, 0:nseq, :].rearrange("p b h -> p (b h)"),
                                          start=True, stop=True), reads=["rhsd", "cst"], writes=["ps3"])
            PP.op("act", lambda e: e.activation(out=decay[:, 0:nseq * 16], in_=psf(3)[:, 0:nseq * 16], func=AF.Exp),
                 reads=["ps3"], writes=["decay"])
            chk("A1i")
            if S:
                chk("SA1i")
            def stage_load(bq):
                kq = bq % 2
                hTq, hTbq, hkq, hbkq = (hT, hTb, "hT", "hTb") if kq == 0 else (hT2, hTb2, "hT2", "hTb2")
                for bb in ([] if bq == 0 else [bq + 1]):
                    if bb < nseq:
                        PP.dma("sp", hnat[bb % 2][:], D["st_ssm"][bb].rearrange("(hp h2) q n -> (h2 q) hp n", h2=2), writes=["hnat%d" % (bb % 2)])
                for half in range(2):
                    pi = 1 + half
                    PP.op("pe", transposes([(psf(pi)[:, j * 128:(j + 1) * 128], hnat[kq][:, half * 4 + j, :], identf) for j in range(4)]),
                          reads=["hnat%d" % kq, "cst"], writes=["ps%d" % pi])
                    PP.op("act", lambda e, half=half, pi=pi, hTq=hTq: e.copy(out=hTq[:, half * 512:(half + 1) * 512], in_=psf(pi)),
                          reads=["ps%d" % pi], writes=[hkq])
                    PP.op("act", lambda e, half=half, pi=pi, hTbq=hTbq: e.copy(out=hTbq[:, half * 512:(half + 1) * 512], in_=psf(pi)),
                          reads=["ps%d" % pi], writes=[hbkq])

            for b in range(nseq):
                k2 = b % 2
                hTx, hTbx, hk, hbk = (hT, hTb, "hT", "hTb") if k2 == 0 else (hT2, hTb2, "hT2", "hTb2")
                if S and stop_after == "X2":
                    pass
                elif S:
                    if b == 0:
                        stage_load(0)
                    PP.op("dve", lambda e, k2=k2, b=b, hTx=hTx, hTbx=hTbx: e.tensor_copy(out=CTp[k2][:, :, 8 * b:8 * b + 8], in_=xbcS[:, 10:12, 8 * b:8 * b + 8]),
                         reads=["xbcS"], writes=["CTp%d" % k2])
                    PP.op("dve", lambda e, k2=k2, b=b, hTx=hTx, hTbx=hTbx: e.tensor_scalar_mul(out=Bp[k2][:], in0=Btok[:], scalar1=seqmask[:, b:b + 1]),
                         reads=["Btok", "cst"], writes=["Bp%d" % k2])
                    ct = [CTp[k2][:, g, :] for g in range(2)]
                    bt = [Bp[k2][:, g * 128:(g + 1) * 128] for g in range(2)]
                    ctk, btk = "CTp%d" % k2, "Bp%d" % k2
                else:
                    ct = [xbcS[:, 10 + g, :] for g in range(2)]
                    bt = [Btok[:, g * 128:(g + 1) * 128] for g in range(2)]
                    ctk, btk = "xbcS", "Btok"
                for g in range(2):
                    PP.op("pe", lambda e, g=g, ct=ct, b=b, hTx=hTx, hTbx=hTbx: e.matmul(psf(6 + g), lhsT=ct[g], rhs=hTbx[:, g * 512:(g + 1) * 512],
                                                                  start=(b == 0), stop=(b == nseq - 1)),
                         reads=[ctk, hbk], writes=["ps%d" % (6 + g)])
                if S:
                    PP.op("dve", lambda e, k2=k2, b=b, hTx=hTx, hTbx=hTbx: e.memset(CTp[k2][:, :, 8 * b:8 * b + 8], 0.0), reads=[], writes=["CTp%d" % k2])
                for g in range(2):
                    pi = 4 + g
                    PP.op("pe", lambda e, g=g, bt=bt, pi=pi, hTx=hTx, hTbx=hTbx: e.matmul(psf(pi), lhsT=bt[g], rhs=xw[:, g * 512:(g + 1) * 512], start=True, stop=True),
                         reads=[btk, "xw"], writes=["ps%d" % pi])
                    PP.op("dve", lambda e, g=g, b=b, hTx=hTx, hTbx=hTbx: e.tensor_tensor(
                        out=hTx[:, g * 512:(g + 1) * 512].rearrange("p (h q) -> p h q", h=8),
                        in0=hTx[:, g * 512:(g + 1) * 512].rearrange("p (h q) -> p h q", h=8),
                        in1=decay[:, b * 16 + g * 8:b * 16 + g * 8 + 8].unsqueeze(2).to_broadcast([128, 8, 64]), op=ALU.mult),
                        reads=[hk, "decay"], writes=[hk])
                    PP.op("dve", lambda e, g=g, pi=pi, hTx=hTx, hTbx=hTbx: e.tensor_tensor(out=hTx[:, g * 512:(g + 1) * 512], in0=hTx[:, g * 512:(g + 1) * 512],
                                                                     in1=psf(pi), op=ALU.add), reads=[hk, "ps%d" % pi], writes=[hk])
                if S and b + 1 < nseq:
                    stage_load(b + 1)
                if not S:
                    PP.op("dve", lambda e, hTx=hTx, hTbx=hTbx: e.tensor_copy(out=hTbx[:], in_=hTx[:]), reads=[hk], writes=[hbk])
                if (S and stop_after not in ('X1', 'X2')) or ti == LASTP:
                    for half in range(2):
                        pi = 4 + half
                        PP.op("pe", transposes([(psf(pi)[:, j * 128:(j + 1) * 128], hTx[:, (half * 4 + j) * 128:(half * 4 + j + 1) * 128], identf)
                                               for j in range(4)]), reads=[hk, "cst"], writes=["ps%d" % pi])
                        PP.op("act", lambda e, half=half, pi=pi, k2=k2, hTx=hTx, hTbx=hTbx: e.copy(
                            out=hout[k2][:, half * 4:half * 4 + 4, :].rearrange("p j n -> p (j n)"), in_=psf(pi)),
                            reads=["ps%d" % pi], writes=["hout%d" % k2])
                    dst = D["ssm_s"][b] if S else D["ssm_p"]
                    PP.dma("sp", dst.rearrange("(hp h2) q n -> (h2 q) hp n", h2=2), hout[k2][:], reads=["hout%d" % k2])
            chk("A1j")
            if S:
                chk("SA1j")
            if stop_after == "DBG" and ti == 0:
                PP.dma("sp", D["yp"][0:128, 0:256], sv[:], reads=["cum", "ecum", "toend", "la", "dt"])
                PP.dma("pool", D["yp"][128:256, :], xw[:], reads=["xw"])
                PP.dma("pool", D["yp"][256:384, 0:256], Btok[:], reads=["Btok"])
                PP.dma("pool", D["yp"][384:512, :], xdt[:], reads=["xdt"])
                PP.dma("pool", D["yp"][512:640, :], xs_tok[:], reads=["xs_tok"])
                PP.dma("sp", D["yp"][640:768, :], hTx[:], reads=[hk])
                PP.dma("sp", D["yp"][768:896, 0:256], decay[:], reads=["decay"])
                PP.dma("sp", D["yp"][896:1024, :], xc[:, 0:8, :].rearrange("p c t -> p (c t)"), reads=["xc%d" % i for i in range(12)])
                chk("DBG")
            for g in range(2):
                sl = slice(g * 512, (g + 1) * 512)
                PP.op("dve", lambda e, g=g, sl=sl: e.tensor_tensor(
                    out=yg[:, sl].rearrange("p (h q) -> p h q", h=8), in0=psf(6 + g).rearrange("p (h q) -> p h q", h=8),
                    in1=sv[:, 96 + g * 8:96 + g * 8 + 8].unsqueeze(2).to_broadcast([128, 8, 64]), op=ALU.mult),
                    reads=["ps%d" % (6 + g), "ecum"], writes=["yg%d" % g])
                PP.op("dve", lambda e, sl=sl: e.tensor_tensor(out=yg[:, sl], in0=yg[:, sl], in1=yi[:, sl], op=ALU.add),
                     reads=["yg%d" % g, "yi%d" % g], writes=["yg%d" % g])
                PP.op("dve", lambda e, sl=sl: e.tensor_tensor(out=yg[:, sl], in0=yg[:, sl], in1=xsd[:, sl], op=ALU.add),
                     reads=["yg%d" % g, "xsd"], writes=["yg%d" % g])
                PP.op("dve", lambda e, sl=sl: e.tensor_tensor(out=yg[:, sl], in0=yg[:, sl], in1=zs[:, sl], op=ALU.mult),
                     reads=["yg%d" % g, "zs%d" % g], writes=["yg%d" % g])
                PP.op("act", lambda e, g=g, sl=sl: e.activation(out=junk2[:, sl], in_=yg[:, sl], func=AF.Square,
                                                               accum_out=sv[:, 128 + g:129 + g]), reads=["yg%d" % g], writes=["junk2", "rg%d" % g])
                PP.op("act", lambda e, g=g: e.activation(out=sv[:, 130 + g:131 + g], in_=sv[:, 128 + g:129 + g], func=AF.Sqrt, bias=EPS,
                                                        scale=1.0 / 512), reads=["rg%d" % g], writes=["rgs%d" % g])
                PP.op("dve", lambda e, g=g: e.reciprocal(out=sv[:, 130 + g:131 + g], in_=sv[:, 130 + g:131 + g]),
                     reads=["rgs%d" % g], writes=["rgs%d" % g])
                PP.op("dve", lambda e, g=g, sl=sl: e.scalar_tensor_tensor(out=yn[:, sl], in0=yg[:, sl], scalar=sv[:, 130 + g:131 + g],
                                                                          in1=gso[:, sl], op0=ALU.mult, op1=ALU.mult),
                     reads=["yg%d" % g, "rgs%d" % g, "gso"], writes=["yn%d" % g])
            PP.op("pe", transposes([(psb(4)[:, kc * 128:(kc + 1) * 128], yn[:, kc * 128:(kc + 1) * 128], identb[:]) for kc in range(8)]),
                 reads=["yn0", "yn1", "identb"], writes=["ps4"])
            PP.op("act", lambda e: e.copy(out=yT[:].rearrange("p k t -> p (k t)"), in_=psb(4)), reads=["ps4"], writes=["yT"])
            PP.dma("sp", yTs[ti], yT[:].rearrange("p k t -> p (k t)"), reads=["yT"])
            if ti == 0:
                chk("T0")

        W2 = nc.alloc_sbuf_tensor_at("W2e", [128, 8, 3080], BF16, offset=phase_base)
        WO = nc.alloc_sbuf_tensor_at("WOe", [128, 16, 1024], BF16, offset=phase_base + 49280)

        def prefetch_a2():
            P.barrier(engs=("pool",))
            load_w(W2, D["w_in"], 0, 8, 2576, 5656, "W2")
            load_w(WO, D["w_out"], 0, 16, 0, 1024, "WO")
        run_pipelined(P, a1_tile, list(range(NPT[0])) + [16], after_last_front=prefetch_a2)

        chk("A1")

        new_phase()
        alloc([128, 8, 3080], BF16)
        alloc([128, 16, 1024], BF16)
        gml = alloc([128, 1024], F32)
        gpo = alloc([128, 1024], F32)
        bcast_load(gml[:], D["g_mlstm_out"], "gml", 1024)
        bcast_load(gpo[:], D["g_post_mix"], "gpo", 1024)
        bg = alloc([128, 8], F32)
        bcast_load(bg[:, 0:4], D["b_igate"], "bg", 4)
        bcast_load(bg[:, 4:8], D["b_fgate"], "bg2", 4)
        wk2 = {"junk": alloc([128, 1024], BF16), "ss": alloc([128, 4], F32)}
        xt2_ = [alloc([128, 1024], F32) for _ in range(2)]
        uT2_ = [alloc([128, 8, 128], BF16) for _ in range(2)]
        yTl_ = [alloc([128, 8, 128], BF16) for _ in range(2)]
        ktok_ = [alloc([128, 4, 128], BF16) for _ in range(2)]
        v_ext_ = [alloc([128, 4, 257], BF16) for _ in range(2)]
        sig_ = [alloc([128, 1024], F32) for _ in range(2)]
        sg_ = [alloc([128, 64], F32) for _ in range(2)]
        sg2 = alloc([128, 64], F32)
        qT_ = [alloc([128, 4, 128], BF16) for _ in range(2)]
        kT_ = [alloc([128, 4, 128], BF16) for _ in range(2)]
        rhsB = alloc([128, 4, 128], F32)
        tmpB = alloc([128, 4, 128], F32)
        Dm = alloc([128, 4, 128], F32)
        Sb = alloc([128, 4, 128], BF16)
        ST = alloc([128, 4, 128], BF16)
        numA = alloc([128, 4, 257], F32)
        num = alloc([128, 4, 256], F32)
        hn = alloc([128, 1024], BF16)
        hTT = alloc([128, 8, 128], BF16)
        Cst = [alloc([128, 4, 257], F32) for _ in range(2)]
        Cb_ = [alloc([128, 4, 257], BF16) for _ in range(2)]
        Cb = Cb_[0]
        nall = alloc([64, 128], F32)
        nT = alloc([128, 16, 4], F32)
        nout = alloc([128, 16, 4], F32)
        nrow = alloc([64, 128], F32)
        kw = alloc([128, 4, 128], BF16)
        kwp = [alloc([128, 4, 128], BF16) for _ in range(2)]
        qTp = [alloc([128, 4, 128], BF16) for _ in range(2)]
        rhsc = alloc([128, 16, 4], F32)
        scale_bc = alloc([128, 64], F32)
        x1t = alloc([128, 1024], F32)
        mpv = alloc([128, 4], F32)
        mpvS = alloc([128, 4], F32)
        for k in range(2):
            P.op("dve", lambda e, k=k: e.memset(v_ext_[k][:, :, 256:257], 1.0), writes=["v_ext1c@%d" % k])
        P.op("dve", lambda e: e.memset(mpv[:], 0.0), writes=["mprev"])
        P.op("dve", lambda e: e.memset(Cst[0][:], 0.0), writes=["Cst0", "Cst0n"])
        P.op("dve", lambda e: e.memset(Cb[:], 0.0), writes=["Cb0"])
        for k in range(2):
            P.op("dve", lambda e, k=k: e.memset(qTp[k][:], 0.0), writes=["qTp%d" % k])

        def a2_tile(ti, sink):
            S = (ti == 16)
            p = ti % 2
            mp, mpk = (mpvS, "mprevS") if S else (mpv, "mprev")
            FM = {1: 0, 2: 1, 3: 0, 4: 1, 5: 0, 6: 1, 7: 1}
            BM = {8: 6, 9: 7, 6: 6, 7: 7, 0: 6, 1: 7, 2: 2, 3: 3, 4: 4, 5: 5}
            cur = {"m": FM}
            PP = KM(P, p, set(['uT2', 'yTl', 'xt2', 'ktok', 'v_ext0', 'v_ext1', 'v_ext1c', 'gates', 'sig0', 'sig1', 'qT', 'kT', 'ge', 'lfn', 'csl', 'beta', 'cmx', 'mu', 'nmu', 'mt', 'muend', 'sint']), sink, cur=cur, bankmap=FM)

            def psf(i):
                return ps[cur["m"][i]][:]

            def psb(i):
                return ps[cur["m"][i]][:].bitcast(BF16)
            xt2, uT2, yTl, ktok, v_ext, sig, sg, qT, kT = xt2_[p], uT2_[p], yTl_[p], ktok_[p], v_ext_[p], sig_[p], sg_[p], qT_[p], kT_[p]
            V = "S" if S else "P"
            nseq = 16 if S else 1
            maskU, negT, lastbc = C("maskU" + V), C("negT" + V), C("lastbc" + V)
            seqmask, lastsel = C("seqmask" + V), C("lastsel" + V)
            PP.dma("sp", uT2[:].rearrange("p k t -> p (k t)"), uTs[ti], writes=["uT2"])
            PP.dma("sp", yTl[:].rearrange("p k t -> p (k t)"), yTs[ti], writes=["yTl"])
            PP.dma("sp", xt2[:], x_src(ti), writes=["xt2"])
            if S:
                for b in range(16):
                    P.alias.setdefault(mpk, []).append("%s#%d" % (mpk, b))
                    PP.dma("sp", mp[8 * b:8 * b + 8, 0:4], D["st_m"][b:b + 1, :].partition_broadcast(8), writes=["%s#%d" % (mpk, b)])
                PP.dma("sp", nall[:], D["st_n"].rearrange("b h d -> (b h) d"), writes=["nall"])
                PP.op("pe", lambda e: e.transpose(psf(2)[:, 0:64], nall[:], identf[0:64, 0:64]), reads=["nall", "cst"], writes=["ps2"])
                PP.op("act", lambda e: e.copy(out=nT[:].rearrange("p b h -> p (b h)"), in_=psf(2)[:, 0:64]), reads=["ps2"], writes=["nT"])

            def tokmm(pi, c0, c1):
                PP.op("pe", mm_group(psf(pi)[:, 0:c1 - c0], [(uT2[:, kc, :], W2[:, kc, c0:c1]) for kc in range(8)]),
                     reads=["uT2", "W2"], writes=["ps%d" % pi])
            tokmm(1, 512, 1024)
            PP.op("act", lambda e: e.copy(out=ktok[:].rearrange("p h d -> p (h d)"), in_=psf(1)), reads=["ps1"], writes=["ktok"])
            for n in range(2):
                tokmm(2 + n, 1024 + n * 512, 1536 + n * 512)
                PP.op("act", lambda e, n=n: e.copy(out=v_ext[:, 2 * n:2 * n + 2, 0:256], in_=psf(2 + n).rearrange("p (h v) -> p h v", h=2)),
                     reads=["ps%d" % (2 + n)], writes=["v_ext%d" % n])
            vkeys = ["v_ext0", "v_ext1", "v_ext1c"]
            tokmm(4, 2048, 2056)
            PP.op("dve", lambda e: e.tensor_tensor(out=sg[:, 0:8], in0=psf(4)[:, 0:8], in1=bg[:], op=ALU.add),
                 reads=["ps4", "bg", "bg2"], writes=["gates"])
            for n in range(2):
                tokmm(5 + n, 2056 + n * 512, 2568 + n * 512)
                PP.op("act", lambda e, n=n: e.activation(out=sig[:, n * 512:(n + 1) * 512], in_=psf(5 + n), func=AF.Sigmoid),
                     reads=["ps%d" % (5 + n)], writes=["sig%d" % n])
            for (pi, c0, dst, dkey, scl) in ((7, 0, qT, "qT", 128.0 ** -0.5), (1, 512, kT, "kT", 1.0)):
                def fq(e, pi=pi, c0=c0):
                    ins = None
                    for h in range(4):
                        for kc in range(8):
                            ins = e.matmul(psf(pi)[:, h * 128:(h + 1) * 128], lhsT=W2[:, kc, c0 + h * 128:c0 + (h + 1) * 128],
                                           rhs=uT2[:, kc, :], start=(kc == 0), stop=(kc == 7))
                    return ins
                PP.op("pe", fq, reads=["uT2", "W2"], writes=["ps%d" % pi])
                PP.op("act", lambda e, pi=pi, dst=dst, scl=scl: e.mul(out=dst[:].rearrange("p h t -> p (h t)"), in_=psf(pi), mul=scl),
                     reads=["ps%d" % pi], writes=[dkey])
            yield
            PP.bankmap = BM
            cur["m"] = BM
            PP.op("act", lambda e: e.activation(out=sg[:, 8:12], in_=sg[:, 4:8], func=AF.Exp, scale=-1.0), reads=["gates"], writes=["ge"])
            PP.op("act", lambda e: e.activation(out=sg[:, 12:16], in_=sg[:, 8:12], func=AF.Ln, bias=1.0), reads=["ge"], writes=["lfn"])
            PP.op("pe", lambda e: e.matmul(psf(8)[:, 8:12], lhsT=maskU, rhs=sg[:, 12:16], start=True, stop=True),
                 reads=["lfn", "cst"], writes=["ps8"])
            PP.op("dve", lambda e: e.tensor_copy(out=sg[:, 60:64], in_=psf(8)[:, 8:12]), reads=["ps8"], writes=["csl"])
            PP.op("dve", lambda e: e.tensor_tensor(out=sg[:, 16:20], in0=sg[:, 0:4], in1=sg[:, 60:64], op=ALU.add),
                 reads=["gates", "csl"], writes=["beta"])
            PP.op("dve", lambda e: e.tensor_tensor(out=rhsB[:], in0=identf.unsqueeze(1).to_broadcast([128, 4, 128]),
                                                  in1=sg[:, 16:20].unsqueeze(2).to_broadcast([128, 4, 128]), op=ALU.mult),
                 reads=["beta", "cst"], writes=["rhsB"])
            PP.op("pe", lambda e: e.matmul(psf(9), lhsT=onesf, rhs=rhsB[:].rearrange("p h s -> p (h s)"), start=True, stop=True),
                 reads=["rhsB", "cst"], writes=["ps9"])
            PP.op("dve", lambda e: e.tensor_tensor(out=tmpB[:], in0=psf(9).rearrange("p (h s) -> p h s", h=4),
                                                  in1=negT.unsqueeze(1).to_broadcast([128, 4, 128]), op=ALU.add),
                 reads=["ps9", "cst"], writes=["tmpB"])
            PP.op("dve", lambda e: e.reduce_max(out=sg[:, 20:24], in_=tmpB[:], axis=AX.X), reads=["tmpB"], writes=["cmx"])
            PP.op("dve", lambda e: e.tensor_tensor(out=sg[:, 24:28], in0=sg[:, 20:24], in1=mp[:, 0:4], op=ALU.max),
                 reads=["cmx", mpk], writes=["mu"])
            PP.op("dve", lambda e: e.tensor_scalar_mul(out=sg[:, 36:40], in0=sg[:, 24:28], scalar1=-1.0), reads=["mu"], writes=["nmu"])
            for h in range(4):
                PP.op("act", lambda e, h=h: e.activation(out=Dm[:, h, :], in_=tmpB[:, h, :], func=AF.Exp, bias=sg[:, 36 + h:37 + h], scale=1.0),
                     reads=["tmpB", "nmu"], writes=["Dm%d" % h])
            PP.op("pe", lambda e: [e.matmul(psf(6)[:, h * 128:(h + 1) * 128], lhsT=qT[:, h, :], rhs=kT[:, h, :], start=True, stop=True)
                                  for h in range(4)][-1], reads=["qT", "kT"], writes=["ps6"])
            PP.op("dve", lambda e: e.tensor_tensor(out=Sb[:], in0=psf(6).rearrange("p (h s) -> p h s", h=4), in1=Dm[:], op=ALU.mult),
                 reads=["ps6", "Dm0", "Dm1", "Dm2", "Dm3"], writes=["Sb"])
            PP.op("pe", transposes([(psb(7)[:, h * 128:(h + 1) * 128], Sb[:, h, :], identb[:]) for h in range(4)]),
                 reads=["Sb", "identb"], writes=["ps7"])
            PP.op("act", lambda e: e.copy(out=ST[:].rearrange("p h t -> p (h t)"), in_=psb(7)[:, 0:512]), reads=["ps7"], writes=["ST"])
            PP.op("dve", lambda e: e.tensor_tensor(out=sg2[:, 20:24], in0=mp[:, 0:4], in1=sg[:, 24:28], op=ALU.subtract),
                 reads=[mpk, "mu"], writes=["scd"])
            PP.op("act", lambda e: e.activation(out=sg[:, 40:44], in_=sg2[:, 20:24], func=AF.Exp), reads=["scd"], writes=["sint"])
            PP.op("dve", lambda e: e.tensor_tensor(out=sg2[:, 24:28], in0=sg[:, 60:64], in1=sg[:, 24:28], op=ALU.subtract),
                 reads=["csl", "mu"], writes=["emn0"])
            PP.op("act", lambda e: e.activation(out=sg2[:, 24:28], in_=sg2[:, 24:28], func=AF.Exp), reads=["emn0"], writes=["emn"])
            PP.op("dve", lambda e: e.tensor_tensor(out=sg[:, 28:32], in0=sg[:, 24:28], in1=sg[:, 60:64], op=ALU.subtract),
                 reads=["mu", "csl"], writes=["mt"])
            PP.op("pe", lambda e: e.matmul(psf(8)[:, 16:24], lhsT=lastbc, rhs=sg[:, 24:32], start=True, stop=True),
                 reads=["mu", "mt", "cst"], writes=["ps8"])
            PP.op("dve", lambda e: e.tensor_copy(out=sg[:, 44:52], in_=psf(8)[:, 16:24]), reads=["ps8"], writes=["muend"])
            PP.op("dve", lambda e: e.tensor_tensor(out=sg2[:, 0:4], in0=sg[:, 16:20], in1=sg[:, 44:48], op=ALU.subtract),
                 reads=["beta", "muend"], writes=["wend0"])
            PP.op("act", lambda e: e.activation(out=sg2[:, 0:4], in_=sg2[:, 0:4], func=AF.Exp), reads=["wend0"], writes=["wend"])
            PP.op("dve", lambda e: e.tensor_tensor(out=kw[:], in0=ktok[:], in1=sg2[:, 0:4].unsqueeze(2).to_broadcast([128, 4, 128]), op=ALU.mult),
                 reads=["ktok", "wend"], writes=["kw"])
            PP.op("dve", lambda e: e.tensor_tensor(out=rhsc[:, 0:nseq, :], in0=lastsel.unsqueeze(2).to_broadcast([128, nseq, 4]),
                                                  in1=sg2[:, 20:24].unsqueeze(1).to_broadcast([128, nseq, 4]), op=ALU.mult),
                 reads=["scd", "cst"], writes=["rhsc"])
            PP.op("pe", lambda e: e.matmul(psf(8)[:, 32:32 + nseq * 4], lhsT=onesf, rhs=rhsc[:, 0:nseq, :].rearrange("p b h -> p (b h)"),
                                          start=True, stop=True), reads=["rhsc", "cst"], writes=["ps8"])
            PP.op("act", lambda e: e.activation(out=scale_bc[:, 0:nseq * 4], in_=psf(8)[:, 32:32 + nseq * 4], func=AF.Exp),
                 reads=["ps8"], writes=["scale_bc"])
            for h in range(4):
                pi = h % 2
                PP.op("pe", lambda e, h=h, pi=pi: e.matmul(psf(pi)[:, 0:257], lhsT=ST[:, h, :], rhs=v_ext[:, h, :], start=True, stop=True),
                     reads=["ST"] + vkeys, writes=["ps%d" % pi])
                PP.op("act", lambda e, h=h, pi=pi: e.copy(out=numA[:, h, :], in_=psf(pi)[:, 0:257]), reads=["ps%d" % pi], writes=["numA%d" % h])
            for b in range(nseq):
                k2 = b % 2 if S else 0
                ck = "Cst%d" % k2
                if S:
                    for bb in ([0, 1] if b == 0 else [b + 1]):
                        if bb < nseq:
                            PP.dma("sp", Cst[bb % 2][:, :, 0:256], D["st_c"][bb].rearrange("h d v -> d h v"), writes=["Cst%d" % (bb % 2)])
                    PP.op("dve", lambda e, k2=k2, b=b: e.tensor_copy(out=Cst[k2][:, :, 256], in_=nT[:, b, :]), reads=["nT"], writes=[ck + "n"])
                    PP.op("act", lambda e, k2=k2: e.copy(out=Cb_[k2][:], in_=Cst[k2][:]), reads=[ck, ck + "n"], writes=["Cb%d" % k2])
                    PP.op("dve", lambda e, k2=k2, b=b: e.tensor_copy(out=qTp[k2][:, :, 8 * b:8 * b + 8], in_=qT[:, :, 8 * b:8 * b + 8]),
                         reads=["qT"], writes=["qTp%d" % k2])
                    PP.op("dve", lambda e, k2=k2, b=b: e.tensor_scalar_mul(out=kwp[k2][:].rearrange("p h d -> p (h d)"),
                                                                           in0=kw[:].rearrange("p h d -> p (h d)"), scalar1=seqmask[:, b:b + 1]),
                         reads=["kw", "cst"], writes=["kwp%d" % k2])
                    qx, qk = qTp[k2], "qTp%d" % k2
                    kx, kk = kwp[k2], "kwp%d" % k2
                else:
                    qx, qk = qT, "qT"
                    kx, kk = kw, "kw"
                for h in range(4):
                    PP.op("pe", lambda e, h=h, qx=qx, b=b, k2=k2: e.matmul(psf(2 + h)[:, 0:257], lhsT=qx[:, h, :], rhs=Cb_[k2][:, h, :],
                                                                  start=(b == 0), stop=(b == nseq - 1)),
                         reads=[qk, "Cb%d" % k2], writes=["ps%d" % (2 + h)])
                if S:
                    PP.op("dve", lambda e, k2=k2, b=b: e.memset(qTp[k2][:, :, 8 * b:8 * b + 8], 0.0), writes=["qTp%d" % k2])
                for h in range(4):
                    pi = h % 2
                    PP.op("pe", lambda e, h=h, pi=pi, kx=kx: e.matmul(psf(pi)[:, 0:257], lhsT=kx[:, h, :], rhs=v_ext[:, h, :], start=True, stop=True),
                         reads=[kk] + vkeys, writes=["ps%d" % pi])
                    PP.op("dve", lambda e, h=h, pi=pi, k2=k2, b=b: e.scalar_tensor_tensor(
                        out=Cst[k2][:, h, :], in0=Cst[k2][:, h, :], scalar=scale_bc[:, b * 4 + h:b * 4 + h + 1], in1=psf(pi)[:, 0:257],
                        op0=ALU.mult, op1=ALU.add), reads=[ck, ck + "n", "scale_bc", "ps%d" % pi], writes=[ck, ck + "n"])
                if S:
                    PP.dma("sp", D["c_s"][b].rearrange("h d v -> d h v"), Cst[k2][:, :, 0:256], reads=[ck, ck + "n"])
                    PP.op("dve", lambda e, k2=k2, b=b: e.tensor_copy(out=nout[:, b, :], in_=Cst[k2][:, :, 256]), reads=[ck, ck + "n"], writes=["nout"])
                else:
                    if ti == LASTP:
                        PP.dma("sp", D["c_p"].rearrange("h d v -> d h v"), Cst[0][:, :, 0:256], reads=[ck, ck + "n"])
                        PP.dma("sp", D["n_p"].rearrange("h d -> d h"), Cst[0][:, :, 256], reads=[ck, ck + "n"], allow_slow_non_contiguous=True)
            if S:
                PP.op("pe", lambda e: e.transpose(psf(8)[0:64, 0:128], nout[:].rearrange("p b h -> p (b h)"), identf), reads=["nout", "cst"], writes=["ps8"])
                PP.op("act", lambda e: e.copy(out=nrow[:], in_=psf(8)[0:64, 0:128]), reads=["ps8"], writes=["nrow"])
                PP.dma("sp", D["n_s"].rearrange("b h d -> (b h) d"), nrow[:], reads=["nrow"])
            for h in range(4):
                PP.op("dve", lambda e, h=h: e.scalar_tensor_tensor(out=num[:, h, :], in0=psf(2 + h)[:, 0:256], scalar=sg[:, 40 + h:41 + h],
                                                                   in1=numA[:, h, 0:256], op0=ALU.mult, op1=ALU.add),
                     reads=["ps%d" % (2 + h), "sint", "numA%d" % h], writes=["num%d" % h])
                PP.op("dve", lambda e, h=h: e.scalar_tensor_tensor(out=sg2[:, 4 + h:5 + h], in0=psf(2 + h)[:, 256:257], scalar=sg[:, 40 + h:41 + h],
                                                                   in1=numA[:, h, 256:257], op0=ALU.mult, op1=ALU.add),
                     reads=["ps%d" % (2 + h), "sint", "numA%d" % h], writes=["den%d" % h])
            if not S:
                PP.op("act", lambda e: e.copy(out=Cb[:], in_=Cst[0][:]), reads=["Cst0", "Cst0n"], writes=["Cb0"])
            dkeys = ["den%d" % h for h in range(4)]
            PP.op("dve", lambda e: e.tensor_scalar_mul(out=sg2[:, 28:32], in0=sg2[:, 4:8], scalar1=-1.0), reads=dkeys, writes=["denn"])
            PP.op("dve", lambda e: e.tensor_tensor(out=sg2[:, 4:8], in0=sg2[:, 4:8], in1=sg2[:, 28:32], op=ALU.max), reads=dkeys + ["denn"], writes=["dena"])
            PP.op("dve", lambda e: e.tensor_tensor(out=sg2[:, 4:8], in0=sg2[:, 4:8], in1=sg2[:, 24:28], op=ALU.max), reads=["dena", "emn"], writes=["denm"])
            PP.op("dve", lambda e: e.reciprocal(out=sg2[:, 8:12], in_=sg2[:, 4:8]), reads=["denm"], writes=["rden"])
            for h in range(4):
                PP.op("act", lambda e, h=h: e.activation(out=wk2["junk"][:, 0:256], in_=num[:, h, :], func=AF.Square, scale=sg2[:, 8 + h:9 + h],
                                                        accum_out=sg2[:, 12 + h:13 + h]), reads=["num%d" % h, "rden"], writes=["junk", "ssh%d" % h])
            skeys = ["ssh%d" % h for h in range(4)]
            PP.op("act", lambda e: e.activation(out=sg2[:, 16:20], in_=sg2[:, 12:16], func=AF.Sqrt, bias=EPS, scale=1.0 / 256), reads=skeys, writes=["rsh"])
            PP.op("dve", lambda e: e.reciprocal(out=sg2[:, 16:20], in_=sg2[:, 16:20]), reads=["rsh"], writes=["rsh"])
            PP.op("dve", lambda e: e.tensor_tensor(out=sg2[:, 16:20], in0=sg2[:, 16:20], in1=sg2[:, 8:12], op=ALU.mult), reads=["rsh", "rden"], writes=["comb"])
            for h in range(4):
                PP.op("dve", lambda e, h=h: e.scalar_tensor_tensor(out=num[:, h, :], in0=num[:, h, :], scalar=sg2[:, 16 + h:17 + h],
                                                                   in1=gml[:, h * 256:(h + 1) * 256], op0=ALU.mult, op1=ALU.mult),
                     reads=["num%d" % h, "comb", "gml"], writes=["num%d" % h])
            nkeys = ["num%d" % h for h in range(4)]
            PP.op("dve", lambda e: e.tensor_tensor(out=hn[:], in0=num[:].rearrange("p h v -> p (h v)"), in1=sig[:], op=ALU.mult),
                 reads=nkeys + ["sig0", "sig1"], writes=["hn"])
            PP.op("pe", transposes([(psb(7)[:, kc * 128:(kc + 1) * 128], hn[:, kc * 128:(kc + 1) * 128], identb[:]) for kc in range(8)]),
                 reads=["hn", "identb"], writes=["ps7"])
            PP.op("act", lambda e: e.copy(out=hTT[:].rearrange("p k t -> p (k t)"), in_=psb(7)), reads=["ps7"], writes=["hTT"])
            for n in range(2):
                PP.op("pe", mm_group(psf(n), [(yTl[:, kc, :], WO[:, kc, n * 512:(n + 1) * 512]) for kc in range(8)] +
                                    [(hTT[:, kc, :], WO[:, 8 + kc, n * 512:(n + 1) * 512]) for kc in range(8)]),
                     reads=["yTl", "hTT", "WO"], writes=["ps%d" % n])
            post_norm_res(0, 1, xt2[:], "xt2", gpo[:], "gpo", wk2, x1t, "x1t", P=PP, psf=psf)
            PP.dma("sp", x1s[ti], x1t[:], reads=["x1t0", "x1t1"])
            if S:
                for b in range(16):
                    PP.dma("sp", D["m_s"][b:b + 1, :], sg[8 * b + 7:8 * b + 8, 28:32], reads=["mt"])
            else:
                if ti == LASTP:
                    PP.dma("sp", D["m_p"], sg[127:128, 28:32], reads=["mt"])
                PP.op("dve", lambda e: e.tensor_copy(out=mp[:, 0:4], in_=sg[:, 48:52]), reads=["muend"], writes=[mpk])

        WQ = nc.alloc_sbuf_tensor_at("WQe", [128, 8, 1024], BF16, offset=phase_base)
        WX = nc.alloc_sbuf_tensor_at("WXe", [128, 8, 1024], BF16, offset=phase_base + 16384)

        def prefetch_b():
            P.barrier(engs=("pool",))
            load_w(WQ, D["w_xq"], 0, 8, 0, 1024, "WQ")
            load_w(WX, D["w_xo"], 0, 8, 0, 1024, "WX")
        run_pipelined(P, a2_tile, list(range(NPT[0])) + [16], after_last_front=prefetch_b)
        chk("A2")

        new_phase()
        alloc([128, 8, 1024], BF16)
        alloc([128, 8, 1024], BF16)
        HI = (top[0] - 2 * 45056) // 64 * 64
        WG = nc.alloc_sbuf_tensor_at("WGe", [128, 8, 2816], BF16, offset=HI)
        WU = nc.alloc_sbuf_tensor_at("WUe", [128, 8, 2816], BF16, offset=HI + 45056)
        load_w(WG, D["w_gate"], 0, 8, 0, 2816, "WG")
        load_w(WU, D["w_up"], 0, 8, 0, 2816, "WU")
        top[0] = HI
        gpx = alloc([128, 1024], F32)
        gox = alloc([128, 1024], F32)
        bcast_load(gpx[:], D["g_pre_x"], "gpx", 1024)
        bcast_load(gox[:], D["g_post_x"], "gox", 1024)
        junk3 = alloc([128, 1024], BF16)
        wk3_ = [{"junk": junk3, "ss": alloc([128, 4], F32), "u": alloc([128, 1024], BF16)} for _ in range(2)]
        xt3_ = [alloc([128, 1024], F32) for _ in range(2)]
        uT3_ = [alloc([128, 8, 128], BF16) for _ in range(2)]
        qT3 = alloc([128, 8, 128], BF16)
        Pn = alloc([128, 4, 256], BF16)
        PT = alloc([128, 8, 128], BF16)
        otok = alloc([128, 1024], BF16)
        oT = alloc([128, 8, 128], BF16)
        x2t_ = [alloc([128, 1024], F32)] * 2
        sm3 = alloc([128, 16], F32)
        Kb = [alloc([128, 2, 1024], BF16) for _ in range(2)]
        Vs = [alloc([128, 2, 1024], BF16) for _ in range(2)]
        KTb = [alloc([128, 8, 256], BF16) for _ in range(2)]
        qTp3 = [alloc([128, 8, 128], BF16) for _ in range(2)]
        PTp = [alloc([128, 8, 128], BF16) for _ in range(2)]
        for k in range(2):
            P.op("dve", lambda e, k=k: e.memset(qTp3[k][:], 0.0), writes=["qTp3%d" % k])
            P.op("dve", lambda e, k=k: e.memset(PTp[k][:], 0.0), writes=["PTp%d" % k])

        def b_tile(ti, sink):
            S = (ti == 16)
            p = ti % 2
            PP = KM(P, p, {"xt3", "uT3"}, sink)
            wk3, xt3, uT3, x2t = wk3_[p], xt3_[p], uT3_[p], x2t_[p]
            nseq = 16 if S else 1
            PP.dma("sp", xt3[:], x1s[ti], writes=["xt3"])
            rms_uT(xt3[:], "xt3", gpx[:], "gpx", wk3, uT3, "uT3", 0, P=PP, sfx="@%d" % p)
            if S:
                for bb in range(2):
                    PP.dma("pool", Kb[bb][:], D["ck"][bb].rearrange("(mc m) f -> m mc f", mc=2), writes=["Kb%d" % bb])
                for bb in range(2):
                    PP.dma("pool", Vs[bb][:], D["cv"][bb].rearrange("(mc m) f -> m mc f", mc=2), writes=["Vs%d" % bb])
            yield
            for half in range(2):
                pi = 1 + half
                def fq(e, half=half, pi=pi):
                    ins = None
                    for c4 in range(4):
                        fc = half * 4 + c4
                        for kc in range(8):
                            ins = e.matmul(psf(pi)[:, c4 * 128:(c4 + 1) * 128], lhsT=WQ[:, kc, fc * 128:(fc + 1) * 128], rhs=uT3[:, kc, :],
                                           start=(kc == 0), stop=(kc == 7))
                    return ins
                PP.op("pe", fq, reads=["uT3", "WQ"], writes=["ps%d" % pi])
                PP.op("act", lambda e, half=half, pi=pi: e.mul(out=qT3[:, half * 4:half * 4 + 4, :].rearrange("p c t -> p (c t)"), in_=psf(pi), mul=1.0 / 16.0),
                     reads=["ps%d" % pi], writes=["qT3%d" % half])
            qkeys = ["qT30", "qT31"]
            for b in range(nseq):
                k2 = b % 2
                if S:
                    for half in range(2):
                        pi = 1 + half
                        PP.op("pe", transposes([(psb(pi)[:, c4 * 256 + mc * 128:c4 * 256 + (mc + 1) * 128],
                                                Kb[k2][:, mc, (half * 4 + c4) * 128:(half * 4 + c4 + 1) * 128], identb[:])
                                               for c4 in range(4) for mc in range(2)]), reads=["Kb%d" % k2, "identb"], writes=["ps%d" % pi])
                        PP.op("act", lambda e, half=half, pi=pi, k2=k2: e.copy(out=KTb[k2][:, half * 4:half * 4 + 4, :].rearrange("p c m -> p (c m)"), in_=psb(pi)),
                             reads=["ps%d" % pi], writes=["KTb%d" % k2])
                    if b + 2 < nseq:
                        PP.dma("pool", Kb[k2][:], D["ck"][b + 2].rearrange("(mc m) f -> m mc f", mc=2), writes=["Kb%d" % k2])
                    PP.op("dve", lambda e, k2=k2, b=b: e.tensor_copy(out=qTp3[k2][:, :, 8 * b:8 * b + 8], in_=qT3[:, :, 8 * b:8 * b + 8]),
                         reads=qkeys, writes=["qTp3%d" % k2])
                    qx, qk, kx, kk = qTp3[k2], ["qTp3%d" % k2], KTb[k2], "KTb%d" % k2
                else:
                    qx, qk, kx, kk = qT3, qkeys, KT, "KT"
                for h in range(4):
                    def fs(e, h=h, qx=qx, kx=kx, b=b):
                        ins = None
                        for dc in range(2):
                            ins = e.matmul(psf(3 + h)[:, 0:256], lhsT=qx[:, 2 * h + dc, :], rhs=kx[:, 2 * h + dc, :],
                                           start=(b == 0 and dc == 0), stop=(b == nseq - 1 and dc == 1))
                        return ins
                    PP.op("pe", fs, reads=qk + [kk], writes=["ps%d" % (3 + h)])
                if S:
                    PP.op("dve", lambda e, k2=k2, b=b: e.memset(qTp3[k2][:, :, 8 * b:8 * b + 8], 0.0), writes=["qTp3%d" % k2])
            for h in range(4):
                PP.op("dve", lambda e, h=h: e.reduce_max(out=sm3[:, h:h + 1], in_=psf(3 + h)[:, 0:256], axis=AX.X), reads=["ps%d" % (3 + h)], writes=["mx%d" % h])
                PP.op("dve", lambda e, h=h: e.tensor_scalar_mul(out=sm3[:, 4 + h:5 + h], in0=sm3[:, h:h + 1], scalar1=-1.0), reads=["mx%d" % h], writes=["nmx%d" % h])
                PP.op("act", lambda e, h=h: e.activation(out=Pn[:, h, :], in_=psf(3 + h)[:, 0:256], func=AF.Exp, bias=sm3[:, 4 + h:5 + h], scale=1.0,
                                                        accum_out=sm3[:, 8 + h:9 + h]), reads=["ps%d" % (3 + h), "nmx%d" % h], writes=["Pn%d" % h, "rs%d" % h])
                PP.op("dve", lambda e, h=h: e.reciprocal(out=sm3[:, 12 + h:13 + h], in_=sm3[:, 8 + h:9 + h]), reads=["rs%d" % h], writes=["ri%d" % h])
            pkeys = ["Pn%d" % h for h in range(4)]
            PP.op("pe", transposes([(psb(7)[:, (2 * h + mc) * 128:(2 * h + mc + 1) * 128], Pn[:, h, mc * 128:(mc + 1) * 128], identb[:])
                                   for h in range(4) for mc in range(2)]), reads=pkeys + ["identb"], writes=["ps7"])
            PP.op("act", lambda e: e.copy(out=PT[:].rearrange("p c t -> p (c t)"), in_=psb(7)), reads=["ps7"], writes=["PT"])
            for b in range(nseq):
                k2 = b % 2
                if S:
                    PP.op("dve", lambda e, k2=k2, b=b: e.tensor_copy(out=PTp[k2][:, :, 8 * b:8 * b + 8], in_=PT[:, :, 8 * b:8 * b + 8]),
                         reads=["PT"], writes=["PTp%d" % k2])
                    px, pk, vx, vk = PTp[k2], "PTp%d" % k2, Vs[k2], "Vs%d" % k2
                else:
                    px, pk, vx, vk = PT, "PT", Vb, "Vb"
                for h in range(4):
                    def fo(e, h=h, px=px, vx=vx, b=b):
                        ins = None
                        for mc in range(2):
                            ins = e.matmul(psf(3 + h)[:, 0:256], lhsT=px[:, 2 * h + mc, :], rhs=vx[:, mc, h * 256:(h + 1) * 256],
                                           start=(b == 0 and mc == 0), stop=(b == nseq - 1 and mc == 1))
                        return ins
                    PP.op("pe", fo, reads=[pk, vk], writes=["ps%d" % (3 + h)])
                if S:
                    if b + 2 < nseq:
                        PP.dma("pool", Vs[k2][:], D["cv"][b + 2].rearrange("(mc m) f -> m mc f", mc=2), writes=["Vs%d" % k2])
                    PP.op("dve", lambda e, k2=k2, b=b: e.memset(PTp[k2][:, :, 8 * b:8 * b + 8], 0.0), writes=["PTp%d" % k2])
            for h in range(4):
                PP.op("dve", lambda e, h=h: e.tensor_scalar_mul(out=otok[:, h * 256:(h + 1) * 256], in0=psf(3 + h)[:, 0:256], scalar1=sm3[:, 12 + h:13 + h]),
                     reads=["ps%d" % (3 + h), "ri%d" % h], writes=["otok%d" % h])
            okeys = ["otok%d" % h for h in range(4)]
            PP.op("pe", transposes([(psb(7)[:, kc * 128:(kc + 1) * 128], otok[:, kc * 128:(kc + 1) * 128], identb[:]) for kc in range(8)]),
                 reads=okeys + ["identb"], writes=["ps7"])
            PP.op("act", lambda e: e.copy(out=oT[:].rearrange("p k t -> p (k t)"), in_=psb(7)), reads=["ps7"], writes=["oT"])
            for n in range(2):
                PP.op("pe", mm_group(psf(1 + n), [(oT[:, kc, :], WX[:, kc, n * 512:(n + 1) * 512]) for kc in range(8)]),
                     reads=["oT", "WX"], writes=["ps%d" % (1 + n)])
            post_norm_res(1, 2, xt3[:], "xt3", gox[:], "gox", wk3, x2t, "x2t", P=PP, sfx="@%d" % p)
            PP.dma("sp", x2s[ti], x2t[:], reads=["x2t0", "x2t1"])

        run_pipelined(P, b_tile, list(range(NPT[0])) + [16])
        chk("B")

        new_phase()
        WD = alloc([128, 22, 1024], BF16)
        load_w(WD, D["w_down"], 0, 22, 0, 1024, "WD")
        gpf = alloc([128, 1024], F32)
        gof = alloc([128, 1024], F32)
        bcast_load(gpf[:], D["g_pre_ffn"], "gpf", 1024)
        bcast_load(gof[:], D["g_post_ffn"], "gof", 1024)
        wk4_ = [{"junk": alloc([128, 1024], BF16), "ss": alloc([128, 4], F32), "u": alloc([128, 1024], BF16)} for _ in range(2)]
        xt4_ = [alloc([128, 1024], F32) for _ in range(2)]
        uT4_ = [alloc([128, 8, 128], BF16) for _ in range(2)]
        hT4 = alloc([128, 22, 128], BF16)
        gs = [alloc([128, 512], F32) for _ in range(2)]
        yt_ = [alloc([128, 1024], F32) for _ in range(2)]

        def c_tile(ti, sink):
            p = ti % 2
            PP = KM(P, p, {"xt4", "uT4", "yt0", "yt1"}, sink)
            wk4, xt4, uT4, yt = wk4_[p], xt4_[p], uT4_[p], yt_[p]
            PP.dma("sp", xt4[:], x2s[ti], writes=["xt4"])
            rms_uT(xt4[:], "xt4", gpf[:], "gpf", wk4, uT4, "uT4", 0, P=PP, sfx="@%d" % p)
            yield
            for q in range(6):
                nch = 4 if q < 5 else 2
                pg, pu = 1 + (q % 2) * 2, 2 + (q % 2) * 2
                for (pi, Wt, wkey) in ((pg, WG, "WG"), (pu, WU, "WU")):
                    def fg(e, q=q, nch=nch, pi=pi, Wt=Wt):
                        ins = None
                        for c4 in range(nch):
                            hc = q * 4 + c4
                            for kc in range(8):
                                ins = e.matmul(psf(pi)[:, c4 * 128:(c4 + 1) * 128], lhsT=Wt[:, kc, hc * 128:(hc + 1) * 128], rhs=uT4[:, kc, :],
                                               start=(kc == 0), stop=(kc == 7))
                        return ins
                    PP.op("pe", fg, reads=["uT4", wkey], writes=["ps%d" % pi])
                PP.op("act", lambda e, q=q, nch=nch, pg=pg: e.activation(out=gs[q % 2][:, 0:nch * 128], in_=psf(pg)[:, 0:nch * 128], func=AF.Silu),
                     reads=["ps%d" % pg], writes=["gs%d" % (q % 2)])
                PP.op("dve", lambda e, q=q, nch=nch, pu=pu: e.tensor_tensor(out=hT4[:, q * 4:q * 4 + nch, :].rearrange("p c t -> p (c t)"),
                                                                          in0=psf(pu)[:, 0:nch * 128], in1=gs[q % 2][:, 0:nch * 128], op=ALU.mult),
                     reads=["ps%d" % pu, "gs%d" % (q % 2)], writes=["hT4_%d" % q])
            hkeys = ["hT4_%d" % q for q in range(6)]
            for n in range(2):
                PP.op("pe", mm_group(psf(5 + n), [(hT4[:, hc, :], WD[:, hc, n * 512:(n + 1) * 512]) for hc in range(22)]),
                     reads=hkeys + ["WD"], writes=["ps%d" % (5 + n)])
            post_norm_res(5, 6, xt4[:], "xt4", gof[:], "gof", wk4, yt, "yt", P=PP, sfx="@%d" % p)
            PP.dma("sp", y_dst(ti), yt[:], reads=["yt0", "yt1"])

        run_pipelined(P, c_tile, list(range(NPT[0])) + [16])
        chk("C")
        P.emit()


_NC_CACHE = {}


def _in_maps(inp):
    f = lambda a: np.ascontiguousarray(np.asarray(a, dtype=np.float32))
    w = {k: f(inp[k]).reshape(W_SHAPES[k]) for k in W_SHAPES}
    maps = []
    for c in range(NCORES):
        sl = slice(16 * c, 16 * c + 16)
        m = dict(w)
        m["xp"] = f(inp["x_prompt"][c])
        m["xs"] = f(inp["x_sample"][sl]).reshape(128, 1024)
        m["mem"] = f(inp["mem_prompt"][c])
        m["st_ssm"] = f(inp["state_ssm"][0, sl])
        m["st_conv"] = f(inp["state_conv"][0, sl]).reshape(48, 1536)
        m["st_c"] = f(inp["state_mlstm_c"][0, sl])
        m["st_n"] = f(inp["state_mlstm_n"][0, sl])
        m["st_m"] = f(inp["state_mlstm_m"][0, sl])
        m["ck"] = f(inp["cache_mem_k"][0, sl]).reshape(16, 256, 1024)
        m["cv"] = f(inp["cache_mem_v"][0, sl]).reshape(16, 256, 1024)
        m["consts"] = CONSTS
        maps.append(m)
    return maps


def _assemble(results):
    g = lambda k: [np.asarray(r[k], dtype=np.float32) for r in results]
    yp = np.stack(g("yp"))
    ys = np.concatenate(g("ys")).reshape(128, 8, 1024)
    mk = np.stack(g("mk")).reshape(1, 8, 256, 4, 256)
    mv = np.stack(g("mv")).reshape(1, 8, 256, 4, 256)
    ssm_p = np.stack(g("ssm_p")).reshape(1, 8, 16, 64, 128)
    conv_p = np.stack(g("conv_p")).reshape(1, 8, 3, 1536)
    c_p = np.stack(g("c_p")).reshape(1, 8, 4, 128, 256)
    n_p = np.stack(g("n_p")).reshape(1, 8, 4, 128)
    m_p = np.stack(g("m_p")).reshape(1, 8, 4)
    ssm_s = np.concatenate(g("ssm_s")).reshape(1, 128, 16, 64, 128)
    conv_s = np.concatenate(g("conv_s")).reshape(1, 128, 3, 1536)
    c_s = np.concatenate(g("c_s")).reshape(1, 128, 4, 128, 256)
    n_s = np.concatenate(g("n_s")).reshape(1, 128, 4, 128)
    m_s = np.concatenate(g("m_s")).reshape(1, 128, 4)
    return (yp, ys, mk, mv, ssm_p, conv_p, c_p, n_p, m_p, ssm_s, conv_s, c_s, n_s, m_s)


def kernel(**inputs):
    if "nc" not in _NC_CACHE:
        _NC_CACHE["nc"] = build_nc("C")
    res = run_bass_kernel_spmd(_NC_CACHE["nc"], _in_maps(inputs), core_ids=list(range(NCORES)))
    return _assemble(res.results)
```
